# Optimizing a Trainium2 kernel written in Bass

```python
import jax, jax.numpy as jnp
from jax import lax
import numpy as np

D_MODEL = 1024
BATCH = 2
SEQ = 8192
DEPTH = 1

CONV_WIDTH = D_MODEL
CONV_SIZE = 31
SSM_WIDTH = D_MODEL // 2
SSM_GROUP = 16
SSM_GROUPS = SSM_WIDTH // SSM_GROUP
SSM_STATE = 64
DT_MIN = 0.001
DT_MAX = 0.1
RMS_EPS = 1e-6
LN_EPS = 1e-5
IN_SPLITS = (2 * CONV_WIDTH, CONV_WIDTH, SSM_WIDTH, SSM_WIDTH, D_MODEL, D_MODEL)
IN_WIDTH = sum(IN_SPLITS)
IN_OFFSETS = [int(v) for v in np.cumsum(IN_SPLITS)[:-1]]

kernel_name = "hybrid_conformer_conv_s5_gated_block"


def rms_norm(x, g):
    xf = x.astype(jnp.float32)
    y = xf * lax.rsqrt(jnp.mean(xf * xf, axis=-1, keepdims=True) + RMS_EPS)
    return (y * g.astype(jnp.float32)).astype(x.dtype)


def layer_norm(x, g, b):
    xf = x.astype(jnp.float32)
    mu = jnp.mean(xf, axis=-1, keepdims=True)
    xc = xf - mu
    var = jnp.mean(xc * xc, axis=-1, keepdims=True)
    y = xc * lax.rsqrt(var + LN_EPS) * g.astype(jnp.float32) + b.astype(jnp.float32)
    return y.astype(x.dtype)


def causal_depthwise_conv(u, w, b):
    c = u.shape[-1]
    y = lax.conv_general_dilated(
        u, w[:, None, :].astype(u.dtype), window_strides=(1,),
        padding=[(CONV_SIZE - 1, 0)],
        dimension_numbers=("NWC", "WIO", "NWC"),
        feature_group_count=c)
    return y + b


def s5_mimo(u, lam_re, lam_im, log_dt, b_re, b_im, c_re, c_im, d):
    bsz, length, _ = u.shape
    f32 = jnp.float32
    uf = u.astype(f32).reshape(bsz, length, SSM_GROUPS, SSM_GROUP)
    dt = jnp.exp(log_dt.astype(f32))[:, None]
    lr = lam_re.astype(f32)
    li = lam_im.astype(f32)
    mag = jnp.exp(lr * dt)
    ar = mag * jnp.cos(li * dt)
    ai = mag * jnp.sin(li * dt)
    den = lr * lr + li * li
    zr = ((ar - 1.0) * lr + ai * li) / den
    zi = (ai * lr - (ar - 1.0) * li) / den
    br = b_re.astype(f32)
    bi = b_im.astype(f32)
    bbar_re = zr[..., None] * br - zi[..., None] * bi
    bbar_im = zr[..., None] * bi + zi[..., None] * br
    bu_re = jnp.einsum("blgh,gph->blgp", uf, bbar_re)
    bu_im = jnp.einsum("blgh,gph->blgp", uf, bbar_im)
    a_re = jnp.broadcast_to(ar, bu_re.shape)
    a_im = jnp.broadcast_to(ai, bu_re.shape)

    def combine(e1, e2):
        a1r, a1i, b1r, b1i = e1
        a2r, a2i, b2r, b2i = e2
        return (a2r * a1r - a2i * a1i,
                a2r * a1i + a2i * a1r,
                a2r * b1r - a2i * b1i + b2r,
                a2r * b1i + a2i * b1r + b2i)

    _, _, s_re, s_im = lax.associative_scan(combine, (a_re, a_im, bu_re, bu_im), axis=1)
    y = (jnp.einsum("blgp,ghp->blgh", s_re, c_re.astype(f32))
         - jnp.einsum("blgp,ghp->blgh", s_im, c_im.astype(f32))
         + d.astype(f32) * uf)
    return y.reshape(bsz, length, SSM_WIDTH).astype(u.dtype)


def hybrid_layer(x, pre_g, w_in, conv_w, conv_b, conv_ln_g, conv_ln_b, w_conv_out,
                 lam_re, lam_im, log_dt, b_re, b_im, c_re, c_im, d,
                 w_glu, b_glu, w_ssm_out, w_out, post_g):
    h = rms_norm(x, pre_g)
    proj = jnp.einsum("bld,de->ble", h, w_in)
    conv_in, z_c, u_s, z_s, g_c, g_s = jnp.split(proj, IN_OFFSETS, axis=-1)
    ca, cb = jnp.split(conv_in, 2, axis=-1)
    cu = ca * jax.nn.sigmoid(cb)
    cu = causal_depthwise_conv(cu, conv_w, conv_b)
    cu = jax.nn.silu(layer_norm(cu, conv_ln_g, conv_ln_b))
    conv_out = jnp.einsum("blc,cd->bld", cu * jax.nn.silu(z_c), w_conv_out)
    y = jax.nn.gelu(s5_mimo(u_s, lam_re, lam_im, log_dt, b_re, b_im, c_re, c_im, d))
    y = y * jax.nn.sigmoid(jnp.einsum("blc,ce->ble", y, w_glu) + b_glu)
    ssm_out = jnp.einsum("blc,cd->bld", y * jax.nn.silu(z_s), w_ssm_out)
    merged = jax.nn.sigmoid(g_c) * conv_out + jax.nn.sigmoid(g_s) * ssm_out
    out = jnp.einsum("bld,de->ble", merged, w_out)
    return x + rms_norm(out, post_g)


def setup_inputs(seed: int = 0) -> dict:
    key = jax.random.key(seed)
    ks = jax.random.split(key, 24)
    f32 = jnp.float32
    nrm = lambda k, shape, scale: jax.random.normal(k, shape, f32) * scale
    L, G, P, H = DEPTH, SSM_GROUPS, SSM_STATE, SSM_GROUP
    n = jnp.arange(P, dtype=f32)
    log_dt = (jnp.log(DT_MIN) + jax.random.uniform(ks[11], (L, G), f32)
              * (jnp.log(DT_MAX) - jnp.log(DT_MIN)))
    return {
        "x": jax.random.normal(ks[0], (BATCH, SEQ, D_MODEL), f32),
        "pre_norm_gain": 1.0 + nrm(ks[1], (L, D_MODEL), 0.05),
        "w_in": nrm(ks[2], (L, D_MODEL, IN_WIDTH), D_MODEL ** -0.5),
        "conv_w": nrm(ks[3], (L, CONV_SIZE, CONV_WIDTH), CONV_SIZE ** -0.5),
        "conv_b": nrm(ks[4], (L, CONV_WIDTH), 0.02),
        "conv_ln_gain": 1.0 + nrm(ks[5], (L, CONV_WIDTH), 0.05),
        "conv_ln_bias": nrm(ks[6], (L, CONV_WIDTH), 0.02),
        "w_conv_out": nrm(ks[7], (L, CONV_WIDTH, D_MODEL), CONV_WIDTH ** -0.5),
        "ssm_lambda_re": -0.5 + nrm(ks[8], (L, G, P), 0.01),
        "ssm_lambda_im": jnp.pi * n + nrm(ks[9], (L, G, P), 0.01),
        "ssm_log_dt": log_dt,
        "ssm_b_re": nrm(ks[12], (L, G, P, H), (2.0 * H) ** -0.5),
        "ssm_b_im": nrm(ks[13], (L, G, P, H), (2.0 * H) ** -0.5),
        "ssm_c_re": nrm(ks[14], (L, G, H, P), (2.0 * P) ** -0.5),
        "ssm_c_im": nrm(ks[15], (L, G, H, P), (2.0 * P) ** -0.5),
        "ssm_d": nrm(ks[16], (L, G, H), 1.0),
        "w_ssm_glu": nrm(ks[17], (L, SSM_WIDTH, SSM_WIDTH), SSM_WIDTH ** -0.5),
        "b_ssm_glu": nrm(ks[18], (L, SSM_WIDTH), 0.02),
        "w_ssm_out": nrm(ks[19], (L, SSM_WIDTH, D_MODEL), SSM_WIDTH ** -0.5),
        "w_out": nrm(ks[20], (L, D_MODEL, D_MODEL), D_MODEL ** -0.5),
        "post_norm_gain": 1.0 + nrm(ks[21], (L, D_MODEL), 0.05),
    }


def reference(x, pre_norm_gain, w_in, conv_w, conv_b, conv_ln_gain, conv_ln_bias, w_conv_out,
              ssm_lambda_re, ssm_lambda_im, ssm_log_dt, ssm_b_re, ssm_b_im, ssm_c_re, ssm_c_im,
              ssm_d, w_ssm_glu, b_ssm_glu, w_ssm_out, w_out, post_norm_gain):
    for l in range(DEPTH):
        x = hybrid_layer(x, pre_norm_gain[l], w_in[l], conv_w[l], conv_b[l], conv_ln_gain[l],
                         conv_ln_bias[l], w_conv_out[l], ssm_lambda_re[l], ssm_lambda_im[l],
                         ssm_log_dt[l], ssm_b_re[l], ssm_b_im[l], ssm_c_re[l], ssm_c_im[l],
                         ssm_d[l], w_ssm_glu[l], b_ssm_glu[l], w_ssm_out[l], w_out[l],
                         post_norm_gain[l])
    return x
```

```python
import math
import numpy as np
import concourse.bass as bass
import concourse.mybir as mybir
from concourse.bass_utils import run_bass_kernel_spmd

F32 = mybir.dt.float32
BF16 = mybir.dt.bfloat16
AF = mybir.ActivationFunctionType
ALU = mybir.AluOpType

NCORES = 8
import os
NO_CC = bool(os.environ.get('NO_CC'))
D = 1024
TOK = 2048
HALO = 32
NT = TOK + HALO
INW = 6144
R0 = 16
NB = TOK // R0
NEXP = R0 + 1
TWO_PI = float(2.0 * np.pi)
MAGIC = 12582912.0
C1 = 6.28125
C2 = float(2.0 * np.pi - 6.28125)

OFF_CA, OFF_CB, OFF_ZC, OFF_U, OFF_ZS, OFF_GC, OFF_GS = 0, 1024, 2048, 3072, 3584, 4096, 5120


class Sched:
    def __init__(self, nc, n_dma_sems=24):
        self.nc = nc
        self.engs = {"pe": nc.tensor, "act": nc.scalar, "dve": nc.vector, "pool": nc.gpsimd, "sp": nc.sync}
        self.sem = {k: nc.alloc_semaphore("prog_" + k) for k in self.engs}
        self.cnt = {k: 0 for k in self.engs}
        self.waited = {k: {} for k in self.engs}
        self.bufs = {}
        self.semobj = {("E", k): self.sem[k] for k in self.engs}
        self.dpool = {}
        for q, n in (("sp", n_dma_sems), ("pool", 16)):
            sems = [nc.alloc_semaphore("dma_%s%d" % (q, i)) for i in range(n)]
            self.dpool[q] = {"sems": sems, "n": 0}
            for i, sm_ in enumerate(sems):
                self.semobj[("D", q, i)] = sm_
        self.dn = 0
        self.nwaits = 0
        self.fences = []
        self.fence_scratch = nc.alloc_sbuf_tensor("fence_scr", [128, 8], F32).ap()

    def _need(self, e, deps):
        best = {}
        for d in deps:
            if d is None:
                continue
            k, v = d
            if e == "pe" and k == ("E", "pe"):
                continue
            if v > best.get(k, 0):
                best[k] = v
        for k, v in best.items():
            if v > self.waited[e].get(k, 0):
                self.engs[e].wait_ge(self.semobj[k], v)
                self.waited[e][k] = v
                self.nwaits += 1

    def _deps(self, r, w):
        deps = []
        for k in r:
            b = self.bufs.get(k)
            if b is not None:
                deps.append(b["w"])
        for k in w:
            b = self.bufs.get(k)
            if b is not None:
                deps.append(b["w"])
                deps.extend(b["r"].items())
        return deps

    def _mark(self, tag, r, w):
        for k in r:
            b = self.bufs.setdefault(k, {"w": None, "r": {}})
            if tag[1] > b["r"].get(tag[0], 0):
                b["r"][tag[0]] = tag[1]
        for k in w:
            self.bufs[k] = {"w": tag, "r": {}}

    def fence(self, name, old_keys):
        self.op("dve", lambda e: e.memset(self.fence_scratch, 0.0), w=list(old_keys) + [("fence", name)])
        self.fences.append(("fence", name))

    def op(self, e, fn, r=(), w=(), signal=True):
        r = list(r) + self.fences
        self._need(e, self._deps(r, w))
        ins = fn(self.engs[e])
        if signal:
            self.cnt[e] += 1
            ins.then_inc(self.sem[e], 1)
            idx = self.cnt[e]
        else:
            assert e == "pe"
            idx = self.cnt[e] + 1
        self._mark((("E", e), idx), r, w)
        return ins

    def dma(self, e, out, in_, r=(), w=(), **kw):
        self.custom_dma(e, lambda g: g.dma_start(out=out, in_=in_, **kw), r=r, w=w)

    def custom_dma(self, e, fn, r=(), w=()):
        P = self.dpool[e]
        ns = len(P["sems"])
        s = P["n"] % ns
        val = 16 * (P["n"] // ns + 1)
        r = list(r) + self.fences
        deps = self._deps(r, w)
        if P["n"] >= ns:
            deps.append((("D", e, s), val - 16))
        self._need(e, deps)
        fn(self.engs[e]).then_inc(P["sems"][s], 16)
        P["n"] += 1
        self.dn += 1
        self._mark((("D", e, s), val), r, w)

    def finish(self, e, keys):
        self._need(e, self._deps(keys, ()))


def build_program(s5_on=True, debug=(), mode='B'):
    nc = bass.Bass("TRN2", target_bir_lowering=False)
    S = Sched(nc)

    def din(name, shape):
        return nc.dram_tensor(name, list(shape), F32, kind="ExternalInput").ap()

    x = din("x", [NT + 3 * TOK, D] if mode == "F" else [NT, D])
    w_in = din("w_in", [D, INW])
    pre_g = din("pre_g", [D])
    conv_w = din("conv_w", [31, D])
    conv_b = din("conv_b", [D])
    ln_g = din("ln_g", [D])
    ln_b = din("ln_b", [D])
    w_co = din("w_co", [D, D])
    lam_re = din("lam_re", [32, 64])
    lam_im = din("lam_im", [32, 64])
    log_dt = din("log_dt", [32])
    b_re = din("b_re", [32, 64, 16])
    b_im = din("b_im", [32, 64, 16])
    c_re = din("c_re", [32, 16, 64])
    c_im = din("c_im", [32, 16, 64])
    d_in = din("d_in", [32, 16])
    w_glu = din("w_glu", [512, 512])
    b_glu = din("b_glu", [512])
    w_so = din("w_so", [512, D])
    w_out = din("w_out", [D, D])
    post_g = din("post_g", [D])
    c_ident = din("c_ident", [128, 128])
    c_kramp = din("c_kramp", [128, NEXP + 2])
    c_bramp = din("c_bramp", [128, NB])
    c_pmask = din("c_pmask", [128, 2])
    c_sel = din("c_sel", [128, 24])
    out_d = nc.dram_tensor("out", [TOK, D], F32, kind="ExternalOutput").ap()
    ag_in = nc.dram_tensor("ag_in", [128, 32], F32).ap()
    ag_out = nc.dram_tensor("ag_out", [4 * 128, 32], F32).ap()

    dbg_out = {}

    def sb(name, shape, dt):
        return nc.alloc_sbuf_tensor(name, list(shape), dt).ap()

    def ps(name, shape, dt=F32):
        return nc.alloc_psum_tensor(name, list(shape), dt).ap()

    hTraw = sb("hT", [128, 8 * NT], BF16)
    hT = hTraw.rearrange("p (k t) -> p k t", k=8)
    uT = sb("uT", [128, 4, TOK], BF16)
    BIGB = 104 * 1024
    big = sb("big", [128, BIGB // 2], BF16)

    def carve(off, shape, dt, base=None):
        base = big if base is None else base
        n = int(np.prod(shape[1:]))
        esz = 4 if dt == F32 else 2
        assert off % 4 == 0
        v = base[:, off // 2: off // 2 + n * esz // 2]
        if dt == F32:
            v = v.bitcast(F32)
        if len(shape) == 3:
            v = v.rearrange("p (a b) -> p a b", a=shape[1])
        elif len(shape) == 4:
            v = v.rearrange("p (a b c) -> p a b c", a=shape[1], b=shape[2])
        elif len(shape) == 5:
            v = v.rearrange("p (a b c d) -> p a b c d", a=shape[1], b=shape[2], c=shape[3])
        return v

    y2 = carve(0, [128, 4, TOK], BF16)
    cu = carve(16384, [128, 8, NT], BF16)
    dg = [carve(16384 + 33280 + i * 7936, [128, 31, 128], BF16) for i in range(2)]
    mrg = carve(16384 + 33280 + 15872, [128, 8, TOK], BF16)
    assert 16384 + 33280 + 15872 + 32768 <= BIGB
    woutb = uT.rearrange("p a b -> p (a b)").rearrange("p (k c) -> p k c", k=8)
    pgB = carve(16384, [128, D], F32, base=hTraw)
    ot = [carve(20480 + i * 4096, [128, D], F32, base=hTraw) for i in range(2)]
    wst = [sb("wst%d" % i, [128, 8, 128], F32) for i in range(2)]
    wbf = [sb("wbf%d" % i, [128, 8, 128], BF16) for i in range(4)]
    xt = [sb("xt%d" % i, [128, D], F32) for i in range(3)]
    xs = [sb("xs%d" % i, [128, D], BF16) for i in range(2)]
    gt = [sb("gt%d" % i, [128, 512], F32) for i in range(2)]
    junk = gt[1].bitcast(BF16)
    K_JUNK = ("gt", id(gt[1]))
    sg = [sb("sg%d" % i, [128, 512], F32) for i in range(4)]
    identf = sb("identf", [128, 128], F32)
    identb = sb("identb", [128, 128], BF16)
    onesb = sb("onesb", [128, 128], BF16)
    gT = sb("gT", [128, 8], F32)
    cbT = sb("cbT", [128, 8], F32)
    lngT = sb("lngT", [128, 8], F32)
    lnbT = sb("lnbT", [128, 8], F32)
    bgluT = sb("bgluT", [128, 4], F32)
    dcol = sb("dcol", [128, 4], F32)
    ssq = sb("ssq", [128, 80], F32)
    rstd = sb("rstd", [128, 80], F32)
    cwT = sb("cwT", [128, 8, 32], F32)
    st_mean, st_rstd, st_tmp = sg[0], sg[1], sg[2]
    K_MEAN, K_RSTD, K_TMP = ("sg", id(sg[0])), ("sg", id(sg[1])), ("sg", id(sg[2]))
    sqb = [sb("sqb%d" % i, [128, 512], BF16) for i in range(2)]

    PS = [ps("ps%d" % i, [128, 512]) for i in range(8)]

    E = S.engs
    for i in range(8):
        pass

    S.dma("pool", identf[:], c_ident, w=["identf"])
    S.op("dve", lambda e: e.tensor_copy(out=identb[:], in_=identf[:]), r=["identf"], w=["identb"])
    S.op("dve", lambda e: e.memset(onesb[:], 1.0), w=["onesb"])
    S.op("dve", lambda e: e.memset(ssq[:], 0.0), w=["ssq"])

    def load_cols(dst, src, n, key):
        S.dma("pool", dst[:, 0:n], src.rearrange("(c p) -> p c", p=128), w=[key], allow_slow_non_contiguous=True)

    load_cols(gT, pre_g, 8, "gT")
    load_cols(cbT, conv_b, 8, "cbT")
    load_cols(lngT, ln_g, 8, "lngT")
    load_cols(lnbT, ln_b, 8, "lnbT")
    load_cols(bgluT, b_glu, 4, "bgluT")

    pst = PS[0].bitcast(BF16)

    p1_cnt = [0]
    p1_pend = []
    XB = xt + [w_.rearrange("p a b -> p (a b)") for w_ in wst]
    KX = [("xt", 0), ("xt", 1), ("xt", 2), ("wst", 0), ("wst", 1)]

    def p1_front(tt_, passno):
        rows = HALO if tt_ == 0 else 128
        r0 = 0 if tt_ == 0 else HALO + (tt_ - 1) * 128
        xr = r0 if passno == 3 else NT + passno * TOK + (tt_ - 1) * 128
        g = p1_cnt[0]
        p1_cnt[0] += 1
        b = g % 5
        tt = passno * 17 + tt_
        S.dma("sp", XB[b][:rows, :], x[xr:xr + rows, :], w=[KX[b]])
        S.op("act", lambda e: e.activation(out=junk[:rows, :], in_=XB[b][:rows, :], func=AF.Square,
                                           accum_out=ssq[:rows, tt:tt + 1]),
             r=[KX[b], "ssq"], w=[K_JUNK, ("ssq", tt)])
        S.op("act", lambda e: e.activation(out=rstd[:rows, tt:tt + 1], in_=ssq[:rows, tt:tt + 1], func=AF.Sqrt,
                                           scale=1.0 / D, bias=epsc[:rows, 0:1]),
             r=[("ssq", tt), "epsc"], w=[("rstd", tt)])
        S.op("dve", lambda e: e.reciprocal(out=rstd[:rows, tt:tt + 1], in_=rstd[:rows, tt:tt + 1]),
             r=[("rstd", tt)], w=[("rstd", tt)])
        return (tt_, passno, g)

    def p1_back(tt_, passno, g):
        rows = HALO if tt_ == 0 else 128
        r0 = 0 if tt_ == 0 else HALO + (tt_ - 1) * 128
        b = g % 5
        pbk = g % 2
        pst = PS[pbk].bitcast(BF16)
        tt = passno * 17 + tt_
        if g % 2 == 1:
            S.op("dve", lambda e: e.tensor_scalar(out=xs[pbk][:rows, :], in0=XB[b][:rows, :],
                                                  scalar1=rstd[:rows, tt:tt + 1], scalar2=None, op0=ALU.mult),
                 r=[KX[b], ("rstd", tt)], w=[("xs", pbk)])
        else:
            S.op("act", lambda e: e.activation(out=xs[pbk][:rows, :], in_=XB[b][:rows, :], func=AF.Copy,
                                               scale=rstd[:rows, tt:tt + 1]),
                 r=[KX[b], ("rstd", tt)], w=[("xs", pbk)])
        pv = pst.rearrange("p (k t) -> p k t", k=8)
        for kc in range(8):
            S.op("pe", lambda e: e.transpose(out=pv[:, kc, :rows], in_=xs[pbk][:rows, kc * 128:(kc + 1) * 128],
                                             identity=identb[:rows, :rows]),
                 r=[("xs", pbk), "identb"], w=[("ps", pbk)], signal=(kc == 7))
        S.op("dve", lambda e: e.tensor_tensor(out=hT[:, :, r0:r0 + rows], in0=pv[:, :, :rows],
                                              in1=gT[:, :].unsqueeze(2).broadcast_to([128, 8, rows]), op=ALU.mult),
             r=[("ps", pbk), "gT"], w=[("hT", tt_)])

    def p1_push(tt_, passno):
        st = p1_front(tt_, passno)
        if os.environ.get("NO_STAG"):
            p1_back(*st)
            return
        if p1_pend:
            p1_back(*p1_pend.pop())
        p1_pend.append(st)

    def p1_flush():
        if p1_pend:
            p1_back(*p1_pend.pop())

    epsc = sb("epsc", [128, 2], F32)
    S.op("dve", lambda e: e.memset(epsc[:, 0:1], 1e-6), w=["epsc"])
    S.op("dve", lambda e: e.memset(epsc[:, 1:2], 1e-5), r=["epsc"], w=["epsc"])

    def run_p1(passno):
        for tt in range(0 if passno == 3 else 1, 17):
            p1_push(tt, passno)
        p1_flush()

    def p1_tb_tiles(passno, tb):
        return ([0] if (passno == 3 and tb == 0) else []) + list(range(1 + 4 * tb, 5 + 4 * tb))

    FUSED = (mode == "F" and s5_on)
    if not FUSED:
        run_p1(3)

    HT_KEYS = [("hT", tt) for tt in range(17)]

    def ht_keys(tb):
        return [("hT", 1 + tb * 4 + i) for i in range(4)]

    slab_n = [0]

    plan = []
    plan += [(w_in, OFF_ZS + c * 128, 8) for c in range(4)]
    plan += [(w_glu, c * 128, 4) for c in range(4)]
    for i in range(8):
        plan += [(w_in, OFF_CA + i * 128, 8), (w_in, OFF_CB + i * 128, 8)]
    plan += [(w_in, OFF_ZC + i * 128, 8) for i in range(8)]
    for j in range(8):
        plan += [(w_co, j * 128, 8), (w_so, j * 128, 4), (w_in, OFF_GC + j * 128, 8), (w_in, OFF_GS + j * 128, 8)]
    issued = [0]

    def _issue(n):
        src, col0, kch = plan[n]
        a, b = n % 2, n % 4
        S.dma("sp", wst[a][:, :kch, :], src[:, col0:col0 + 128].rearrange("(k p) c -> p k c", p=128),
              w=[("wst", a)])
        S.op("pool", lambda e: e.tensor_copy(out=wbf[b][:, :kch, :], in_=wst[a][:, :kch, :]),
             r=[("wst", a)], w=[("wbf", b)])

    def slab_load(src, col0, kch=8):
        n = slab_n[0]
        slab_n[0] += 1
        assert plan[n][1] == col0 and plan[n][2] == kch and plan[n][0] is src, (n, col0, kch)
        while issued[0] <= min(n + 1, len(plan) - 1):
            _issue(issued[0])
            issued[0] += 1
        return n % 4

    def mm_block(pbank, b, rhs_fn, rkeys, kch=8, n=512):
        for k in range(kch):
            S.op("pe", lambda e: e.matmul(PS[pbank][:, :n], lhsT=wbf[b][:, k, :], rhs=rhs_fn(k),
                                          start=(k == 0), stop=(k == kch - 1)),
                 r=[("wbf", b)] + rkeys, w=[("ps", pbank)], signal=(k == kch - 1))

    bank = [1]

    def nextbank(lo=1, hi=8):
        b = bank[0]
        if not (lo <= b < hi):
            b = lo
        bank[0] = lo + (b + 1 - lo) % (hi - lo)
        return b

    for c in range(4):
        a = c % 2
        S.dma("sp", wst[a][:, :, :], w_in[:, OFF_U + c * 128: OFF_U + (c + 1) * 128].rearrange("(k p) c -> p k c", p=128),
              w=[("wst", a)])
        S.op("pool", lambda e: e.tensor_copy(out=wbf[c][:, :, :], in_=wst[a][:, :, :]), r=[("wst", a)], w=[("wbf", c)])

    def run_p2_tb(tb):
        for c in range(4):
            if True:
                pb = nextbank(2, 8)
                for k in range(8):
                    S.op("pe", lambda e: e.matmul(PS[pb][:, :], lhsT=wbf[c][:, k, :],
                                                  rhs=hT[:, k, HALO + tb * 512: HALO + (tb + 1) * 512],
                                                  start=(k == 0), stop=(k == 7)),
                         r=[("wbf", c)] + ht_keys(tb), w=[("ps", pb)], signal=(k == 7))
                if s5_on and c % 2 == 1:
                    S.op("dve", lambda e: e.tensor_copy(
                        out=uT.rearrange("p c (j b) -> p c j b", j=16)[:, c, :, tb * 32:(tb + 1) * 32],
                        in_=PS[pb][:, :].rearrange("p (b j) -> p j b", j=16)),
                         r=[("ps", pb)], w=[("uT", c, tb)])
                elif s5_on:
                    S.op("act", lambda e: e.activation(
                        out=uT.rearrange("p c (j b) -> p c j b", j=16)[:, c, :, tb * 32:(tb + 1) * 32],
                        in_=PS[pb][:, :].rearrange("p (b j) -> p j b", j=16), func=AF.Copy),
                         r=[("ps", pb)], w=[("uT", c, tb)])
                else:
                    S.op("act", lambda e: e.activation(out=uT[:, c, tb * 512:(tb + 1) * 512], in_=PS[pb][:, :],
                                                       func=AF.Copy),
                         r=[("ps", pb)], w=[("uT", c, tb)])

    def run_p2():
        for tb in range(4):
            run_p2_tb(tb)

    if mode != "F":
        run_p2()


    if s5_on:
        PI2 = float(np.pi / 2)
        A_ = lambda off, shape, dt: carve(off, shape, dt)
        ar = A_(0, [128, 19, 16], F32)
        ai = A_(1216, [128, 19, 16], F32)
        mag = A_(2432, [128, 19, 16], F32)
        ang = A_(3648, [128, 19, 16], F32)
        nn = A_(4864, [128, 19, 16], F32)
        sm = A_(6080, [128, 16, 16], F32)
        LR, LI, DT, LDR, LDI, DEN, ZR, ZI, AM1, T0, T1, T2 = [sm[:, i, :] for i in range(12)]
        Braw = [A_(7168 + i * 1024, [128, 16, 16], F32) for i in range(2)]
        bb = [A_(9216 + i * 1024, [128, 16, 16], F32) for i in range(2)]
        bbBD = [A_(11264 + i * 2048, [128, 16, 32], F32) for i in range(2)]
        pad = [A_(15360 + i * 4096, [128, 16, 128], BF16) for i in range(2)]
        Craw = [A_(23552 + i * 1024, [128, 4, 64], F32) for i in range(2)]
        Cexp = [A_(25600 + i * 512, [128, 128], F32) for i in range(2)]
        CBD = [A_(26624 + i * 2048, [128, 16, 32], F32) for i in range(2)]
        cosT = A_(30720, [128, 16, 128], F32)
        sinT = A_(38912, [128, 16, 128], F32)
        zz = [A_(47104 + i * 8192, [128, 16, 128], F32) for i in range(2)]
        ww = zz
        Eall = A_(63488, [128, 4, 16, 2, 128], BF16)
        tmp = A_(96256, [128, 4, 128], F32)
        tmp2 = A_(98304, [128, 4, 128], F32)
        Sst = [A_(63488 + i * 4224, [128, 16, 132], BF16) for i in range(2)]
        Xt = A_(96256, [128, 16, 2, 128], BF16)
        CAt = A_(47104, [128, 17, 4, 2, 32], BF16)
        lagT = A_(55296 + 1024, [128, 16, 128], BF16)
        CAtmp = [A_(72192 + i * 8704, [128, 17, 4, 32], F32) for i in range(2)]
        CAt_b = [CAt, A_(89600, [128, 17, 4, 2, 32], BF16)]
        lagT_b = [lagT, A_(98304, [128, 16, 128], BF16)]
        ZK0 = [("zz0", c) for c in range(4)]
        ZK1 = [("zz1", c) for c in range(4)]
        uPM = uT.rearrange("p c (j b) -> p c j b", j=16)
        kramp = A_(104448, [128, 19], F32)
        bramp = A_(104448 + 128, [128, NB], F32)
        pmask = sb("pmask", [128, 2], F32)
        selt = sb("selt", [128, 24], F32)
        agbuf = sb("agbuf", [128, 32], F32)
        Gt = sb("Gt", [128, 4, 32], F32)
        Tm = sb("Tm", [128, 3, 32], F32)
        Sin = sb("Sin", [128, 32], F32)

        from collections import deque
        from functools import partial
        bgq = deque()
        tick_n = [0]

        def tick():
            tick_n[0] += 1
            if bgq and tick_n[0] % 4 == 0:
                bgq.popleft()()

        def TT(out, a, b, op, r, w, e="dve"):
            tick()
            S.op(e, lambda g: g.tensor_tensor(out=out, in0=a, in1=b, op=op), r=r, w=w)

        def TS(out, a, s1, s2, op0, op1, r, w, e="dve"):
            if s2 is None:
                S.op(e, lambda g: g.tensor_scalar(out=out, in0=a, scalar1=s1, scalar2=None, op0=op0), r=r, w=w)
            else:
                S.op(e, lambda g: g.tensor_scalar(out=out, in0=a, scalar1=s1, scalar2=s2, op0=op0, op1=op1), r=r, w=w)

        def STT(out, a, sc, b, op0, op1, r, w, e="dve"):
            S.op(e, lambda g: g.scalar_tensor_tensor(out=out, in0=a, scalar=sc, in1=b, op0=op0, op1=op1), r=r, w=w)

        def ACTF(out, a, func, r, w, **kw):
            S.op("act", lambda g: g.activation(out=out, in_=a, func=func, **kw), r=r, w=w)

        def bc(ap, axis, shape):
            return ap.unsqueeze(axis).broadcast_to(shape)

        halfpi = sb("halfpi", [128, 1], F32)
        S.op("dve", lambda g: g.memset(halfpi[:, :], PI2), w=["halfpi"])

        def sincos(a, n, A, ka, kn, kA, bshape=None):
            KN = kn if isinstance(kn, list) else [kn]
            TS(n, a, 1.0 / TWO_PI, MAGIC, ALU.mult, ALU.add, [ka], KN)
            TS(n, n, MAGIC, None, ALU.subtract, None, KN, KN)
            STT(a, n, -C1, a, ALU.mult, ALU.add, KN + [ka], [ka])
            STT(a, n, -C2, a, ALU.mult, ALU.add, KN + [ka], [ka])
            STT(n, a, -1.0, a, ALU.mult, ALU.max, [ka], KN)
            ACTF(A, a, AF.Sin, [ka], [kA], scale=0.5)
            ACTF(a, n, AF.Sin, KN + [ka, "halfpi"], [ka], scale=-0.5, bias=halfpi[:, 0:1])
            STT(a, A, 2.0, a, ALU.mult, ALU.mult, [kA, ka], [ka])
            TT(A, A, A, ALU.mult, [kA], [kA])
            TS(A, A, -2.0, 1.0, ALU.mult, ALU.add, [kA], [kA])

        def range_reduce(a, n, ka, kn):
            TS(n, a, 1.0 / TWO_PI, MAGIC, ALU.mult, ALU.add, [ka], [kn])
            TS(n, n, MAGIC, None, ALU.subtract, None, [kn], [kn])
            STT(a, n, -C1, a, ALU.mult, ALU.add, [kn, ka], [ka])
            STT(a, n, -C2, a, ALU.mult, ALU.add, [kn, ka], [ka])
            TS(n, a, float(np.pi), -TWO_PI, ALU.is_gt, ALU.mult, [ka], [kn])
            TT(a, a, n, ALU.add, [ka, kn], [ka])
            TS(n, a, -float(np.pi), TWO_PI, ALU.is_lt, ALU.mult, [ka], [kn])
            TT(a, a, n, ALU.add, [ka, kn], [ka])
            TS(a, a, 3.1415925, -3.1415925, ALU.min, ALU.max, [ka], [ka])

        for gl in range(2):
            rs = slice(gl * 64, (gl + 1) * 64)
            S.dma("pool", LR[rs, :], lam_re.rearrange("(q gl) p -> gl p q", gl=2)[gl], w=["sm"],
                  allow_slow_non_contiguous=True)
            S.dma("pool", LI[rs, :], lam_im.rearrange("(q gl) p -> gl p q", gl=2)[gl], w=["sm"],
                  allow_slow_non_contiguous=True)
            S.dma("pool", DT[rs, :], bass.AP(log_dt.tensor, gl, [[0, 64], [2, 16]]), w=["sm"],
                  allow_slow_non_contiguous=True)
            S.dma("pool", Braw[0][rs, :, :], b_re.rearrange("(q gl) p h -> gl p q h", gl=2)[gl], w=["Braw"])
            S.dma("pool", Braw[1][rs, :, :], b_im.rearrange("(q gl) p h -> gl p q h", gl=2)[gl], w=["Braw"])
        S.dma("pool", Craw[0][:, :, :], c_re.rearrange("(c g) h p -> (g h) c p", c=4), w=["Craw"])
        S.dma("pool", Craw[1][:, :, :], c_im.rearrange("(c g) h p -> (g h) c p", c=4), w=["Craw"])
        S.dma("pool", dcol[:, :], d_in.rearrange("(c g) h -> (g h) c", c=4), w=["dcol"], allow_slow_non_contiguous=True)
        S.dma("pool", kramp[:], c_kramp, w=["kramp"])
        S.dma("pool", bramp[:], c_bramp, w=["bramp"])
        S.dma("pool", pmask[:], c_pmask, w=["pmask"])
        S.dma("pool", selt[:], c_sel, w=["selt"])

        if FUSED:
            for tt in range(1, 17):
                bgq.append(partial(p1_push, tt, 0))
            bgq.append(p1_flush)
            for tb in range(4):
                bgq.append(partial(run_p2_tb, tb))
                for tt in p1_tb_tiles(1, tb):
                    bgq.append(partial(p1_push, tt, 1))
            bgq.append(p1_flush)
            if True:
                while bgq:
                    bgq.popleft()()

        ACTF(DT, DT, AF.Exp, ["sm"], ["sm"])
        TT(LDR, LR, DT, ALU.mult, ["sm"], ["sm"])
        TT(LDI, LI, DT, ALU.mult, ["sm"], ["sm"])
        sh3 = [128, 19, 16]
        TT(mag, bc(kramp[:, :], 2, sh3), bc(LDR, 1, sh3), ALU.mult, ["kramp", "sm"], ["mag"])
        ACTF(mag, mag, AF.Exp, ["mag"], ["mag"])
        TT(ang, bc(kramp[:, :], 2, sh3), bc(LDI, 1, sh3), ALU.mult, ["kramp", "sm"], ["ang"])
        sincos(ang, nn, ar, "ang", "nn", "ar")
        TT(ar, ar, mag, ALU.mult, ["ar", "mag"], ["ar"])
        TT(ai, ang, mag, ALU.mult, ["ang", "mag"], ["ai"])
        AK = ["ar", "ai"]

        sh4 = [128, 16, NB]
        TT(sinT, bc(LDI, 2, sh4), bc(bramp[:, :], 1, sh4), ALU.mult, ["sm", "bramp"], ["sinT"])
        sincos(sinT, zz[0], cosT, "sinT", ZK0, "cosT")

        TT(DEN, LR, LR, ALU.mult, ["sm"], ["sm"])
        TT(T0, LI, LI, ALU.mult, ["sm"], ["sm"])
        TT(DEN, DEN, T0, ALU.add, ["sm"], ["sm"])
        S.op("dve", lambda g: g.reciprocal(out=DEN, in_=DEN), r=["sm"], w=["sm"])
        TS(AM1, ar[:, 1, :], -1.0, None, ALU.add, None, ["ar", "sm"], ["sm"])
        TT(T0, AM1, LR, ALU.mult, ["sm"], ["sm"])
        TT(T1, ai[:, 1, :], LI, ALU.mult, ["ai", "sm"], ["sm"])
        TT(T0, T0, T1, ALU.add, ["sm"], ["sm"])
        TT(ZR, T0, DEN, ALU.mult, ["sm"], ["sm"])
        TT(T0, ai[:, 1, :], LR, ALU.mult, ["ai", "sm"], ["sm"])
        TT(T1, AM1, LI, ALU.mult, ["sm"], ["sm"])
        TT(T0, T0, T1, ALU.subtract, ["sm"], ["sm"])
        TT(ZI, T0, DEN, ALU.mult, ["sm"], ["sm"])
        shb = [128, 16, 16]
        t_a, t_b = CBD[0][:, :, 0:16], CBD[0][:, :, 16:32]
        TT(t_a, bc(ZR, 2, shb), Braw[0][:, :, :], ALU.mult, ["sm", "Braw"], ["CBD"])
        TT(t_b, bc(ZI, 2, shb), Braw[1][:, :, :], ALU.mult, ["sm", "Braw"], ["CBD"])
        TT(bb[0][:, :, :], t_a, t_b, ALU.subtract, ["CBD"], ["bb"])
        TT(t_a, bc(ZR, 2, shb), Braw[1][:, :, :], ALU.mult, ["sm", "Braw", "bb"], ["CBD"])
        TT(t_b, bc(ZI, 2, shb), Braw[0][:, :, :], ALU.mult, ["sm", "Braw"], ["CBD"])
        TT(bb[1][:, :, :], t_a, t_b, ALU.add, ["CBD"], ["bb"])
        for i in range(2):
            S.op("dve", lambda g: g.memset(bbBD[i][:, :, :], 0.0), w=["bbBD"])
            S.op("pool", lambda g: g.memset(pad[i][:, :, :], 0.0), w=["pad"])
            for gl in range(2):
                rs = slice(gl * 64, (gl + 1) * 64)
                S.op("dve", lambda g: g.tensor_copy(out=bbBD[i][rs, :, gl * 16:(gl + 1) * 16], in_=bb[i][rs, :, :]),
                     r=["bb", "bbBD"], w=["bbBD"])
            for qq in range(4):
                S.op("dve", lambda g: g.tensor_copy(out=pad[i][:, qq::4, 32 * qq:32 * qq + 32], in_=bbBD[i][:, qq::4, :]),
                     r=["bbBD", "pad"], w=["pad"])

        for i in range(2):
            pc = PS[i][:, :].rearrange("p (q c) -> p q c", q=16)
            for c in range(4):
                for gl in range(2):
                    TS(Cexp[c % 2][:, gl * 64:(gl + 1) * 64], Craw[i][:, c, :], pmask[:, gl:gl + 1], None, ALU.mult, None,
                       ["Craw", "pmask", ("Cexp", c % 2)], [("Cexp", c % 2)])
                for qq in range(4):
                    S.op("pe", lambda g: g.matmul(pc[:, 4 * c + qq, :], lhsT=Cexp[c % 2][:, :],
                                                  rhs=identf[:, 32 * qq:32 * qq + 32], start=True, stop=True),
                         r=[("Cexp", c % 2), "identf"], w=[("ps", i)], signal=True)
            S.op("dve", lambda g: g.tensor_copy(out=CBD[i][:, :, :], in_=pc), r=[("ps", i), "CBD"], w=["CBD"])

        def build_E():
            shx = [128, 16, 4, 32]
            Xv = lambda ri: Xt[:, :, ri, :].rearrange("p k (q c) -> p k q c", q=4)
            W0 = zz[0].rearrange("p k (q c) -> p k q c", q=4)
            W1 = zz[1].rearrange("p k (q c) -> p k q c", q=4)
            for c in (0, 1, 2, 3):
                qs = slice(4 * c, 4 * c + 4)
                a_r = bc(ar[:, 0:16, qs], 3, shx)
                a_i = bc(ai[:, 0:16, qs], 3, shx)
                b_r = bc(bbBD[0][:, qs, :], 1, shx)
                b_i = bc(bbBD[1][:, qs, :], 1, shx)
                TT(W0, a_r, b_r, ALU.mult, AK + ["bbBD"], ZK0)
                TT(W1, a_i, b_i, ALU.mult, AK + ["bbBD"], ZK1)
                TT(Xv(0), W0, W1, ALU.subtract, ZK0 + ZK1, ["Xt", "tmp", "tmp2"])
                TT(W0, a_r, b_i, ALU.mult, AK + ["bbBD", "Xt"], ZK0)
                TT(W1, a_i, b_r, ALU.mult, AK + ["bbBD", "Xt"], ZK1)
                TT(Xv(1), W0, W1, ALU.add, ZK0 + ZK1, ["Xt", "tmp", "tmp2"])
                for k4 in range(4):
                    pe_ = PS[k4].bitcast(BF16).rearrange("p (s c) -> p s c", s=8)
                    for kk in range(4):
                        for ri in range(2):
                            k = 4 * k4 + kk
                            S.op("pe", lambda g: g.transpose(out=pe_[:, 2 * kk + ri, :], in_=Xt[:, k, ri, :], identity=identb[:, :]),
                                 r=["Xt", "tmp", "tmp2", "identb"], w=[("ps", k4)], signal=(kk == 3 and ri == 1))
                    ACTF(Eall[:, c, 4 * k4:4 * k4 + 4, :, :].rearrange("p a b c -> p (a b) c"), pe_, AF.Copy,
                         [("ps", k4)], ["Eall"])

        def p3a():
            for c in range(4):
                qs = slice(4 * c, 4 * c + 4)
                pl = [PS[4 + 2 * (c % 2) + ri][:, :].rearrange("p (q b) -> p q b", q=4) for ri in range(2)]
                for qq in range(4):
                    rs = slice(32 * qq, 32 * qq + 32)
                    for ri in range(2):
                        for j in range(16):
                            S.op("pe", lambda g: g.matmul(pl[ri][:, qq, :], lhsT=Eall[rs, c, 15 - j, ri, :], rhs=uPM[rs, c, j, :],
                                                          start=(j == 0), stop=(j == 15), tile_position=(32 * qq, 0)),
                                 r=["Eall", "EallT"] + [("uT", c, tb) for tb in range(4)], w=[("ps", 4 + 2 * (c % 2) + ri)],
                                 signal=(j == 15 and qq == 3))
                kl = [("ps", 4 + 2 * (c % 2)), ("ps", 5 + 2 * (c % 2))]
                cs, sn = cosT[:, qs, :], sinT[:, qs, :]
                TT(tmp, pl[1], sn, ALU.mult, [kl[1], "sinT"], ["tmp"])
                TT(zz[0][:, qs, :], pl[0], cs, ALU.mult, [kl[0], "cosT"], [("zz0", c)])
                TT(zz[0][:, qs, :], zz[0][:, qs, :], tmp, ALU.add, [("zz0", c), "tmp"], [("zz0", c)])
                TT(tmp2, pl[0], sn, ALU.mult, [kl[0], "sinT"], ["tmp2"])
                TT(zz[1][:, qs, :], pl[1], cs, ALU.mult, [kl[1], "cosT"], [("zz1", c)])
                TT(zz[1][:, qs, :], zz[1][:, qs, :], tmp2, ALU.subtract, [("zz1", c), "tmp2"], [("zz1", c)])

        ZK = [ZK0, ZK1]

        def scan_pass(init_fn, kin):
            for q in range(16):
                for ri in range(2):
                    S.op("dve", lambda g: g.tensor_tensor_scan(out=ww[ri][:, q, :],
                                                               data0=mag[:, 16, q:q + 1].broadcast_to([128, NB]),
                                                               data1=zz[ri][:, q, :], initial=init_fn(ri, q),
                                                               op0=ALU.mult, op1=ALU.add),
                         r=["mag", ("zz%d" % ri, q // 4)] + kin, w=[("zz%d" % ri, q // 4)])

        def local_final():
            c127, s127 = cosT[:, :, NB - 1], sinT[:, :, NB - 1]
            wr127, wi127 = ww[0][:, :, NB - 1], ww[1][:, :, NB - 1]
            TT(T0, c127, wr127, ALU.mult, ["cosT", *ZK0, "sm"], ["sm"])
            TT(T1, s127, wi127, ALU.mult, ["sinT", *ZK1, "sm"], ["sm"])
            TT(agbuf[:, 0:16], T0, T1, ALU.subtract, ["sm"], ["agbuf"])
            TT(T0, s127, wr127, ALU.mult, ["sinT", *ZK0, "sm"], ["sm"])
            TT(T1, c127, wi127, ALU.mult, ["cosT", *ZK1, "sm"], ["sm"])
            TT(agbuf[:, 16:32], T0, T1, ALU.add, ["sm", "agbuf"], ["agbuf"])

        def accumulate(idx):
            pr, pi_ = ar[:, idx, :], ai[:, idx, :]
            tr, ti = agbuf[:, 0:16], agbuf[:, 16:32]
            TT(T0, pr, tr, ALU.mult, AK + ["agbuf", "sm"], ["sm"])
            TT(T1, pi_, ti, ALU.mult, AK + ["agbuf", "sm"], ["sm"])
            TT(T0, T0, T1, ALU.subtract, ["sm"], ["sm"])
            TT(Sin[:, 0:16], Sin[:, 0:16], T0, ALU.add, ["sm", "Sin"], ["Sin"])
            TT(T0, pr, ti, ALU.mult, AK + ["agbuf", "sm"], ["sm"])
            TT(T1, pi_, tr, ALU.mult, AK + ["agbuf", "sm"], ["sm"])
            TT(T0, T0, T1, ALU.add, ["sm"], ["sm"])
            TT(Sin[:, 16:32], Sin[:, 16:32], T0, ALU.add, ["sm", "Sin"], ["Sin"])

        build_E()
        if mode == "F":
            S.op("dve", lambda g: g.memset(Sin[:, :], 0.0), w=["Sin"])
            while bgq:
                bgq.popleft()()
            for m_ in range(4):
                if m_ > 0:
                    for tb in range(4):
                        run_p2_tb(tb)
                        if m_ < 3:
                            for tt in p1_tb_tiles(m_ + 1, tb):
                                p1_push(tt, m_ + 1)
                    p1_flush()
                p3a()
                if m_ < 3:
                    scan_pass(lambda ri, q: 0.0, [])
                    local_final()
                    accumulate([0, 17, 18][m_])
        else:
            p3a()
            scan_pass(lambda ri, q: 0.0, [])
            local_final()
            if mode == 'A':
                sloc = nc.dram_tensor("sloc", [128, 32], F32, kind="ExternalOutput").ap()
                S.dma("sp", sloc, agbuf[:, :], r=["agbuf"], w=["sloc"])
                S.finish("sp", ["sloc"])
                return nc, S
            if mode == 'B':
                gin = nc.dram_tensor("gin", [4 * 128, 32], F32, kind="ExternalInput").ap()
                S.dma("pool", ag_out, gin, w=["ag_out"])
            elif NO_CC:
                for jj in range(4):
                    S.dma("pool", ag_out[jj * 128:(jj + 1) * 128, :], ag_in, r=["ag_in"], w=["ag_out"])
            else:
                S.dma("pool", ag_in, agbuf[:, :], r=["agbuf"], w=["ag_in"])
                S.custom_dma("pool", lambda g: g.collective_compute("AllGather", ALU.bypass,
                                                                    replica_groups=[[0, 1, 2, 3], [4, 5, 6, 7]],
                                                                    ins=[ag_in], outs=[ag_out]),
                             r=["ag_in"], w=["ag_out"])
            S.dma("pool", Gt[:, :, :], ag_out.rearrange("(j p) c -> p j c", p=128), r=["ag_out"], w=["Gt"])
            for m in range(3):
                for j in range(4):
                    if j == 0:
                        TS(Tm[:, m, :], Gt[:, 0, :], selt[:, m:m + 1], None, ALU.mult, None, ["Gt", "selt", "Tm"], ["Tm"])
                    else:
                        STT(Tm[:, m, :], Gt[:, j, :], selt[:, 3 * j + m:3 * j + m + 1], Tm[:, m, :], ALU.mult, ALU.add,
                            ["Gt", "selt", "Tm"], ["Tm"])
            S.op("dve", lambda g: g.tensor_copy(out=Sin[:, :], in_=Tm[:, 0, :]), r=["Tm"], w=["Sin"])
            for m in (1, 2):
                pr, pi_ = ar[:, 16 + m, :], ai[:, 16 + m, :]
                tr, ti = Tm[:, m, 0:16], Tm[:, m, 16:32]
                TT(T0, pr, tr, ALU.mult, AK + ["Tm", "sm"], ["sm"])
                TT(T1, pi_, ti, ALU.mult, AK + ["Tm", "sm"], ["sm"])
                TT(T0, T0, T1, ALU.subtract, ["sm"], ["sm"])
                TT(Sin[:, 0:16], Sin[:, 0:16], T0, ALU.add, ["sm", "Sin"], ["Sin"])
                TT(T0, pr, ti, ALU.mult, AK + ["Tm", "sm"], ["sm"])
                TT(T1, pi_, tr, ALU.mult, AK + ["Tm", "sm"], ["sm"])
                TT(T0, T0, T1, ALU.add, ["sm"], ["sm"])
                TT(Sin[:, 16:32], Sin[:, 16:32], T0, ALU.add, ["sm", "Sin"], ["Sin"])
        scan_pass(lambda ri, q: Sin[:, 16 * ri + q:16 * ri + q + 1], ["Sin"])
        for ri in range(2):
            S.op("dve", lambda g: g.tensor_copy(out=Sst[ri][:, :, 0], in_=Sin[:, 16 * ri:16 * ri + 16]),
                 r=["Sin"], w=[("Sst", ri), "Eall"])
        for c in range(4):
            qs = slice(4 * c, 4 * c + 4)
            cs, sn = cosT[:, qs, :], sinT[:, qs, :]
            TT(tmp, cs, ww[0][:, qs, :], ALU.mult, ["cosT", *ZK0], ["tmp"])
            TT(tmp2, sn, ww[1][:, qs, :], ALU.mult, ["sinT", *ZK1], ["tmp2"])
            TT(Sst[0][:, qs, 1:NB + 1], tmp, tmp2, ALU.subtract, ["tmp", "tmp2"], [("Sst", 0), "Eall"])
            TT(tmp, sn, ww[0][:, qs, :], ALU.mult, ["sinT", *ZK0], ["tmp"])
            TT(tmp2, cs, ww[1][:, qs, :], ALU.mult, ["cosT", *ZK1], ["tmp2"])
            TT(Sst[1][:, qs, 1:NB + 1], tmp, tmp2, ALU.add, ["tmp", "tmp2"], [("Sst", 1), "Eall"])

        shc = [128, 17, 4, 32]
        ZZW = [*ZK0, *ZK1, "tmp", "tmp2"] + ZK[0] + ZK[1]
        for c in range(4):
            CAt, lagT = CAt_b[c % 2], lagT_b[c % 2]
            KCA, KLG = ("CAt", c % 2), ("lagT", c % 2)
            XK = ["tmp", "tmp2"] if c % 2 == 1 else []
            qs = slice(4 * c, 4 * c + 4)
            a_r = bc(ar[:, 0:17, qs], 3, shc)
            a_i = bc(ai[:, 0:17, qs], 3, shc)
            c_r = bc(CBD[0][:, qs, :], 1, shc)
            c_i = bc(CBD[1][:, qs, :], 1, shc)
            TT(CAtmp[0], a_r, c_r, ALU.mult, AK + ["CBD"] + ZZW, ["CAtmp0", "EallT"])
            TT(CAtmp[1], a_i, c_i, ALU.mult, AK + ["CBD"] + ZZW, ["CAtmp1", "EallT"])
            TT(CAt[:, :, :, 0, :], CAtmp[0], CAtmp[1], ALU.subtract, ["CAtmp0", "CAtmp1"] + ZZW, [KCA] + XK)
            TT(CAtmp[0], a_r, c_i, ALU.mult, AK + ["CBD", KCA], ["CAtmp0"])
            TT(CAtmp[1], a_i, c_r, ALU.mult, AK + ["CBD", KCA], ["CAtmp1"])
            STT(CAt[:, :, :, 1, :], CAtmp[0], -1.0, CAtmp[1], ALU.mult, ALU.subtract, ["CAtmp0", "CAtmp1"], [KCA])
            for k4 in range(4):
                pg = PS[k4][:, :].rearrange("p (s c) -> p s c", s=4)
                for kk in range(4):
                    k = 4 * k4 + kk
                    for qq in range(4):
                        q = 4 * c + qq
                        S.op("pe", lambda g: g.matmul(pg[:, kk, 32 * qq:32 * qq + 32], lhsT=pad[0][:, q, :],
                                                      rhs=CAt[:, k, qq, 0, :], start=True, stop=False),
                             r=["pad", KCA], w=[("ps", k4)], signal=False)
                        S.op("pe", lambda g: g.matmul(pg[:, kk, 32 * qq:32 * qq + 32], lhsT=pad[1][:, q, :],
                                                      rhs=CAt[:, k, qq, 1, :], start=False, stop=True),
                             r=["pad", KCA], w=[("ps", k4)], signal=(kk == 3 and qq == 3))
                if k4 == 0:
                    STT(pg[:, 0, :], identf[:, :], dcol[:, c:c + 1], pg[:, 0, :], ALU.mult, ALU.add,
                        ["identf", "dcol", ("ps", 0)], [("ps", 0)])
                ACTF(lagT[:, 4 * k4:4 * k4 + 4, :], pg, AF.Copy, [("ps", k4)] + ZZW, [KLG] + XK)
            UK = [("uT", c, tb) for tb in range(4)]
            for j in range(16):
                py = PS[4 + j // 4][:, :].rearrange("p (s b) -> p s b", s=4)[:, j % 4, :]
                kb = ("ps", 4 + j // 4)
                for qq in range(4):
                    q = 4 * c + qq
                    rs = slice(32 * qq, 32 * qq + 32)
                    for ri in range(2):
                        S.op("pe", lambda g: g.matmul(py[rs, :], lhsT=CAt[:, j + 1, qq, ri, :], rhs=Sst[ri][:, q, 0:NB],
                                                      start=(ri == 0), stop=False, tile_position=(0, 32 * qq)),
                             r=[KCA, ("Sst", ri)], w=[kb], signal=False)
                for i in range(j + 1):
                    S.op("pe", lambda g: g.matmul(py[:, :], lhsT=lagT[:, j - i, :], rhs=uPM[:, c, i, :],
                                                  start=False, stop=(i == j)),
                         r=[KLG] + UK, w=[kb], signal=(i == j and j % 4 == 3))
            uv = uT[:, c, :].rearrange("p (b j) -> p j b", j=16)
            for j4 in range(4):
                pyv = PS[4 + j4][:, :].rearrange("p (s b) -> p s b", s=4)
                if j4 % 2 == 1:
                    S.op("dve", lambda g: g.tensor_copy(out=uv[:, 4 * j4:4 * j4 + 4, :], in_=pyv), r=[("ps", 4 + j4)], w=UK)
                else:
                    ACTF(uv[:, 4 * j4:4 * j4 + 4, :], pyv, AF.Copy, [("ps", 4 + j4)], UK)
        S.fence("s5done", ["sm", "ar", "ai", "mag", "ang", "nn", "Braw", "bb", "bbBD", "pad", "Craw", "CBD", "cosT",
                           "sinT", *ZK0, *ZK1, "tmp", "tmp2", "Eall", "Xt", ("CAt", 0), ("CAt", 1), ("lagT", 0), ("lagT", 1), "CAtmp0", "CAtmp1",
                           ("Sst", 0), ("Sst", 1), ("Cexp", 0), ("Cexp", 1), "zz0"] + ZK[0] + ZK[1])

    UT_ALL = [("uT", c, tb) for c in range(4) for tb in range(4)]
    for c in range(4):
        wb = slab_load(w_in, OFF_ZS + c * 128)
        for tb in range(4):
            pb = nextbank()
            mm_block(pb, wb, lambda k: hT[:, k, HALO + tb * 512: HALO + (tb + 1) * 512], ht_keys(tb))
            S.op("act", lambda e: e.activation(out=y2[:, c, tb * 512:(tb + 1) * 512], in_=PS[pb][:, :], func=AF.Silu),
                 r=[("ps", pb)], w=[("y2", c, tb)])
    gi = 0
    GELU_SQ = float(math.sqrt(0.044715 * 1.5957691216))
    for c in range(4):
        for tb in range(4):
            g0 = gt[gi % 2]
            gi += 1
            yv = uT[:, c, tb * 512:(tb + 1) * 512]
            S.op("act", lambda e: e.activation(out=g0[:], in_=yv, func=AF.Square, scale=GELU_SQ),
                 r=[("uT", c, tb)], w=[("gt", id(g0))])
            S.op("dve", lambda e: e.scalar_tensor_tensor(out=g0[:], in0=g0[:], scalar=1.5957691216, in1=yv,
                                                         op0=ALU.add, op1=ALU.mult),
                 r=[("gt", id(g0)), ("uT", c, tb)], w=[("gt", id(g0))])
            S.op("act", lambda e: e.activation(out=g0[:], in_=g0[:], func=AF.Sigmoid),
                 r=[("gt", id(g0))], w=[("gt", id(g0))])
            S.op("dve", lambda e: e.tensor_tensor(out=yv, in0=g0[:], in1=yv, op=ALU.mult),
                 r=[("gt", id(g0)), ("uT", c, tb)], w=[("uT", c, tb)])

    for c in range(4):
        wb = slab_load(w_glu, c * 128, kch=4)
        for tb in range(4):
            pb = nextbank()
            mm_block(pb, wb, lambda k: uT[:, k, tb * 512:(tb + 1) * 512], [("uT", k, tb) for k in range(4)], kch=4)
            s0 = sg[(c * 4 + tb) % 4]
            S.op("act", lambda e: e.activation(out=s0[:], in_=PS[pb][:, :], func=AF.Sigmoid,
                                               bias=bgluT[:, c:c + 1]),
                 r=[("ps", pb), "bgluT"], w=[("sg", id(s0))])
            S.op("dve", lambda e: e.tensor_tensor(out=s0[:], in0=s0[:], in1=uT[:, c, tb * 512:(tb + 1) * 512], op=ALU.mult),
                 r=[("sg", id(s0)), ("uT", c, tb)], w=[("sg", id(s0))])
            S.op("dve", lambda e: e.tensor_tensor(out=y2[:, c, tb * 512:(tb + 1) * 512], in0=s0[:],
                                                  in1=y2[:, c, tb * 512:(tb + 1) * 512], op=ALU.mult),
                 r=[("sg", id(s0)), ("y2", c, tb)], w=[("y2", c, tb)])

    def wout_slab(jc):
        a = jc % 2
        S.dma("sp", wst[a][:, :, :], w_out[:, jc * 128:(jc + 1) * 128].rearrange("(k p) c -> p k c", p=128),
              w=[("wst", a)])
        S.op("pool", lambda e: e.tensor_copy(out=woutb[:, :, jc * 128:(jc + 1) * 128], in_=wst[a][:, :, :]),
             r=[("wst", a)], w=[("woutb", jc)] + UT_ALL)

    cwsb = xt[0]
    S.dma("pool", cwsb[0:31, :], conv_w, w=[("xt", 0)])
    pcw = PS[0][:, 0:256].rearrange("p (c k) -> p c k", c=8)
    for ch in range(8):
        S.op("pe", lambda e: e.matmul(pcw[:, ch, 0:31], lhsT=cwsb[0:31, ch * 128:(ch + 1) * 128],
                                      rhs=identf[0:31, 0:31], start=True, stop=True),
             r=[("xt", 0), "identf"], w=[("ps", 0)], signal=(ch == 7))
    S.op("dve", lambda e: e.tensor_copy(out=cwT[:, :, 0:31], in_=pcw[:, :, 0:31]), r=[("ps", 0)], w=["cwT"])

    def tokblocks():
        return [(0, HALO, [("hT", 0)])] + [(HALO + tb * 512, 512, ht_keys(tb)) for tb in range(4)]

    for i in range(8):
        wa = slab_load(w_in, OFF_CA + i * 128)
        wbb = slab_load(w_in, OFF_CB + i * 128)
        for bi, (c0, n, keys) in enumerate(tokblocks()):
            pa = nextbank()
            pbk = nextbank()
            mm_block(pa, wa, lambda k: hT[:, k, c0:c0 + n], keys, n=n)
            mm_block(pbk, wbb, lambda k: hT[:, k, c0:c0 + n], keys, n=n)
            s0 = sg[(i * 5 + bi) % 4]
            S.op("act", lambda e: e.activation(out=s0[:, :n], in_=PS[pbk][:, :n], func=AF.Sigmoid),
                 r=[("ps", pbk)], w=[("sg", id(s0))])
            S.op("dve", lambda e: e.tensor_tensor(out=cu[:, i, c0:c0 + n], in0=PS[pa][:, :n], in1=s0[:, :n],
                                                  op=ALU.mult),
                 r=[("ps", pa), ("sg", id(s0))], w=[("cu", i, bi)])

    SUMB, SQB = 6, 7
    for i in range(8):
        pass
    dg_built = {}

    dg_cnt = [0]

    def build_dg(i):
        nb_ = dg_cnt[0] % 2
        dg_cnt[0] += 1
        t = dg[nb_]
        S.op("dve", lambda e: e.tensor_tensor(out=t[:, :, :], in0=identf[:, :].unsqueeze(1).broadcast_to([128, 31, 128]),
                                              in1=cwT[:, i, 0:31].unsqueeze(2).broadcast_to([128, 31, 128]), op=ALU.mult),
             r=["identf", "cwT"], w=[("dg", nb_)])
        return t, nb_

    def conv_tb(tb):
        for i in range(8):
            t, nb_ = build_dg(i)
            if tb == 3:
                wout_slab(i)
            pb = nextbank(1, 6)
            base = HALO + tb * 512 - 30
            for k in range(31):
                S.op("pe", lambda e: e.matmul(PS[pb][:, :], lhsT=t[:, k, :], rhs=cu[:, i, base + k: base + k + 512],
                                              start=(k == 0), stop=(k == 30)),
                     r=[("dg", nb_), ("cu", i, tb), ("cu", i, 1 + tb)], w=[("ps", pb)], signal=(k == 30))
            S.op("act", lambda e: e.activation(out=cu[:, i, HALO + tb * 512: HALO + (tb + 1) * 512], in_=PS[pb][:, :],
                                               func=AF.Identity, bias=cbT[:, i:i + 1]),
                 r=[("ps", pb), "cbT"], w=[("cu", i, 1 + tb)])

    CU_KEYS = lambda i: [("cu", i, bi) for bi in range(5)]
    vi = 0

    def zc_slab(i):
        wb = slab_load(w_in, OFF_ZC + i * 128)
        for tb2 in range(4):
            pb = nextbank(1, 6)
            mm_block(pb, wb, lambda k: hT[:, k, HALO + tb2 * 512: HALO + (tb2 + 1) * 512], ht_keys(tb2))
            S.op("act", lambda e: e.activation(out=mrg[:, i, tb2 * 512:(tb2 + 1) * 512], in_=PS[pb][:, :], func=AF.Silu),
                 r=[("ps", pb)], w=[("mrg", i, tb2)])

    for tb in reversed(range(4)):
        conv_tb(tb)
        vs = lambda i: cu[:, i, HALO + tb * 512: HALO + (tb + 1) * 512]
        for i in range(8):
            q = sqb[vi % 2]
            vi += 1
            S.op("dve", lambda e: e.tensor_tensor(out=q[:], in0=vs(i), in1=vs(i), op=ALU.mult),
                 r=[("cu", i, 1 + tb)], w=[("sqb", id(q))])
            S.op("pe", lambda e: e.matmul(PS[SUMB][:, :], lhsT=onesb[:], rhs=vs(i), start=(i == 0), stop=(i == 7)),
                 r=["onesb", ("cu", i, 1 + tb)], w=[("ps", SUMB)], signal=False)
            S.op("pe", lambda e: e.matmul(PS[SQB][:, :], lhsT=onesb[:], rhs=q[:], start=(i == 0), stop=(i == 7)),
                 r=["onesb", ("sqb", id(q))], w=[("ps", SQB)], signal=True)
        zc_slab(2 * (3 - tb))
        zc_slab(2 * (3 - tb) + 1)
        S.op("act", lambda e: e.activation(out=st_mean[:], in_=PS[SUMB][:, :], func=AF.Copy, scale=1.0 / D),
             r=[("ps", SUMB)], w=[K_MEAN])
        S.op("dve", lambda e: e.tensor_tensor(out=st_tmp[:], in0=st_mean[:], in1=st_mean[:], op=ALU.mult),
             r=[K_MEAN], w=[K_TMP])
        S.op("dve", lambda e: e.scalar_tensor_tensor(out=st_tmp[:], in0=PS[SQB][:, :], scalar=1.0 / D, in1=st_tmp[:],
                                                     op0=ALU.mult, op1=ALU.subtract),
             r=[("ps", SQB), K_TMP], w=[K_TMP])
        S.op("act", lambda e: e.activation(out=st_rstd[:], in_=st_tmp[:], func=AF.Sqrt, bias=epsc[:, 1:2]),
             r=[K_TMP, "epsc"], w=[K_RSTD])
        S.op("dve", lambda e: e.reciprocal(out=st_rstd[:], in_=st_rstd[:]), r=[K_RSTD], w=[K_RSTD])
        for i in range(8):
            g0 = gt[i % 2]
            S.op("dve", lambda e: e.tensor_tensor(out=g0[:], in0=vs(i), in1=st_mean[:], op=ALU.subtract),
                 r=[("cu", i, 1 + tb), K_MEAN], w=[("gt", id(g0))])
            S.op("dve", lambda e: e.tensor_tensor(out=g0[:], in0=g0[:], in1=st_rstd[:], op=ALU.mult),
                 r=[("gt", id(g0)), K_RSTD], w=[("gt", id(g0))])
            S.op("act", lambda e: e.activation(out=vs(i), in_=g0[:], func=AF.Silu, scale=lngT[:, i:i + 1],
                                               bias=lnbT[:, i:i + 1]),
                 r=[("gt", id(g0)), "lngT", "lnbT"], w=[("cu", i, 1 + tb)])
    for tb in range(4):
        for i in range(8):
            hsl = slice(HALO + tb * 512, HALO + (tb + 1) * 512)
            S.op("dve", lambda e: e.tensor_tensor(out=cu[:, i, hsl], in0=cu[:, i, hsl], in1=mrg[:, i, tb * 512:(tb + 1) * 512],
                                                  op=ALU.mult),
                 r=[("cu", i, 1 + tb), ("mrg", i, tb)], w=[("cu", i, 1 + tb)])

    for j in range(8):
        wco_b = slab_load(w_co, j * 128)
        wso_b = slab_load(w_so, j * 128, kch=4)
        wgc_b = slab_load(w_in, OFF_GC + j * 128)
        for phase in range(2):
            if phase == 1:
                wgs_b = slab_load(w_in, OFF_GS + j * 128)
            for tb in range(4):
                tsl = slice(tb * 512, (tb + 1) * 512)
                hsl = slice(HALO + tb * 512, HALO + (tb + 1) * 512)
                if phase == 0:
                    pa = nextbank(1, 8)
                    pc = nextbank(1, 8)
                    mm_block(pa, wco_b, lambda k: cu[:, k, hsl], [("cu", k, 1 + tb) for k in range(8)])
                    mm_block(pc, wgc_b, lambda k: hT[:, k, hsl], ht_keys(tb))
                    s0 = sg[tb % 4]
                    S.op("act", lambda e: e.activation(out=s0[:], in_=PS[pc][:, :], func=AF.Sigmoid),
                         r=[("ps", pc)], w=[("sg", id(s0))])
                    S.op("dve", lambda e: e.tensor_tensor(out=mrg[:, j, tsl], in0=PS[pa][:, :], in1=s0[:], op=ALU.mult),
                         r=[("ps", pa), ("sg", id(s0))], w=[("mrg", j, tb)])
                else:
                    pb2 = nextbank(1, 8)
                    pd = nextbank(1, 8)
                    mm_block(pb2, wso_b, lambda k: y2[:, k, tsl], [("y2", k, tb) for k in range(4)], kch=4)
                    mm_block(pd, wgs_b, lambda k: hT[:, k, hsl], ht_keys(tb))
                    s0 = sg[tb % 4]
                    S.op("act", lambda e: e.activation(out=s0[:], in_=PS[pd][:, :], func=AF.Sigmoid),
                         r=[("ps", pd)], w=[("sg", id(s0))])
                    S.op("dve", lambda e: e.tensor_tensor(out=s0[:], in0=PS[pb2][:, :], in1=s0[:], op=ALU.mult),
                         r=[("ps", pb2), ("sg", id(s0))], w=[("sg", id(s0))])
                    S.op("dve", lambda e: e.tensor_tensor(out=mrg[:, j, tsl], in0=s0[:], in1=mrg[:, j, tsl], op=ALU.add),
                         r=[("sg", id(s0)), ("mrg", j, tb)], w=[("mrg", j, tb)])

    S.fence("hTdead", HT_KEYS)
    S.dma("pool", pgB[:], post_g.partition_broadcast(128), w=["pgB"])
    WOUT_KEYS = [("woutb", jc) for jc in range(8)]
    ssq2 = sb("ssq2", [128, 32], F32)
    rstd2 = sb("rstd2", [128, 16], F32)
    S.op("dve", lambda e: e.memset(ssq2[:], 0.0), w=["ssq2"])
    for tt in range(16):
        b = tt % 2
        tb = tt // 4
        S.dma("sp", xt[b][:, :], x[HALO + tt * 128: HALO + (tt + 1) * 128, :], w=[("xt", b)])
        pbs = [nextbank(1, 8), nextbank(1, 8)]
        for hf in range(2):
            for k in range(8):
                S.op("pe", lambda e: e.matmul(PS[pbs[hf]][:, :], lhsT=mrg[:, k, tt * 128:(tt + 1) * 128],
                                              rhs=woutb[:, k, hf * 512:(hf + 1) * 512], start=(k == 0), stop=(k == 7)),
                     r=[("mrg", k, tb)] + WOUT_KEYS[hf * 4:(hf + 1) * 4], w=[("ps", pbs[hf])], signal=(k == 7))
            S.op("act", lambda e: e.activation(out=junk[:, 0:512], in_=PS[pbs[hf]][:, :], func=AF.Square,
                                               accum_out=ssq2[:, 2 * tt + hf: 2 * tt + hf + 1]),
                 r=[("ps", pbs[hf]), "ssq2"], w=[K_JUNK, ("ssq2", tt, hf)])
        S.op("dve", lambda e: e.tensor_tensor(out=rstd2[:, tt:tt + 1], in0=ssq2[:, 2 * tt:2 * tt + 1],
                                              in1=ssq2[:, 2 * tt + 1:2 * tt + 2], op=ALU.add),
             r=[("ssq2", tt, 0), ("ssq2", tt, 1)], w=[("rstd2", tt)])
        S.op("act", lambda e: e.activation(out=rstd2[:, tt:tt + 1], in_=rstd2[:, tt:tt + 1], func=AF.Sqrt,
                                           scale=1.0 / D, bias=epsc[:, 0:1]),
             r=[("rstd2", tt), "epsc"], w=[("rstd2", tt)])
        S.op("dve", lambda e: e.reciprocal(out=rstd2[:, tt:tt + 1], in_=rstd2[:, tt:tt + 1]),
             r=[("rstd2", tt)], w=[("rstd2", tt)])
        for hf in range(2):
            hs = slice(hf * 512, (hf + 1) * 512)
            S.op("dve", lambda e: e.scalar_tensor_tensor(out=ot[b][:, hs], in0=PS[pbs[hf]][:, :],
                                                         scalar=rstd2[:, tt:tt + 1], in1=pgB[:, hs],
                                                         op0=ALU.mult, op1=ALU.mult),
                 r=[("ps", pbs[hf]), ("rstd2", tt), "pgB"], w=[("ot", b, hf)])
            S.op("pool", lambda e: e.tensor_tensor(out=ot[b][:, hs], in0=ot[b][:, hs], in1=xt[b][:, hs], op=ALU.add),
                 r=[("ot", b, hf), ("xt", b)], w=[("ot", b, hf)])
        S.dma("sp", out_d[tt * 128:(tt + 1) * 128, :], ot[b][:, :], r=[("ot", b, 0), ("ot", b, 1)], w=[("outd", tt)])
    dbg_aps = {"hT": hT, "uT": uT, "cu": cu, "mrg": mrg, "y2": y2}
    dbg_keys = {"hT": HT_KEYS, "uT": UT_ALL, "cu": [("cu", i, b) for i in range(8) for b in range(5)],
                "mrg": [("mrg", j, tb) for j in range(8) for tb in range(4)],
                "y2": [("y2", c, tb) for c in range(4) for tb in range(4)]}
    fin = [("outd", tt) for tt in range(16)]
    for name in debug:
        ap = dbg_aps[name]
        dd = nc.dram_tensor("dbg_" + name, list(ap.shape), ap.dtype, kind="ExternalOutput").ap()
        S.dma("sp", dd, ap, r=dbg_keys[name], w=[("dbgout", name)])
        fin.append(("dbgout", name))
    S.finish("sp", fin)
    return nc, S


_CONST = {}


def _consts():
    if not _CONST:
        _CONST["c_ident"] = np.eye(128, dtype=np.float32)
        kr = np.concatenate([np.arange(NEXP), [2048, 4096]]).astype(np.float32)
        _CONST["c_kramp"] = np.ascontiguousarray(np.broadcast_to(kr, (128, NEXP + 2)))
        br = (R0 * (np.arange(NB) + 1)).astype(np.float32)
        _CONST["c_bramp"] = np.ascontiguousarray(np.broadcast_to(br, (128, NB)))
        pm = np.zeros((128, 2), np.float32)
        par = (np.arange(128) // 16) % 2
        pm[par == 0, 0] = 1.0
        pm[par == 1, 1] = 1.0
        _CONST["c_pmask"] = pm
    return _CONST


def make_in_maps(inputs, fused=False):
    f = lambda a: np.ascontiguousarray(np.asarray(a, dtype=np.float32))
    x = f(inputs["x"])
    shared = {
        "w_in": f(inputs["w_in"][0]), "pre_g": f(inputs["pre_norm_gain"][0]), "conv_w": f(inputs["conv_w"][0]),
        "conv_b": f(inputs["conv_b"][0]), "ln_g": f(inputs["conv_ln_gain"][0]), "ln_b": f(inputs["conv_ln_bias"][0]),
        "w_co": f(inputs["w_conv_out"][0]), "lam_re": f(inputs["ssm_lambda_re"][0]),
        "lam_im": f(inputs["ssm_lambda_im"][0]), "log_dt": f(inputs["ssm_log_dt"][0]),
        "b_re": f(inputs["ssm_b_re"][0]), "b_im": f(inputs["ssm_b_im"][0]), "c_re": f(inputs["ssm_c_re"][0]),
        "c_im": f(inputs["ssm_c_im"][0]), "d_in": f(inputs["ssm_d"][0]), "w_glu": f(inputs["w_ssm_glu"][0]),
        "b_glu": f(inputs["b_ssm_glu"][0]), "w_so": f(inputs["w_ssm_out"][0]), "w_out": f(inputs["w_out"][0]),
        "post_g": f(inputs["post_norm_gain"][0]),
    }
    shared.update(_consts())
    maps = []
    for c in range(NCORES):
        bi, ci = c // 4, c % 4
        xc = np.zeros((NT + 3 * TOK if fused else NT, D), np.float32)
        t0 = ci * TOK
        xc[HALO:NT] = x[bi, t0:t0 + TOK]
        if fused:
            for m_ in range(3):
                cj = ci - 1 - m_
                if cj >= 0:
                    xc[NT + m_ * TOK: NT + (m_ + 1) * TOK] = x[bi, cj * TOK:(cj + 1) * TOK]
        if ci > 0:
            xc[:HALO] = x[bi, t0 - HALO:t0]
        sel = np.zeros((8, 3), np.float32)
        for j in range(4):
            if j < ci:
                sel[j, ci - 1 - j] = 1.0
        m = dict(shared)
        m["x"] = xc
        m["c_sel"] = np.ascontiguousarray(np.broadcast_to(sel.reshape(1, 24), (128, 24)))
        maps.append(m)
    return maps


_PROG = {}


def kernel(**inputs):
    if "ncF" not in _PROG:
        _PROG["ncF"] = build_program(mode='F')[0]
    maps = make_in_maps(inputs, fused=True)
    res = run_bass_kernel_spmd(_PROG["ncF"], maps, core_ids=list(range(NCORES)))
    out = np.empty((2, 8192, D), np.float32)
    for c in range(NCORES):
        out[c // 4, (c % 4) * TOK:(c % 4 + 1) * TOK] = res.results[c]["out"]
    return out
```

```python
import math
import numpy as np
import concourse.bass as bass
import concourse.mybir as mybir
from concourse.bass_utils import run_bass_kernel_spmd

F32 = mybir.dt.float32
BF16 = mybir.dt.bfloat16
AF = mybir.ActivationFunctionType
ALU = mybir.AluOpType

NCORES = 8
import os
NO_CC = bool(os.environ.get('NO_CC'))
D = 1024
TOK = 2048
HALO = 32
NT = TOK + HALO
INW = 6144
R0 = 16
NB = TOK // R0
NEXP = R0 + 1
TWO_PI = float(2.0 * np.pi)
MAGIC = 12582912.0
C1 = 6.28125
C2 = float(2.0 * np.pi - 6.28125)

OFF_CA, OFF_CB, OFF_ZC, OFF_U, OFF_ZS, OFF_GC, OFF_GS = 0, 1024, 2048, 3072, 3584, 4096, 5120


class Sched:
    def __init__(self, nc, n_dma_sems=24):
        self.nc = nc
        self.engs = {"pe": nc.tensor, "act": nc.scalar, "dve": nc.vector, "pool": nc.gpsimd, "sp": nc.sync}
        self.sem = {k: nc.alloc_semaphore("prog_" + k) for k in self.engs}
        self.cnt = {k: 0 for k in self.engs}
        self.waited = {k: {} for k in self.engs}
        self.bufs = {}
        self.semobj = {("E", k): self.sem[k] for k in self.engs}
        self.dpool = {}
        for q, n in (("sp", n_dma_sems), ("pool", 16)):
            sems = [nc.alloc_semaphore("dma_%s%d" % (q, i)) for i in range(n)]
            self.dpool[q] = {"sems": sems, "n": 0}
            for i, sm_ in enumerate(sems):
                self.semobj[("D", q, i)] = sm_
        self.dn = 0
        self.nwaits = 0
        self.fences = []
        self.fence_scratch = nc.alloc_sbuf_tensor("fence_scr", [128, 8], F32).ap()

    def _need(self, e, deps):
        best = {}
        for d in deps:
            if d is None:
                continue
            k, v = d
            if e == "pe" and k == ("E", "pe"):
                continue
            if v > best.get(k, 0):
                best[k] = v
        for k, v in best.items():
            if v > self.waited[e].get(k, 0):
                self.engs[e].wait_ge(self.semobj[k], v)
                self.waited[e][k] = v
                self.nwaits += 1

    def _deps(self, r, w):
        deps = []
        for k in r:
            b = self.bufs.get(k)
            if b is not None:
                deps.append(b["w"])
        for k in w:
            b = self.bufs.get(k)
            if b is not None:
                deps.append(b["w"])
                deps.extend(b["r"].items())
        return deps

    def _mark(self, tag, r, w):
        for k in r:
            b = self.bufs.setdefault(k, {"w": None, "r": {}})
            if tag[1] > b["r"].get(tag[0], 0):
                b["r"][tag[0]] = tag[1]
        for k in w:
            self.bufs[k] = {"w": tag, "r": {}}

    def fence(self, name, old_keys):
        self.op("dve", lambda e: e.memset(self.fence_scratch, 0.0), w=list(old_keys) + [("fence", name)])
        self.fences.append(("fence", name))

    def op(self, e, fn, r=(), w=(), signal=True):
        r = list(r) + self.fences
        self._need(e, self._deps(r, w))
        ins = fn(self.engs[e])
        if signal:
            self.cnt[e] += 1
            ins.then_inc(self.sem[e], 1)
            idx = self.cnt[e]
        else:
            assert e == "pe"
            idx = self.cnt[e] + 1
        self._mark((("E", e), idx), r, w)
        return ins

    def dma(self, e, out, in_, r=(), w=(), **kw):
        self.custom_dma(e, lambda g: g.dma_start(out=out, in_=in_, **kw), r=r, w=w)

    def custom_dma(self, e, fn, r=(), w=()):
        P = self.dpool[e]
        ns = len(P["sems"])
        s = P["n"] % ns
        val = 16 * (P["n"] // ns + 1)
        r = list(r) + self.fences
        deps = self._deps(r, w)
        if P["n"] >= ns:
            deps.append((("D", e, s), val - 16))
        self._need(e, deps)
        fn(self.engs[e]).then_inc(P["sems"][s], 16)
        P["n"] += 1
        self.dn += 1
        self._mark((("D", e, s), val), r, w)

    def finish(self, e, keys):
        self._need(e, self._deps(keys, ()))


def build_program(s5_on=True, debug=(), mode='B'):
    nc = bass.Bass("TRN2", target_bir_lowering=False)
    S = Sched(nc)

    def din(name, shape):
        return nc.dram_tensor(name, list(shape), F32, kind="ExternalInput").ap()

    x = din("x", [NT + 3 * TOK, D] if mode == "F" else [NT, D])
    w_in = din("w_in", [D, INW])
    pre_g = din("pre_g", [D])
    conv_w = din("conv_w", [31, D])
    conv_b = din("conv_b", [D])
    ln_g = din("ln_g", [D])
    ln_b = din("ln_b", [D])
    w_co = din("w_co", [D, D])
    lam_re = din("lam_re", [32, 64])
    lam_im = din("lam_im", [32, 64])
    log_dt = din("log_dt", [32])
    b_re = din("b_re", [32, 64, 16])
    b_im = din("b_im", [32, 64, 16])
    c_re = din("c_re", [32, 16, 64])
    c_im = din("c_im", [32, 16, 64])
    d_in = din("d_in", [32, 16])
    w_glu = din("w_glu", [512, 512])
    b_glu = din("b_glu", [512])
    w_so = din("w_so", [512, D])
    w_out = din("w_out", [D, D])
    post_g = din("post_g", [D])
    c_ident = din("c_ident", [128, 128])
    c_kramp = din("c_kramp", [128, NEXP + 2])
    c_bramp = din("c_bramp", [128, NB])
    c_pmask = din("c_pmask", [128, 2])
    c_sel = din("c_sel", [128, 24])
    out_d = nc.dram_tensor("out", [TOK, D], F32, kind="ExternalOutput").ap()
    ag_in = nc.dram_tensor("ag_in", [128, 32], F32).ap()
    ag_out = nc.dram_tensor("ag_out", [4 * 128, 32], F32).ap()

    dbg_out = {}

    def sb(name, shape, dt):
        return nc.alloc_sbuf_tensor(name, list(shape), dt).ap()

    def ps(name, shape, dt=F32):
        return nc.alloc_psum_tensor(name, list(shape), dt).ap()

    hTraw = sb("hT", [128, 8 * NT], BF16)
    hT = hTraw.rearrange("p (k t) -> p k t", k=8)
    uT = sb("uT", [128, 4, TOK], BF16)
    BIGB = 104 * 1024
    big = sb("big", [128, BIGB // 2], BF16)

    def carve(off, shape, dt, base=None):
        base = big if base is None else base
        n = int(np.prod(shape[1:]))
        esz = 4 if dt == F32 else 2
        assert off % 4 == 0
        v = base[:, off // 2: off // 2 + n * esz // 2]
        if dt == F32:
            v = v.bitcast(F32)
        if len(shape) == 3:
            v = v.rearrange("p (a b) -> p a b", a=shape[1])
        elif len(shape) == 4:
            v = v.rearrange("p (a b c) -> p a b c", a=shape[1], b=shape[2])
        elif len(shape) == 5:
            v = v.rearrange("p (a b c d) -> p a b c d", a=shape[1], b=shape[2], c=shape[3])
        return v

    y2 = carve(0, [128, 4, TOK], BF16)
    cu = carve(16384, [128, 8, NT], BF16)
    dg = [carve(16384 + 33280 + i * 7936, [128, 31, 128], BF16) for i in range(2)]
    mrg = carve(16384 + 33280 + 15872, [128, 8, TOK], BF16)
    assert 16384 + 33280 + 15872 + 32768 <= BIGB
    woutb = uT.rearrange("p a b -> p (a b)").rearrange("p (k c) -> p k c", k=8)
    pgB = carve(16384, [128, D], F32, base=hTraw)
    ot = [carve(20480 + i * 4096, [128, D], F32, base=hTraw) for i in range(2)]
    wst = [sb("wst%d" % i, [128, 8, 128], F32) for i in range(2)]
    wbf = [sb("wbf%d" % i, [128, 8, 128], BF16) for i in range(4)]
    xt = [sb("xt%d" % i, [128, D], F32) for i in range(3)]
    xs = [sb("xs%d" % i, [128, D], BF16) for i in range(2)]
    gt = [sb("gt%d" % i, [128, 512], F32) for i in range(2)]
    junk = gt[1].bitcast(BF16)
    K_JUNK = ("gt", id(gt[1]))
    sg = [sb("sg%d" % i, [128, 512], F32) for i in range(4)]
    identf = sb("identf", [128, 128], F32)
    identb = sb("identb", [128, 128], BF16)
    onesb = sb("onesb", [128, 128], BF16)
    gT = sb("gT", [128, 8], F32)
    cbT = sb("cbT", [128, 8], F32)
    lngT = sb("lngT", [128, 8], F32)
    lnbT = sb("lnbT", [128, 8], F32)
    bgluT = sb("bgluT", [128, 4], F32)
    dcol = sb("dcol", [128, 4], F32)
    ssq = sb("ssq", [128, 80], F32)
    rstd = sb("rstd", [128, 80], F32)
    cwT = sb("cwT", [128, 8, 32], F32)
    st_mean, st_rstd, st_tmp = sg[0], sg[1], sg[2]
    K_MEAN, K_RSTD, K_TMP = ("sg", id(sg[0])), ("sg", id(sg[1])), ("sg", id(sg[2]))
    sqb = [sb("sqb%d" % i, [128, 512], BF16) for i in range(2)]

    PS = [ps("ps%d" % i, [128, 512]) for i in range(8)]

    E = S.engs
    for i in range(8):
        pass

    S.dma("pool", identf[:], c_ident, w=["identf"])
    S.op("dve", lambda e: e.tensor_copy(out=identb[:], in_=identf[:]), r=["identf"], w=["identb"])
    S.op("dve", lambda e: e.memset(onesb[:], 1.0), w=["onesb"])
    S.op("dve", lambda e: e.memset(ssq[:], 0.0), w=["ssq"])

    def load_cols(dst, src, n, key):
        S.dma("pool", dst[:, 0:n], src.rearrange("(c p) -> p c", p=128), w=[key], allow_slow_non_contiguous=True)

    load_cols(gT, pre_g, 8, "gT")
    load_cols(cbT, conv_b, 8, "cbT")
    load_cols(lngT, ln_g, 8, "lngT")
    load_cols(lnbT, ln_b, 8, "lnbT")
    load_cols(bgluT, b_glu, 4, "bgluT")

    pst = PS[0].bitcast(BF16)

    p1_cnt = [0]
    p1_pend = []
    XB = xt + [w_.rearrange("p a b -> p (a b)") for w_ in wst]
    KX = [("xt", 0), ("xt", 1), ("xt", 2), ("wst", 0), ("wst", 1)]

    def p1_front(tt_, passno):
        rows = HALO if tt_ == 0 else 128
        r0 = 0 if tt_ == 0 else HALO + (tt_ - 1) * 128
        xr = r0 if passno == 3 else NT + passno * TOK + (tt_ - 1) * 128
        g = p1_cnt[0]
        p1_cnt[0] += 1
        b = g % 5
        tt = passno * 17 + tt_
        S.dma("sp", XB[b][:rows, :], x[xr:xr + rows, :], w=[KX[b]])
        S.op("act", lambda e: e.activation(out=junk[:rows, :], in_=XB[b][:rows, :], func=AF.Square,
                                           accum_out=ssq[:rows, tt:tt + 1]),
             r=[KX[b], "ssq"], w=[K_JUNK, ("ssq", tt)])
        S.op("act", lambda e: e.activation(out=rstd[:rows, tt:tt + 1], in_=ssq[:rows, tt:tt + 1], func=AF.Sqrt,
                                           scale=1.0 / D, bias=epsc[:rows, 0:1]),
             r=[("ssq", tt), "epsc"], w=[("rstd", tt)])
        S.op("dve", lambda e: e.reciprocal(out=rstd[:rows, tt:tt + 1], in_=rstd[:rows, tt:tt + 1]),
             r=[("rstd", tt)], w=[("rstd", tt)])
        return (tt_, passno, g)

    def p1_back(tt_, passno, g):
        rows = HALO if tt_ == 0 else 128
        r0 = 0 if tt_ == 0 else HALO + (tt_ - 1) * 128
        b = g % 5
        pbk = g % 2
        pst = PS[pbk].bitcast(BF16)
        tt = passno * 17 + tt_
        if g % 2 == 1:
            S.op("dve", lambda e: e.tensor_scalar(out=xs[pbk][:rows, :], in0=XB[b][:rows, :],
                                                  scalar1=rstd[:rows, tt:tt + 1], scalar2=None, op0=ALU.mult),
                 r=[KX[b], ("rstd", tt)], w=[("xs", pbk)])
        else:
            S.op("act", lambda e: e.activation(out=xs[pbk][:rows, :], in_=XB[b][:rows, :], func=AF.Copy,
                                               scale=rstd[:rows, tt:tt + 1]),
                 r=[KX[b], ("rstd", tt)], w=[("xs", pbk)])
        pv = pst.rearrange("p (k t) -> p k t", k=8)
        for kc in range(8):
            S.op("pe", lambda e: e.transpose(out=pv[:, kc, :rows], in_=xs[pbk][:rows, kc * 128:(kc + 1) * 128],
                                             identity=identb[:rows, :rows]),
                 r=[("xs", pbk), "identb"], w=[("ps", pbk)], signal=(kc == 7))
        S.op("dve", lambda e: e.tensor_tensor(out=hT[:, :, r0:r0 + rows], in0=pv[:, :, :rows],
                                              in1=gT[:, :].unsqueeze(2).broadcast_to([128, 8, rows]), op=ALU.mult),
             r=[("ps", pbk), "gT"], w=[("hT", tt_)])

    def p1_push(tt_, passno):
        st = p1_front(tt_, passno)
        if os.environ.get("NO_STAG"):
            p1_back(*st)
            return
        if p1_pend:
            p1_back(*p1_pend.pop())
        p1_pend.append(st)

    def p1_flush():
        if p1_pend:
            p1_back(*p1_pend.pop())

    epsc = sb("epsc", [128, 2], F32)
    S.op("dve", lambda e: e.memset(epsc[:, 0:1], 1e-6), w=["epsc"])
    S.op("dve", lambda e: e.memset(epsc[:, 1:2], 1e-5), r=["epsc"], w=["epsc"])

    def run_p1(passno):
        for tt in range(0 if passno == 3 else 1, 17):
            p1_push(tt, passno)
        p1_flush()

    def p1_tb_tiles(passno, tb):
        return ([0] if (passno == 3 and tb == 0) else []) + list(range(1 + 4 * tb, 5 + 4 * tb))

    FUSED = (mode == "F" and s5_on)
    if not FUSED:
        run_p1(3)

    HT_KEYS = [("hT", tt) for tt in range(17)]

    def ht_keys(tb):
        return [("hT", 1 + tb * 4 + i) for i in range(4)]

    slab_n = [0]

    plan = []
    plan += [(w_in, OFF_ZS + c * 128, 8) for c in range(4)]
    plan += [(w_glu, c * 128, 4) for c in range(4)]
    for i in range(8):
        plan += [(w_in, OFF_CA + i * 128, 8), (w_in, OFF_CB + i * 128, 8)]
    plan += [(w_in, OFF_ZC + i * 128, 8) for i in range(8)]
    for j in range(8):
        plan += [(w_co, j * 128, 8), (w_so, j * 128, 4), (w_in, OFF_GC + j * 128, 8), (w_in, OFF_GS + j * 128, 8)]
    issued = [0]

    def _issue(n):
        src, col0, kch = plan[n]
        a, b = n % 2, n % 4
        S.dma("sp", wst[a][:, :kch, :], src[:, col0:col0 + 128].rearrange("(k p) c -> p k c", p=128),
              w=[("wst", a)])
        S.op("pool", lambda e: e.tensor_copy(out=wbf[b][:, :kch, :], in_=wst[a][:, :kch, :]),
             r=[("wst", a)], w=[("wbf", b)])

    def slab_load(src, col0, kch=8):
        n = slab_n[0]
        slab_n[0] += 1
        assert plan[n][1] == col0 and plan[n][2] == kch and plan[n][0] is src, (n, col0, kch)
        while issued[0] <= min(n + 1, len(plan) - 1):
            _issue(issued[0])
            issued[0] += 1
        return n % 4

    def mm_block(pbank, b, rhs_fn, rkeys, kch=8, n=512):
        for k in range(kch):
            S.op("pe", lambda e: e.matmul(PS[pbank][:, :n], lhsT=wbf[b][:, k, :], rhs=rhs_fn(k),
                                          start=(k == 0), stop=(k == kch - 1)),
                 r=[("wbf", b)] + rkeys, w=[("ps", pbank)], signal=(k == kch - 1))

    bank = [1]

    def nextbank(lo=1, hi=8):
        b = bank[0]
        if not (lo <= b < hi):
            b = lo
        bank[0] = lo + (b + 1 - lo) % (hi - lo)
        return b

    for c in range(4):
        a = c % 2
        S.dma("sp", wst[a][:, :, :], w_in[:, OFF_U + c * 128: OFF_U + (c + 1) * 128].rearrange("(k p) c -> p k c", p=128),
              w=[("wst", a)])
        S.op("pool", lambda e: e.tensor_copy(out=wbf[c][:, :, :], in_=wst[a][:, :, :]), r=[("wst", a)], w=[("wbf", c)])

    def run_p2_tb(tb):
        for c in range(4):
            if True:
                pb = nextbank(2, 8)
                for k in range(8):
                    S.op("pe", lambda e: e.matmul(PS[pb][:, :], lhsT=wbf[c][:, k, :],
                                                  rhs=hT[:, k, HALO + tb * 512: HALO + (tb + 1) * 512],
                                                  start=(k == 0), stop=(k == 7)),
                         r=[("wbf", c)] + ht_keys(tb), w=[("ps", pb)], signal=(k == 7))
                if s5_on and c % 2 == 1:
                    S.op("dve", lambda e: e.tensor_copy(
                        out=uT.rearrange("p c (j b) -> p c j b", j=16)[:, c, :, tb * 32:(tb + 1) * 32],
                        in_=PS[pb][:, :].rearrange("p (b j) -> p j b", j=16)),
                         r=[("ps", pb)], w=[("uT", c, tb)])
                elif s5_on:
                    S.op("act", lambda e: e.activation(
                        out=uT.rearrange("p c (j b) -> p c j b", j=16)[:, c, :, tb * 32:(tb + 1) * 32],
                        in_=PS[pb][:, :].rearrange("p (b j) -> p j b", j=16), func=AF.Copy),
                         r=[("ps", pb)], w=[("uT", c, tb)])
                else:
                    S.op("act", lambda e: e.activation(out=uT[:, c, tb * 512:(tb + 1) * 512], in_=PS[pb][:, :],
                                                       func=AF.Copy),
                         r=[("ps", pb)], w=[("uT", c, tb)])

    def run_p2():
        for tb in range(4):
            run_p2_tb(tb)

    if mode != "F":
        run_p2()


    if s5_on:
        PI2 = float(np.pi / 2)
        A_ = lambda off, shape, dt: carve(off, shape, dt)
        ar = A_(0, [128, 19, 16], F32)
        ai = A_(1216, [128, 19, 16], F32)
        mag = A_(2432, [128, 19, 16], F32)
        ang = A_(3648, [128, 19, 16], F32)
        nn = A_(4864, [128, 19, 16], F32)
        sm = A_(6080, [128, 16, 16], F32)
        LR, LI, DT, LDR, LDI, DEN, ZR, ZI, AM1, T0, T1, T2 = [sm[:, i, :] for i in range(12)]
        Braw = [A_(7168 + i * 1024, [128, 16, 16], F32) for i in range(2)]
        bb = [A_(9216 + i * 1024, [128, 16, 16], F32) for i in range(2)]
        bbBD = [A_(11264 + i * 2048, [128, 16, 32], F32) for i in range(2)]
        pad = [A_(15360 + i * 4096, [128, 16, 128], BF16) for i in range(2)]
        Craw = [A_(23552 + i * 1024, [128, 4, 64], F32) for i in range(2)]
        Cexp = [A_(25600 + i * 512, [128, 128], F32) for i in range(2)]
        CBD = [A_(26624 + i * 2048, [128, 16, 32], F32) for i in range(2)]
        cosT = A_(30720, [128, 16, 128], F32)
        sinT = A_(38912, [128, 16, 128], F32)
        zz = [A_(47104 + i * 8192, [128, 16, 128], F32) for i in range(2)]
        ww = zz
        Eall = A_(63488, [128, 4, 16, 2, 128], BF16)
        tmp = A_(96256, [128, 4, 128], F32)
        tmp2 = A_(98304, [128, 4, 128], F32)
        Sst = [A_(63488 + i * 4224, [128, 16, 132], BF16) for i in range(2)]
        Xt = A_(96256, [128, 16, 2, 128], BF16)
        CAt = A_(47104, [128, 17, 4, 2, 32], BF16)
        lagT = A_(55296 + 1024, [128, 16, 128], BF16)
        CAtmp = [A_(72192 + i * 8704, [128, 17, 4, 32], F32) for i in range(2)]
        CAt_b = [CAt, A_(89600, [128, 17, 4, 2, 32], BF16)]
        lagT_b = [lagT, A_(98304, [128, 16, 128], BF16)]
        ZK0 = [("zz0", c) for c in range(4)]
        ZK1 = [("zz1", c) for c in range(4)]
        uPM = uT.rearrange("p c (j b) -> p c j b", j=16)
        kramp = A_(104448, [128, 19], F32)
        bramp = A_(104448 + 128, [128, NB], F32)
        pmask = sb("pmask", [128, 2], F32)
        selt = sb("selt", [128, 24], F32)
        agbuf = sb("agbuf", [128, 32], F32)
        Gt = sb("Gt", [128, 4, 32], F32)
        Tm = sb("Tm", [128, 3, 32], F32)
        Sin = sb("Sin", [128, 32], F32)

        from collections import deque
        from functools import partial
        bgq = deque()
        tick_n = [0]

        def tick():
            tick_n[0] += 1
            if bgq and tick_n[0] % 4 == 0:
                bgq.popleft()()

        def TT(out, a, b, op, r, w, e="dve"):
            tick()
            S.op(e, lambda g: g.tensor_tensor(out=out, in0=a, in1=b, op=op), r=r, w=w)

        def TS(out, a, s1, s2, op0, op1, r, w, e="dve"):
            if s2 is None:
                S.op(e, lambda g: g.tensor_scalar(out=out, in0=a, scalar1=s1, scalar2=None, op0=op0), r=r, w=w)
            else:
                S.op(e, lambda g: g.tensor_scalar(out=out, in0=a, scalar1=s1, scalar2=s2, op0=op0, op1=op1), r=r, w=w)

        def STT(out, a, sc, b, op0, op1, r, w, e="dve"):
            S.op(e, lambda g: g.scalar_tensor_tensor(out=out, in0=a, scalar=sc, in1=b, op0=op0, op1=op1), r=r, w=w)

        def ACTF(out, a, func, r, w, **kw):
            S.op("act", lambda g: g.activation(out=out, in_=a, func=func, **kw), r=r, w=w)

        def bc(ap, axis, shape):
            return ap.unsqueeze(axis).broadcast_to(shape)

        halfpi = sb("halfpi", [128, 1], F32)
        S.op("dve", lambda g: g.memset(halfpi[:, :], PI2), w=["halfpi"])

        def sincos(a, n, A, ka, kn, kA, bshape=None):
            KN = kn if isinstance(kn, list) else [kn]
            TS(n, a, 1.0 / TWO_PI, MAGIC, ALU.mult, ALU.add, [ka], KN)
            TS(n, n, MAGIC, None, ALU.subtract, None, KN, KN)
            STT(a, n, -C1, a, ALU.mult, ALU.add, KN + [ka], [ka])
            STT(a, n, -C2, a, ALU.mult, ALU.add, KN + [ka], [ka])
            STT(n, a, -1.0, a, ALU.mult, ALU.max, [ka], KN)
            ACTF(A, a, AF.Sin, [ka], [kA], scale=0.5)
            ACTF(a, n, AF.Sin, KN + [ka, "halfpi"], [ka], scale=-0.5, bias=halfpi[:, 0:1])
            STT(a, A, 2.0, a, ALU.mult, ALU.mult, [kA, ka], [ka])
            TT(A, A, A, ALU.mult, [kA], [kA])
            TS(A, A, -2.0, 1.0, ALU.mult, ALU.add, [kA], [kA])

        def range_reduce(a, n, ka, kn):
            TS(n, a, 1.0 / TWO_PI, MAGIC, ALU.mult, ALU.add, [ka], [kn])
            TS(n, n, MAGIC, None, ALU.subtract, None, [kn], [kn])
            STT(a, n, -C1, a, ALU.mult, ALU.add, [kn, ka], [ka])
            STT(a, n, -C2, a, ALU.mult, ALU.add, [kn, ka], [ka])
            TS(n, a, float(np.pi), -TWO_PI, ALU.is_gt, ALU.mult, [ka], [kn])
            TT(a, a, n, ALU.add, [ka, kn], [ka])
            TS(n, a, -float(np.pi), TWO_PI, ALU.is_lt, ALU.mult, [ka], [kn])
            TT(a, a, n, ALU.add, [ka, kn], [ka])
            TS(a, a, 3.1415925, -3.1415925, ALU.min, ALU.max, [ka], [ka])

        for gl in range(2):
            rs = slice(gl * 64, (gl + 1) * 64)
            S.dma("pool", LR[rs, :], lam_re.rearrange("(q gl) p -> gl p q", gl=2)[gl], w=["sm"],
                  allow_slow_non_contiguous=True)
            S.dma("pool", LI[rs, :], lam_im.rearrange("(q gl) p -> gl p q", gl=2)[gl], w=["sm"],
                  allow_slow_non_contiguous=True)
            S.dma("pool", DT[rs, :], bass.AP(log_dt.tensor, gl, [[0, 64], [2, 16]]), w=["sm"],
                  allow_slow_non_contiguous=True)
            S.dma("pool", Braw[0][rs, :, :], b_re.rearrange("(q gl) p h -> gl p q h", gl=2)[gl], w=["Braw"])
            S.dma("pool", Braw[1][rs, :, :], b_im.rearrange("(q gl) p h -> gl p q h", gl=2)[gl], w=["Braw"])
        S.dma("pool", Craw[0][:, :, :], c_re.rearrange("(c g) h p -> (g h) c p", c=4), w=["Craw"])
        S.dma("pool", Craw[1][:, :, :], c_im.rearrange("(c g) h p -> (g h) c p", c=4), w=["Craw"])
        S.dma("pool", dcol[:, :], d_in.rearrange("(c g) h -> (g h) c", c=4), w=["dcol"], allow_slow_non_contiguous=True)
        S.dma("pool", kramp[:], c_kramp, w=["kramp"])
        S.dma("pool", bramp[:], c_bramp, w=["bramp"])
        S.dma("pool", pmask[:], c_pmask, w=["pmask"])
        S.dma("pool", selt[:], c_sel, w=["selt"])

        if FUSED:
            for tt in range(1, 17):
                bgq.append(partial(p1_push, tt, 0))
            bgq.append(p1_flush)
            for tb in range(4):
                bgq.append(partial(run_p2_tb, tb))
                for tt in p1_tb_tiles(1, tb):
                    bgq.append(partial(p1_push, tt, 1))
            bgq.append(p1_flush)
            if True:
                while bgq:
                    bgq.popleft()()

        ACTF(DT, DT, AF.Exp, ["sm"], ["sm"])
        TT(LDR, LR, DT, ALU.mult, ["sm"], ["sm"])
        TT(LDI, LI, DT, ALU.mult, ["sm"], ["sm"])
        sh3 = [128, 19, 16]
        TT(mag, bc(kramp[:, :], 2, sh3), bc(LDR, 1, sh3), ALU.mult, ["kramp", "sm"], ["mag"])
        ACTF(mag, mag, AF.Exp, ["mag"], ["mag"])
        TT(ang, bc(kramp[:, :], 2, sh3), bc(LDI, 1, sh3), ALU.mult, ["kramp", "sm"], ["ang"])
        sincos(ang, nn, ar, "ang", "nn", "ar")
        TT(ar, ar, mag, ALU.mult, ["ar", "mag"], ["ar"])
        TT(ai, ang, mag, ALU.mult, ["ang", "mag"], ["ai"])
        AK = ["ar", "ai"]

        sh4 = [128, 16, NB]
        TT(sinT, bc(LDI, 2, sh4), bc(bramp[:, :], 1, sh4), ALU.mult, ["sm", "bramp"], ["sinT"])
        sincos(sinT, zz[0], cosT, "sinT", ZK0, "cosT")

        TT(DEN, LR, LR, ALU.mult, ["sm"], ["sm"])
        TT(T0, LI, LI, ALU.mult, ["sm"], ["sm"])
        TT(DEN, DEN, T0, ALU.add, ["sm"], ["sm"])
        S.op("dve", lambda g: g.reciprocal(out=DEN, in_=DEN), r=["sm"], w=["sm"])
        TS(AM1, ar[:, 1, :], -1.0, None, ALU.add, None, ["ar", "sm"], ["sm"])
        TT(T0, AM1, LR, ALU.mult, ["sm"], ["sm"])
        TT(T1, ai[:, 1, :], LI, ALU.mult, ["ai", "sm"], ["sm"])
        TT(T0, T0, T1, ALU.add, ["sm"], ["sm"])
        TT(ZR, T0, DEN, ALU.mult, ["sm"], ["sm"])
        TT(T0, ai[:, 1, :], LR, ALU.mult, ["ai", "sm"], ["sm"])
        TT(T1, AM1, LI, ALU.mult, ["sm"], ["sm"])
        TT(T0, T0, T1, ALU.subtract, ["sm"], ["sm"])
        TT(ZI, T0, DEN, ALU.mult, ["sm"], ["sm"])
        shb = [128, 16, 16]
        t_a, t_b = CBD[0][:, :, 0:16], CBD[0][:, :, 16:32]
        TT(t_a, bc(ZR, 2, shb), Braw[0][:, :, :], ALU.mult, ["sm", "Braw"], ["CBD"])
        TT(t_b, bc(ZI, 2, shb), Braw[1][:, :, :], ALU.mult, ["sm", "Braw"], ["CBD"])
        TT(bb[0][:, :, :], t_a, t_b, ALU.subtract, ["CBD"], ["bb"])
        TT(t_a, bc(ZR, 2, shb), Braw[1][:, :, :], ALU.mult, ["sm", "Braw", "bb"], ["CBD"])
        TT(t_b, bc(ZI, 2, shb), Braw[0][:, :, :], ALU.mult, ["sm", "Braw"], ["CBD"])
        TT(bb[1][:, :, :], t_a, t_b, ALU.add, ["CBD"], ["bb"])
        for i in range(2):
            S.op("dve", lambda g: g.memset(bbBD[i][:, :, :], 0.0), w=["bbBD"])
            S.op("pool", lambda g: g.memset(pad[i][:, :, :], 0.0), w=["pad"])
            for gl in range(2):
                rs = slice(gl * 64, (gl + 1) * 64)
                S.op("dve", lambda g: g.tensor_copy(out=bbBD[i][rs, :, gl * 16:(gl + 1) * 16], in_=bb[i][rs, :, :]),
                     r=["bb", "bbBD"], w=["bbBD"])
            for qq in range(4):
                S.op("dve", lambda g: g.tensor_copy(out=pad[i][:, qq::4, 32 * qq:32 * qq + 32], in_=bbBD[i][:, qq::4, :]),
                     r=["bbBD", "pad"], w=["pad"])

        for i in range(2):
            pc = PS[i][:, :].rearrange("p (q c) -> p q c", q=16)
            for c in range(4):
                for gl in range(2):
                    TS(Cexp[c % 2][:, gl * 64:(gl + 1) * 64], Craw[i][:, c, :], pmask[:, gl:gl + 1], None, ALU.mult, None,
                       ["Craw", "pmask", ("Cexp", c % 2)], [("Cexp", c % 2)])
                for qq in range(4):
                    S.op("pe", lambda g: g.matmul(pc[:, 4 * c + qq, :], lhsT=Cexp[c % 2][:, :],
                                                  rhs=identf[:, 32 * qq:32 * qq + 32], start=True, stop=True),
                         r=[("Cexp", c % 2), "identf"], w=[("ps", i)], signal=True)
            S.op("dve", lambda g: g.tensor_copy(out=CBD[i][:, :, :], in_=pc), r=[("ps", i), "CBD"], w=["CBD"])

        def build_E():
            shx = [128, 16, 4, 32]
            Xv = lambda ri: Xt[:, :, ri, :].rearrange("p k (q c) -> p k q c", q=4)
            W0 = zz[0].rearrange("p k (q c) -> p k q c", q=4)
            W1 = zz[1].rearrange("p k (q c) -> p k q c", q=4)
            for c in (0, 1, 2, 3):
                qs = slice(4 * c, 4 * c + 4)
                a_r = bc(ar[:, 0:16, qs], 3, shx)
                a_i = bc(ai[:, 0:16, qs], 3, shx)
                b_r = bc(bbBD[0][:, qs, :], 1, shx)
                b_i = bc(bbBD[1][:, qs, :], 1, shx)
                TT(W0, a_r, b_r, ALU.mult, AK + ["bbBD"], ZK0)
                TT(W1, a_i, b_i, ALU.mult, AK + ["bbBD"], ZK1)
                TT(Xv(0), W0, W1, ALU.subtract, ZK0 + ZK1, ["Xt", "tmp", "tmp2"])
                TT(W0, a_r, b_i, ALU.mult, AK + ["bbBD", "Xt"], ZK0)
                TT(W1, a_i, b_r, ALU.mult, AK + ["bbBD", "Xt"], ZK1)
                TT(Xv(1), W0, W1, ALU.add, ZK0 + ZK1, ["Xt", "tmp", "tmp2"])
                for k4 in range(4):
                    pe_ = PS[k4].bitcast(BF16).rearrange("p (s c) -> p s c", s=8)
                    for kk in range(4):
                        for ri in range(2):
                            k = 4 * k4 + kk
                            S.op("pe", lambda g: g.transpose(out=pe_[:, 2 * kk + ri, :], in_=Xt[:, k, ri, :], identity=identb[:, :]),
                                 r=["Xt", "tmp", "tmp2", "identb"], w=[("ps", k4)], signal=(kk == 3 and ri == 1))
                    ACTF(Eall[:, c, 4 * k4:4 * k4 + 4, :, :].rearrange("p a b c -> p (a b) c"), pe_, AF.Copy,
                         [("ps", k4)], ["Eall"])

        def p3a():
            for c in range(4):
                qs = slice(4 * c, 4 * c + 4)
                pl = [PS[4 + 2 * (c % 2) + ri][:, :].rearrange("p (q b) -> p q b", q=4) for ri in range(2)]
                for qq in range(4):
                    rs = slice(32 * qq, 32 * qq + 32)
                    for ri in range(2):
                        for j in range(16):
                            S.op("pe", lambda g: g.matmul(pl[ri][:, qq, :], lhsT=Eall[rs, c, 15 - j, ri, :], rhs=uPM[rs, c, j, :],
                                                          start=(j == 0), stop=(j == 15), tile_position=(32 * qq, 0)),
                                 r=["Eall", "EallT"] + [("uT", c, tb) for tb in range(4)], w=[("ps", 4 + 2 * (c % 2) + ri)],
                                 signal=(j == 15 and qq == 3))
                kl = [("ps", 4 + 2 * (c % 2)), ("ps", 5 + 2 * (c % 2))]
                cs, sn = cosT[:, qs, :], sinT[:, qs, :]
                TT(tmp, pl[1], sn, ALU.mult, [kl[1], "sinT"], ["tmp"])
                TT(zz[0][:, qs, :], pl[0], cs, ALU.mult, [kl[0], "cosT"], [("zz0", c)])
                TT(zz[0][:, qs, :], zz[0][:, qs, :], tmp, ALU.add, [("zz0", c), "tmp"], [("zz0", c)])
                TT(tmp2, pl[0], sn, ALU.mult, [kl[0], "sinT"], ["tmp2"])
                TT(zz[1][:, qs, :], pl[1], cs, ALU.mult, [kl[1], "cosT"], [("zz1", c)])
                TT(zz[1][:, qs, :], zz[1][:, qs, :], tmp2, ALU.subtract, [("zz1", c), "tmp2"], [("zz1", c)])

        ZK = [ZK0, ZK1]

        def scan_pass(init_fn, kin):
            for q in range(16):
                for ri in range(2):
                    S.op("dve", lambda g: g.tensor_tensor_scan(out=ww[ri][:, q, :],
                                                               data0=mag[:, 16, q:q + 1].broadcast_to([128, NB]),
                                                               data1=zz[ri][:, q, :], initial=init_fn(ri, q),
                                                               op0=ALU.mult, op1=ALU.add),
                         r=["mag", ("zz%d" % ri, q // 4)] + kin, w=[("zz%d" % ri, q // 4)])

        def local_final():
            c127, s127 = cosT[:, :, NB - 1], sinT[:, :, NB - 1]
            wr127, wi127 = ww[0][:, :, NB - 1], ww[1][:, :, NB - 1]
            TT(T0, c127, wr127, ALU.mult, ["cosT", *ZK0, "sm"], ["sm"])
            TT(T1, s127, wi127, ALU.mult, ["sinT", *ZK1, "sm"], ["sm"])
            TT(agbuf[:, 0:16], T0, T1, ALU.subtract, ["sm"], ["agbuf"])
            TT(T0, s127, wr127, ALU.mult, ["sinT", *ZK0, "sm"], ["sm"])
            TT(T1, c127, wi127, ALU.mult, ["cosT", *ZK1, "sm"], ["sm"])
            TT(agbuf[:, 16:32], T0, T1, ALU.add, ["sm", "agbuf"], ["agbuf"])

        def accumulate(idx):
            pr, pi_ = ar[:, idx, :], ai[:, idx, :]
            tr, ti = agbuf[:, 0:16], agbuf[:, 16:32]
            TT(T0, pr, tr, ALU.mult, AK + ["agbuf", "sm"], ["sm"])
            TT(T1, pi_, ti, ALU.mult, AK + ["agbuf", "sm"], ["sm"])
            TT(T0, T0, T1, ALU.subtract, ["sm"], ["sm"])
            TT(Sin[:, 0:16], Sin[:, 0:16], T0, ALU.add, ["sm", "Sin"], ["Sin"])
            TT(T0, pr, ti, ALU.mult, AK + ["agbuf", "sm"], ["sm"])
            TT(T1, pi_, tr, ALU.mult, AK + ["agbuf", "sm"], ["sm"])
            TT(T0, T0, T1, ALU.add, ["sm"], ["sm"])
            TT(Sin[:, 16:32], Sin[:, 16:32], T0, ALU.add, ["sm", "Sin"], ["Sin"])

        build_E()
        if mode == "F":
            S.op("dve", lambda g: g.memset(Sin[:, :], 0.0), w=["Sin"])
            while bgq:
                bgq.popleft()()
            for m_ in range(4):
                if m_ > 0:
                    for tb in range(4):
                        run_p2_tb(tb)
                        if m_ < 3:
                            for tt in p1_tb_tiles(m_ + 1, tb):
                                p1_push(tt, m_ + 1)
                    p1_flush()
                p3a()
                if m_ < 3:
                    scan_pass(lambda ri, q: 0.0, [])
                    local_final()
                    accumulate([0, 17, 18][m_])
        else:
            p3a()
            scan_pass(lambda ri, q: 0.0, [])
            local_final()
            if mode == 'A':
                sloc = nc.dram_tensor("sloc", [128, 32], F32, kind="ExternalOutput").ap()
                S.dma("sp", sloc, agbuf[:, :], r=["agbuf"], w=["sloc"])
                S.finish("sp", ["sloc"])
                return nc, S
            if mode == 'B':
                gin = nc.dram_tensor("gin", [4 * 128, 32], F32, kind="ExternalInput").ap()
                S.dma("pool", ag_out, gin, w=["ag_out"])
            elif NO_CC:
                for jj in range(4):
                    S.dma("pool", ag_out[jj * 128:(jj + 1) * 128, :], ag_in, r=["ag_in"], w=["ag_out"])
            else:
                S.dma("pool", ag_in, agbuf[:, :], r=["agbuf"], w=["ag_in"])
                S.custom_dma("pool", lambda g: g.collective_compute("AllGather", ALU.bypass,
                                                                    replica_groups=[[0, 1, 2, 3], [4, 5, 6, 7]],
                                                                    ins=[ag_in], outs=[ag_out]),
                             r=["ag_in"], w=["ag_out"])
            S.dma("pool", Gt[:, :, :], ag_out.rearrange("(j p) c -> p j c", p=128), r=["ag_out"], w=["Gt"])
            for m in range(3):
                for j in range(4):
                    if j == 0:
                        TS(Tm[:, m, :], Gt[:, 0, :], selt[:, m:m + 1], None, ALU.mult, None, ["Gt", "selt", "Tm"], ["Tm"])
                    else:
                        STT(Tm[:, m, :], Gt[:, j, :], selt[:, 3 * j + m:3 * j + m + 1], Tm[:, m, :], ALU.mult, ALU.add,
                            ["Gt", "selt", "Tm"], ["Tm"])
            S.op("dve", lambda g: g.tensor_copy(out=Sin[:, :], in_=Tm[:, 0, :]), r=["Tm"], w=["Sin"])
            for m in (1, 2):
                pr, pi_ = ar[:, 16 + m, :], ai[:, 16 + m, :]
                tr, ti = Tm[:, m, 0:16], Tm[:, m, 16:32]
                TT(T0, pr, tr, ALU.mult, AK + ["Tm", "sm"], ["sm"])
                TT(T1, pi_, ti, ALU.mult, AK + ["Tm", "sm"], ["sm"])
                TT(T0, T0, T1, ALU.subtract, ["sm"], ["sm"])
                TT(Sin[:, 0:16], Sin[:, 0:16], T0, ALU.add, ["sm", "Sin"], ["Sin"])
                TT(T0, pr, ti, ALU.mult, AK + ["Tm", "sm"], ["sm"])
                TT(T1, pi_, tr, ALU.mult, AK + ["Tm", "sm"], ["sm"])
                TT(T0, T0, T1, ALU.add, ["sm"], ["sm"])
                TT(Sin[:, 16:32], Sin[:, 16:32], T0, ALU.add, ["sm", "Sin"], ["Sin"])
        scan_pass(lambda ri, q: Sin[:, 16 * ri + q:16 * ri + q + 1], ["Sin"])
        for ri in range(2):
            S.op("dve", lambda g: g.tensor_copy(out=Sst[ri][:, :, 0], in_=Sin[:, 16 * ri:16 * ri + 16]),
                 r=["Sin"], w=[("Sst", ri), "Eall"])
        for c in range(4):
            qs = slice(4 * c, 4 * c + 4)
            cs, sn = cosT[:, qs, :], sinT[:, qs, :]
            TT(tmp, cs, ww[0][:, qs, :], ALU.mult, ["cosT", *ZK0], ["tmp"])
            TT(tmp2, sn, ww[1][:, qs, :], ALU.mult, ["sinT", *ZK1], ["tmp2"])
            TT(Sst[0][:, qs, 1:NB + 1], tmp, tmp2, ALU.subtract, ["tmp", "tmp2"], [("Sst", 0), "Eall"])
            TT(tmp, sn, ww[0][:, qs, :], ALU.mult, ["sinT", *ZK0], ["tmp"])
            TT(tmp2, cs, ww[1][:, qs, :], ALU.mult, ["cosT", *ZK1], ["tmp2"])
            TT(Sst[1][:, qs, 1:NB + 1], tmp, tmp2, ALU.add, ["tmp", "tmp2"], [("Sst", 1), "Eall"])

        shc = [128, 17, 4, 32]
        ZZW = [*ZK0, *ZK1, "tmp", "tmp2"] + ZK[0] + ZK[1]
        for c in range(4):
            CAt, lagT = CAt_b[c % 2], lagT_b[c % 2]
            KCA, KLG = ("CAt", c % 2), ("lagT", c % 2)
            XK = ["tmp", "tmp2"] if c % 2 == 1 else []
            qs = slice(4 * c, 4 * c + 4)
            a_r = bc(ar[:, 0:17, qs], 3, shc)
            a_i = bc(ai[:, 0:17, qs], 3, shc)
            c_r = bc(CBD[0][:, qs, :], 1, shc)
            c_i = bc(CBD[1][:, qs, :], 1, shc)
            TT(CAtmp[0], a_r, c_r, ALU.mult, AK + ["CBD"] + ZZW, ["CAtmp0", "EallT"])
            TT(CAtmp[1], a_i, c_i, ALU.mult, AK + ["CBD"] + ZZW, ["CAtmp1", "EallT"])
            TT(CAt[:, :, :, 0, :], CAtmp[0], CAtmp[1], ALU.subtract, ["CAtmp0", "CAtmp1"] + ZZW, [KCA] + XK)
            TT(CAtmp[0], a_r, c_i, ALU.mult, AK + ["CBD", KCA], ["CAtmp0"])
            TT(CAtmp[1], a_i, c_r, ALU.mult, AK + ["CBD", KCA], ["CAtmp1"])
            STT(CAt[:, :, :, 1, :], CAtmp[0], -1.0, CAtmp[1], ALU.mult, ALU.subtract, ["CAtmp0", "CAtmp1"], [KCA])
            for k4 in range(4):
                pg = PS[k4][:, :].rearrange("p (s c) -> p s c", s=4)
                for kk in range(4):
                    k = 4 * k4 + kk
                    for qq in range(4):
                        q = 4 * c + qq
                        S.op("pe", lambda g: g.matmul(pg[:, kk, 32 * qq:32 * qq + 32], lhsT=pad[0][:, q, :],
                                                      rhs=CAt[:, k, qq, 0, :], start=True, stop=False),
                             r=["pad", KCA], w=[("ps", k4)], signal=False)
                        S.op("pe", lambda g: g.matmul(pg[:, kk, 32 * qq:32 * qq + 32], lhsT=pad[1][:, q, :],
                                                      rhs=CAt[:, k, qq, 1, :], start=False, stop=True),
                             r=["pad", KCA], w=[("ps", k4)], signal=(kk == 3 and qq == 3))
                if k4 == 0:
                    STT(pg[:, 0, :], identf[:, :], dcol[:, c:c + 1], pg[:, 0, :], ALU.mult, ALU.add,
                        ["identf", "dcol", ("ps", 0)], [("ps", 0)])
                ACTF(lagT[:, 4 * k4:4 * k4 + 4, :], pg, AF.Copy, [("ps", k4)] + ZZW, [KLG] + XK)
            UK = [("uT", c, tb) for tb in range(4)]
            for j in range(16):
                py = PS[4 + j // 4][:, :].rearrange("p (s b) -> p s b", s=4)[:, j % 4, :]
                kb = ("ps", 4 + j // 4)
                for qq in range(4):
                    q = 4 * c + qq
                    rs = slice(32 * qq, 32 * qq + 32)
                    for ri in range(2):
                        S.op("pe", lambda g: g.matmul(py[rs, :], lhsT=CAt[:, j + 1, qq, ri, :], rhs=Sst[ri][:, q, 0:NB],
                                                      start=(ri == 0), stop=False, tile_position=(0, 32 * qq)),
                             r=[KCA, ("Sst", ri)], w=[kb], signal=False)
                for i in range(j + 1):
                    S.op("pe", lambda g: g.matmul(py[:, :], lhsT=lagT[:, j - i, :], rhs=uPM[:, c, i, :],
                                                  start=False, stop=(i == j)),
                         r=[KLG] + UK, w=[kb], signal=(i == j and j % 4 == 3))
            uv = uT[:, c, :].rearrange("p (b j) -> p j b", j=16)
            for j4 in range(4):
                pyv = PS[4 + j4][:, :].rearrange("p (s b) -> p s b", s=4)
                ACTF(uv[:, 4 * j4:4 * j4 + 4, :], pyv, AF.Copy, [("ps", 4 + j4)], UK)
        S.fence("s5done", ["sm", "ar", "ai", "mag", "ang", "nn", "Braw", "bb", "bbBD", "pad", "Craw", "CBD", "cosT",
                           "sinT", *ZK0, *ZK1, "tmp", "tmp2", "Eall", "Xt", ("CAt", 0), ("CAt", 1), ("lagT", 0), ("lagT", 1), "CAtmp0", "CAtmp1",
                           ("Sst", 0), ("Sst", 1), ("Cexp", 0), ("Cexp", 1), "zz0"] + ZK[0] + ZK[1])

    UT_ALL = [("uT", c, tb) for c in range(4) for tb in range(4)]
    for c in range(4):
        wb = slab_load(w_in, OFF_ZS + c * 128)
        for tb in range(4):
            pb = nextbank()
            mm_block(pb, wb, lambda k: hT[:, k, HALO + tb * 512: HALO + (tb + 1) * 512], ht_keys(tb))
            S.op("act", lambda e: e.activation(out=y2[:, c, tb * 512:(tb + 1) * 512], in_=PS[pb][:, :], func=AF.Silu),
                 r=[("ps", pb)], w=[("y2", c, tb)])
    gi = 0
    GELU_SQ = float(math.sqrt(0.044715 * 1.5957691216))
    for c in range(4):
        for tb in range(4):
            g0 = gt[gi % 2]
            gi += 1
            yv = uT[:, c, tb * 512:(tb + 1) * 512]
            S.op("act", lambda e: e.activation(out=g0[:], in_=yv, func=AF.Square, scale=GELU_SQ),
                 r=[("uT", c, tb)], w=[("gt", id(g0))])
            S.op("dve", lambda e: e.scalar_tensor_tensor(out=g0[:], in0=g0[:], scalar=1.5957691216, in1=yv,
                                                         op0=ALU.add, op1=ALU.mult),
                 r=[("gt", id(g0)), ("uT", c, tb)], w=[("gt", id(g0))])
            S.op("act", lambda e: e.activation(out=g0[:], in_=g0[:], func=AF.Sigmoid),
                 r=[("gt", id(g0))], w=[("gt", id(g0))])
            S.op("dve", lambda e: e.tensor_tensor(out=yv, in0=g0[:], in1=yv, op=ALU.mult),
                 r=[("gt", id(g0)), ("uT", c, tb)], w=[("uT", c, tb)])

    for c in range(4):
        wb = slab_load(w_glu, c * 128, kch=4)
        for tb in range(4):
            pb = nextbank()
            mm_block(pb, wb, lambda k: uT[:, k, tb * 512:(tb + 1) * 512], [("uT", k, tb) for k in range(4)], kch=4)
            s0 = sg[(c * 4 + tb) % 4]
            S.op("act", lambda e: e.activation(out=s0[:], in_=PS[pb][:, :], func=AF.Sigmoid,
                                               bias=bgluT[:, c:c + 1]),
                 r=[("ps", pb), "bgluT"], w=[("sg", id(s0))])
            S.op("dve", lambda e: e.tensor_tensor(out=s0[:], in0=s0[:], in1=uT[:, c, tb * 512:(tb + 1) * 512], op=ALU.mult),
                 r=[("sg", id(s0)), ("uT", c, tb)], w=[("sg", id(s0))])
            S.op("dve", lambda e: e.tensor_tensor(out=y2[:, c, tb * 512:(tb + 1) * 512], in0=s0[:],
                                                  in1=y2[:, c, tb * 512:(tb + 1) * 512], op=ALU.mult),
                 r=[("sg", id(s0)), ("y2", c, tb)], w=[("y2", c, tb)])

    def wout_slab(jc):
        a = jc % 2
        S.dma("sp", wst[a][:, :, :], w_out[:, jc * 128:(jc + 1) * 128].rearrange("(k p) c -> p k c", p=128),
              w=[("wst", a)])
        S.op("pool", lambda e: e.tensor_copy(out=woutb[:, :, jc * 128:(jc + 1) * 128], in_=wst[a][:, :, :]),
             r=[("wst", a)], w=[("woutb", jc)] + UT_ALL)

    cwsb = xt[0]
    S.dma("pool", cwsb[0:31, :], conv_w, w=[("xt", 0)])
    pcw = PS[0][:, 0:256].rearrange("p (c k) -> p c k", c=8)
    for ch in range(8):
        S.op("pe", lambda e: e.matmul(pcw[:, ch, 0:31], lhsT=cwsb[0:31, ch * 128:(ch + 1) * 128],
                                      rhs=identf[0:31, 0:31], start=True, stop=True),
             r=[("xt", 0), "identf"], w=[("ps", 0)], signal=(ch == 7))
    S.op("dve", lambda e: e.tensor_copy(out=cwT[:, :, 0:31], in_=pcw[:, :, 0:31]), r=[("ps", 0)], w=["cwT"])

    def tokblocks():
        return [(0, HALO, [("hT", 0)])] + [(HALO + tb * 512, 512, ht_keys(tb)) for tb in range(4)]

    for i in range(8):
        wa = slab_load(w_in, OFF_CA + i * 128)
        wbb = slab_load(w_in, OFF_CB + i * 128)
        for bi, (c0, n, keys) in enumerate(tokblocks()):
            pa = nextbank()
            pbk = nextbank()
            mm_block(pa, wa, lambda k: hT[:, k, c0:c0 + n], keys, n=n)
            mm_block(pbk, wbb, lambda k: hT[:, k, c0:c0 + n], keys, n=n)
            s0 = sg[(i * 5 + bi) % 4]
            S.op("act", lambda e: e.activation(out=s0[:, :n], in_=PS[pbk][:, :n], func=AF.Sigmoid),
                 r=[("ps", pbk)], w=[("sg", id(s0))])
            S.op("dve", lambda e: e.tensor_tensor(out=cu[:, i, c0:c0 + n], in0=PS[pa][:, :n], in1=s0[:, :n],
                                                  op=ALU.mult),
                 r=[("ps", pa), ("sg", id(s0))], w=[("cu", i, bi)])

    SUMB, SQB = 6, 7
    for i in range(8):
        pass
    dg_built = {}

    dg_cnt = [0]

    def build_dg(i):
        nb_ = dg_cnt[0] % 2
        dg_cnt[0] += 1
        t = dg[nb_]
        S.op("dve", lambda e: e.tensor_tensor(out=t[:, :, :], in0=identf[:, :].unsqueeze(1).broadcast_to([128, 31, 128]),
                                              in1=cwT[:, i, 0:31].unsqueeze(2).broadcast_to([128, 31, 128]), op=ALU.mult),
             r=["identf", "cwT"], w=[("dg", nb_)])
        return t, nb_

    def conv_tb(tb):
        for i in range(8):
            t, nb_ = build_dg(i)
            if tb == 3:
                wout_slab(i)
            pb = nextbank(1, 6)
            base = HALO + tb * 512 - 30
            for k in range(31):
                S.op("pe", lambda e: e.matmul(PS[pb][:, :], lhsT=t[:, k, :], rhs=cu[:, i, base + k: base + k + 512],
                                              start=(k == 0), stop=(k == 30)),
                     r=[("dg", nb_), ("cu", i, tb), ("cu", i, 1 + tb)], w=[("ps", pb)], signal=(k == 30))
            S.op("act", lambda e: e.activation(out=cu[:, i, HALO + tb * 512: HALO + (tb + 1) * 512], in_=PS[pb][:, :],
                                               func=AF.Identity, bias=cbT[:, i:i + 1]),
                 r=[("ps", pb), "cbT"], w=[("cu", i, 1 + tb)])

    CU_KEYS = lambda i: [("cu", i, bi) for bi in range(5)]
    vi = 0

    def zc_slab(i):
        wb = slab_load(w_in, OFF_ZC + i * 128)
        for tb2 in range(4):
            pb = nextbank(1, 6)
            mm_block(pb, wb, lambda k: hT[:, k, HALO + tb2 * 512: HALO + (tb2 + 1) * 512], ht_keys(tb2))
            S.op("act", lambda e: e.activation(out=mrg[:, i, tb2 * 512:(tb2 + 1) * 512], in_=PS[pb][:, :], func=AF.Silu),
                 r=[("ps", pb)], w=[("mrg", i, tb2)])

    for tb in reversed(range(4)):
        conv_tb(tb)
        vs = lambda i: cu[:, i, HALO + tb * 512: HALO + (tb + 1) * 512]
        for i in range(8):
            q = sqb[vi % 2]
            vi += 1
            S.op("dve", lambda e: e.tensor_tensor(out=q[:], in0=vs(i), in1=vs(i), op=ALU.mult),
                 r=[("cu", i, 1 + tb)], w=[("sqb", id(q))])
            S.op("pe", lambda e: e.matmul(PS[SUMB][:, :], lhsT=onesb[:], rhs=vs(i), start=(i == 0), stop=(i == 7)),
                 r=["onesb", ("cu", i, 1 + tb)], w=[("ps", SUMB)], signal=False)
            S.op("pe", lambda e: e.matmul(PS[SQB][:, :], lhsT=onesb[:], rhs=q[:], start=(i == 0), stop=(i == 7)),
                 r=["onesb", ("sqb", id(q))], w=[("ps", SQB)], signal=True)
        zc_slab(2 * (3 - tb))
        zc_slab(2 * (3 - tb) + 1)
        S.op("act", lambda e: e.activation(out=st_mean[:], in_=PS[SUMB][:, :], func=AF.Copy, scale=1.0 / D),
             r=[("ps", SUMB)], w=[K_MEAN])
        S.op("dve", lambda e: e.tensor_tensor(out=st_tmp[:], in0=st_mean[:], in1=st_mean[:], op=ALU.mult),
             r=[K_MEAN], w=[K_TMP])
        S.op("dve", lambda e: e.scalar_tensor_tensor(out=st_tmp[:], in0=PS[SQB][:, :], scalar=1.0 / D, in1=st_tmp[:],
                                                     op0=ALU.mult, op1=ALU.subtract),
             r=[("ps", SQB), K_TMP], w=[K_TMP])
        S.op("act", lambda e: e.activation(out=st_rstd[:], in_=st_tmp[:], func=AF.Sqrt, bias=epsc[:, 1:2]),
             r=[K_TMP, "epsc"], w=[K_RSTD])
        S.op("dve", lambda e: e.reciprocal(out=st_rstd[:], in_=st_rstd[:]), r=[K_RSTD], w=[K_RSTD])
        for i in range(8):
            g0 = gt[i % 2]
            S.op("dve", lambda e: e.tensor_tensor(out=g0[:], in0=vs(i), in1=st_mean[:], op=ALU.subtract),
                 r=[("cu", i, 1 + tb), K_MEAN], w=[("gt", id(g0))])
            S.op("dve", lambda e: e.tensor_tensor(out=g0[:], in0=g0[:], in1=st_rstd[:], op=ALU.mult),
                 r=[("gt", id(g0)), K_RSTD], w=[("gt", id(g0))])
            S.op("act", lambda e: e.activation(out=vs(i), in_=g0[:], func=AF.Silu, scale=lngT[:, i:i + 1],
                                               bias=lnbT[:, i:i + 1]),
                 r=[("gt", id(g0)), "lngT", "lnbT"], w=[("cu", i, 1 + tb)])
    for tb in range(4):
        for i in range(8):
            hsl = slice(HALO + tb * 512, HALO + (tb + 1) * 512)
            S.op("dve", lambda e: e.tensor_tensor(out=cu[:, i, hsl], in0=cu[:, i, hsl], in1=mrg[:, i, tb * 512:(tb + 1) * 512],
                                                  op=ALU.mult),
                 r=[("cu", i, 1 + tb), ("mrg", i, tb)], w=[("cu", i, 1 + tb)])

    for j in range(8):
        wco_b = slab_load(w_co, j * 128)
        wso_b = slab_load(w_so, j * 128, kch=4)
        wgc_b = slab_load(w_in, OFF_GC + j * 128)
        for phase in range(2):
            if phase == 1:
                wgs_b = slab_load(w_in, OFF_GS + j * 128)
            for tb in range(4):
                tsl = slice(tb * 512, (tb + 1) * 512)
                hsl = slice(HALO + tb * 512, HALO + (tb + 1) * 512)
                if phase == 0:
                    pa = nextbank(1, 8)
                    pc = nextbank(1, 8)
                    mm_block(pa, wco_b, lambda k: cu[:, k, hsl], [("cu", k, 1 + tb) for k in range(8)])
                    mm_block(pc, wgc_b, lambda k: hT[:, k, hsl], ht_keys(tb))
                    s0 = sg[tb % 4]
                    S.op("act", lambda e: e.activation(out=s0[:], in_=PS[pc][:, :], func=AF.Sigmoid),
                         r=[("ps", pc)], w=[("sg", id(s0))])
                    S.op("dve", lambda e: e.tensor_tensor(out=mrg[:, j, tsl], in0=PS[pa][:, :], in1=s0[:], op=ALU.mult),
                         r=[("ps", pa), ("sg", id(s0))], w=[("mrg", j, tb)])
                else:
                    pb2 = nextbank(1, 8)
                    pd = nextbank(1, 8)
                    mm_block(pb2, wso_b, lambda k: y2[:, k, tsl], [("y2", k, tb) for k in range(4)], kch=4)
                    mm_block(pd, wgs_b, lambda k: hT[:, k, hsl], ht_keys(tb))
                    s0 = sg[tb % 4]
                    S.op("act", lambda e: e.activation(out=s0[:], in_=PS[pd][:, :], func=AF.Sigmoid),
                         r=[("ps", pd)], w=[("sg", id(s0))])
                    S.op("dve", lambda e: e.tensor_tensor(out=s0[:], in0=PS[pb2][:, :], in1=s0[:], op=ALU.mult),
                         r=[("ps", pb2), ("sg", id(s0))], w=[("sg", id(s0))])
                    S.op("dve", lambda e: e.tensor_tensor(out=mrg[:, j, tsl], in0=s0[:], in1=mrg[:, j, tsl], op=ALU.add),
                         r=[("sg", id(s0)), ("mrg", j, tb)], w=[("mrg", j, tb)])

    S.fence("hTdead", HT_KEYS)
    S.dma("pool", pgB[:], post_g.partition_broadcast(128), w=["pgB"])
    WOUT_KEYS = [("woutb", jc) for jc in range(8)]
    ssq2 = sb("ssq2", [128, 32], F32)
    rstd2 = sb("rstd2", [128, 16], F32)
    S.op("dve", lambda e: e.memset(ssq2[:], 0.0), w=["ssq2"])
    for tt in range(16):
        b = tt % 2
        tb = tt // 4
        S.dma("sp", xt[b][:, :], x[HALO + tt * 128: HALO + (tt + 1) * 128, :], w=[("xt", b)])
        pbs = [nextbank(1, 8), nextbank(1, 8)]
        for hf in range(2):
            for k in range(8):
                S.op("pe", lambda e: e.matmul(PS[pbs[hf]][:, :], lhsT=mrg[:, k, tt * 128:(tt + 1) * 128],
                                              rhs=woutb[:, k, hf * 512:(hf + 1) * 512], start=(k == 0), stop=(k == 7)),
                     r=[("mrg", k, tb)] + WOUT_KEYS[hf * 4:(hf + 1) * 4], w=[("ps", pbs[hf])], signal=(k == 7))
            S.op("act", lambda e: e.activation(out=junk[:, 0:512], in_=PS[pbs[hf]][:, :], func=AF.Square,
                                               accum_out=ssq2[:, 2 * tt + hf: 2 * tt + hf + 1]),
                 r=[("ps", pbs[hf]), "ssq2"], w=[K_JUNK, ("ssq2", tt, hf)])
        S.op("dve", lambda e: e.tensor_tensor(out=rstd2[:, tt:tt + 1], in0=ssq2[:, 2 * tt:2 * tt + 1],
                                              in1=ssq2[:, 2 * tt + 1:2 * tt + 2], op=ALU.add),
             r=[("ssq2", tt, 0), ("ssq2", tt, 1)], w=[("rstd2", tt)])
        S.op("act", lambda e: e.activation(out=rstd2[:, tt:tt + 1], in_=rstd2[:, tt:tt + 1], func=AF.Sqrt,
                                           scale=1.0 / D, bias=epsc[:, 0:1]),
             r=[("rstd2", tt), "epsc"], w=[("rstd2", tt)])
        S.op("dve", lambda e: e.reciprocal(out=rstd2[:, tt:tt + 1], in_=rstd2[:, tt:tt + 1]),
             r=[("rstd2", tt)], w=[("rstd2", tt)])
        for hf in range(2):
            hs = slice(hf * 512, (hf + 1) * 512)
            S.op("dve", lambda e: e.scalar_tensor_tensor(out=ot[b][:, hs], in0=PS[pbs[hf]][:, :],
                                                         scalar=rstd2[:, tt:tt + 1], in1=pgB[:, hs],
                                                         op0=ALU.mult, op1=ALU.mult),
                 r=[("ps", pbs[hf]), ("rstd2", tt), "pgB"], w=[("ot", b, hf)])
            S.op("pool", lambda e: e.tensor_tensor(out=ot[b][:, hs], in0=ot[b][:, hs], in1=xt[b][:, hs], op=ALU.add),
                 r=[("ot", b, hf), ("xt", b)], w=[("ot", b, hf)])
        S.dma("pool", out_d[tt * 128:(tt + 1) * 128, :], ot[b][:, :], r=[("ot", b, 0), ("ot", b, 1)], w=[("outd", tt)])
    dbg_aps = {"hT": hT, "uT": uT, "cu": cu, "mrg": mrg, "y2": y2}
    dbg_keys = {"hT": HT_KEYS, "uT": UT_ALL, "cu": [("cu", i, b) for i in range(8) for b in range(5)],
                "mrg": [("mrg", j, tb) for j in range(8) for tb in range(4)],
                "y2": [("y2", c, tb) for c in range(4) for tb in range(4)]}
    fin = [("outd", tt) for tt in range(16)]
    for name in debug:
        ap = dbg_aps[name]
        dd = nc.dram_tensor("dbg_" + name, list(ap.shape), ap.dtype, kind="ExternalOutput").ap()
        S.dma("sp", dd, ap, r=dbg_keys[name], w=[("dbgout", name)])
        fin.append(("dbgout", name))
    S.finish("sp", fin)
    return nc, S


_CONST = {}


def _consts():
    if not _CONST:
        _CONST["c_ident"] = np.eye(128, dtype=np.float32)
        kr = np.concatenate([np.arange(NEXP), [2048, 4096]]).astype(np.float32)
        _CONST["c_kramp"] = np.ascontiguousarray(np.broadcast_to(kr, (128, NEXP + 2)))
        br = (R0 * (np.arange(NB) + 1)).astype(np.float32)
        _CONST["c_bramp"] = np.ascontiguousarray(np.broadcast_to(br, (128, NB)))
        pm = np.zeros((128, 2), np.float32)
        par = (np.arange(128) // 16) % 2
        pm[par == 0, 0] = 1.0
        pm[par == 1, 1] = 1.0
        _CONST["c_pmask"] = pm
    return _CONST


def make_in_maps(inputs, fused=False):
    f = lambda a: np.ascontiguousarray(np.asarray(a, dtype=np.float32))
    x = f(inputs["x"])
    shared = {
        "w_in": f(inputs["w_in"][0]), "pre_g": f(inputs["pre_norm_gain"][0]), "conv_w": f(inputs["conv_w"][0]),
        "conv_b": f(inputs["conv_b"][0]), "ln_g": f(inputs["conv_ln_gain"][0]), "ln_b": f(inputs["conv_ln_bias"][0]),
        "w_co": f(inputs["w_conv_out"][0]), "lam_re": f(inputs["ssm_lambda_re"][0]),
        "lam_im": f(inputs["ssm_lambda_im"][0]), "log_dt": f(inputs["ssm_log_dt"][0]),
        "b_re": f(inputs["ssm_b_re"][0]), "b_im": f(inputs["ssm_b_im"][0]), "c_re": f(inputs["ssm_c_re"][0]),
        "c_im": f(inputs["ssm_c_im"][0]), "d_in": f(inputs["ssm_d"][0]), "w_glu": f(inputs["w_ssm_glu"][0]),
        "b_glu": f(inputs["b_ssm_glu"][0]), "w_so": f(inputs["w_ssm_out"][0]), "w_out": f(inputs["w_out"][0]),
        "post_g": f(inputs["post_norm_gain"][0]),
    }
    shared.update(_consts())
    maps = []
    for c in range(NCORES):
        bi, ci = c // 4, c % 4
        xc = np.zeros((NT + 3 * TOK if fused else NT, D), np.float32)
        t0 = ci * TOK
        xc[HALO:NT] = x[bi, t0:t0 + TOK]
        if fused:
            for m_ in range(3):
                cj = ci - 1 - m_
                if cj >= 0:
                    xc[NT + m_ * TOK: NT + (m_ + 1) * TOK] = x[bi, cj * TOK:(cj + 1) * TOK]
        if ci > 0:
            xc[:HALO] = x[bi, t0 - HALO:t0]
        sel = np.zeros((8, 3), np.float32)
        for j in range(4):
            if j < ci:
                sel[j, ci - 1 - j] = 1.0
        m = dict(shared)
        m["x"] = xc
        m["c_sel"] = np.ascontiguousarray(np.broadcast_to(sel.reshape(1, 24), (128, 24)))
        maps.append(m)
    return maps


_PROG = {}


def kernel(**inputs):
    if "ncF" not in _PROG:
        _PROG["ncF"] = build_program(mode='F')[0]
    maps = make_in_maps(inputs, fused=True)
    res = run_bass_kernel_spmd(_PROG["ncF"], maps, core_ids=list(range(NCORES)))
    out = np.empty((2, 8192, D), np.float32)
    for c in range(NCORES):
        out[c // 4, (c % 4) * TOK:(c % 4 + 1) * TOK] = res.results[c]["out"]
    return out
```

```python
import math
import numpy as np
import concourse.bass as bass
import concourse.mybir as mybir
from concourse.bass_utils import run_bass_kernel_spmd

F32 = mybir.dt.float32
BF16 = mybir.dt.bfloat16
AF = mybir.ActivationFunctionType
ALU = mybir.AluOpType

NCORES = 8
import os
NO_CC = bool(os.environ.get('NO_CC'))
D = 1024
TOK = 2048
HALO = 32
NT = TOK + HALO
INW = 6144
R0 = 16
NB = TOK // R0
NEXP = R0 + 1
TWO_PI = float(2.0 * np.pi)
MAGIC = 12582912.0
C1 = 6.28125
C2 = float(2.0 * np.pi - 6.28125)

OFF_CA, OFF_CB, OFF_ZC, OFF_U, OFF_ZS, OFF_GC, OFF_GS = 0, 1024, 2048, 3072, 3584, 4096, 5120


class Sched:
    def __init__(self, nc, n_dma_sems=24):
        self.nc = nc
        self.engs = {"pe": nc.tensor, "act": nc.scalar, "dve": nc.vector, "pool": nc.gpsimd, "sp": nc.sync}
        self.sem = {k: nc.alloc_semaphore("prog_" + k) for k in self.engs}
        self.cnt = {k: 0 for k in self.engs}
        self.waited = {k: {} for k in self.engs}
        self.bufs = {}
        self.semobj = {("E", k): self.sem[k] for k in self.engs}
        self.dpool = {}
        for q, n in (("sp", n_dma_sems), ("pool", 16)):
            sems = [nc.alloc_semaphore("dma_%s%d" % (q, i)) for i in range(n)]
            self.dpool[q] = {"sems": sems, "n": 0}
            for i, sm_ in enumerate(sems):
                self.semobj[("D", q, i)] = sm_
        self.dn = 0
        self.nwaits = 0
        self.fences = []
        self.fence_scratch = nc.alloc_sbuf_tensor("fence_scr", [128, 8], F32).ap()

    def _need(self, e, deps):
        best = {}
        for d in deps:
            if d is None:
                continue
            k, v = d
            if e == "pe" and k == ("E", "pe"):
                continue
            if v > best.get(k, 0):
                best[k] = v
        for k, v in best.items():
            if v > self.waited[e].get(k, 0):
                self.engs[e].wait_ge(self.semobj[k], v)
                self.waited[e][k] = v
                self.nwaits += 1

    def _deps(self, r, w):
        deps = []
        for k in r:
            b = self.bufs.get(k)
            if b is not None:
                deps.append(b["w"])
        for k in w:
            b = self.bufs.get(k)
            if b is not None:
                deps.append(b["w"])
                deps.extend(b["r"].items())
        return deps

    def _mark(self, tag, r, w):
        for k in r:
            b = self.bufs.setdefault(k, {"w": None, "r": {}})
            if tag[1] > b["r"].get(tag[0], 0):
                b["r"][tag[0]] = tag[1]
        for k in w:
            self.bufs[k] = {"w": tag, "r": {}}

    def fence(self, name, old_keys):
        self.op("dve", lambda e: e.memset(self.fence_scratch, 0.0), w=list(old_keys) + [("fence", name)])
        self.fences.append(("fence", name))

    def op(self, e, fn, r=(), w=(), signal=True):
        r = list(r) + self.fences
        self._need(e, self._deps(r, w))
        ins = fn(self.engs[e])
        if signal:
            self.cnt[e] += 1
            ins.then_inc(self.sem[e], 1)
            idx = self.cnt[e]
        else:
            assert e == "pe"
            idx = self.cnt[e] + 1
        self._mark((("E", e), idx), r, w)
        return ins

    def dma(self, e, out, in_, r=(), w=(), **kw):
        self.custom_dma(e, lambda g: g.dma_start(out=out, in_=in_, **kw), r=r, w=w)

    def custom_dma(self, e, fn, r=(), w=()):
        P = self.dpool[e]
        ns = len(P["sems"])
        s = P["n"] % ns
        val = 16 * (P["n"] // ns + 1)
        r = list(r) + self.fences
        deps = self._deps(r, w)
        if P["n"] >= ns:
            deps.append((("D", e, s), val - 16))
        self._need(e, deps)
        fn(self.engs[e]).then_inc(P["sems"][s], 16)
        P["n"] += 1
        self.dn += 1
        self._mark((("D", e, s), val), r, w)

    def finish(self, e, keys):
        self._need(e, self._deps(keys, ()))


def build_program(s5_on=True, debug=(), mode='B'):
    nc = bass.Bass("TRN2", target_bir_lowering=False)
    S = Sched(nc)

    def din(name, shape):
        return nc.dram_tensor(name, list(shape), F32, kind="ExternalInput").ap()

    x = din("x", [NT + 3 * TOK, D] if mode == "F" else [NT, D])
    w_in = din("w_in", [D, INW])
    pre_g = din("pre_g", [D])
    conv_w = din("conv_w", [31, D])
    conv_b = din("conv_b", [D])
    ln_g = din("ln_g", [D])
    ln_b = din("ln_b", [D])
    w_co = din("w_co", [D, D])
    lam_re = din("lam_re", [32, 64])
    lam_im = din("lam_im", [32, 64])
    log_dt = din("log_dt", [32])
    b_re = din("b_re", [32, 64, 16])
    b_im = din("b_im", [32, 64, 16])
    c_re = din("c_re", [32, 16, 64])
    c_im = din("c_im", [32, 16, 64])
    d_in = din("d_in", [32, 16])
    w_glu = din("w_glu", [512, 512])
    b_glu = din("b_glu", [512])
    w_so = din("w_so", [512, D])
    w_out = din("w_out", [D, D])
    post_g = din("post_g", [D])
    c_ident = din("c_ident", [128, 128])
    c_kramp = din("c_kramp", [128, NEXP + 2])
    c_bramp = din("c_bramp", [128, NB])
    c_pmask = din("c_pmask", [128, 2])
    c_sel = din("c_sel", [128, 24])
    out_d = nc.dram_tensor("out", [TOK, D], F32, kind="ExternalOutput").ap()
    ag_in = nc.dram_tensor("ag_in", [128, 32], F32).ap()
    ag_out = nc.dram_tensor("ag_out", [4 * 128, 32], F32).ap()

    dbg_out = {}

    def sb(name, shape, dt):
        return nc.alloc_sbuf_tensor(name, list(shape), dt).ap()

    def ps(name, shape, dt=F32):
        return nc.alloc_psum_tensor(name, list(shape), dt).ap()

    hTraw = sb("hT", [128, 8 * NT], BF16)
    hT = hTraw.rearrange("p (k t) -> p k t", k=8)
    uT = sb("uT", [128, 4, TOK], BF16)
    BIGB = 104 * 1024
    big = sb("big", [128, BIGB // 2], BF16)

    def carve(off, shape, dt, base=None):
        base = big if base is None else base
        n = int(np.prod(shape[1:]))
        esz = 4 if dt == F32 else 2
        assert off % 4 == 0
        v = base[:, off // 2: off // 2 + n * esz // 2]
        if dt == F32:
            v = v.bitcast(F32)
        if len(shape) == 3:
            v = v.rearrange("p (a b) -> p a b", a=shape[1])
        elif len(shape) == 4:
            v = v.rearrange("p (a b c) -> p a b c", a=shape[1], b=shape[2])
        elif len(shape) == 5:
            v = v.rearrange("p (a b c d) -> p a b c d", a=shape[1], b=shape[2], c=shape[3])
        return v

    y2 = carve(0, [128, 4, TOK], BF16)
    cu = carve(16384, [128, 8, NT], BF16)
    dg = [carve(16384 + 33280 + i * 7936, [128, 31, 128], BF16) for i in range(2)]
    mrg = carve(16384 + 33280 + 15872, [128, 8, TOK], BF16)
    assert 16384 + 33280 + 15872 + 32768 <= BIGB
    woutb = uT.rearrange("p a b -> p (a b)").rearrange("p (k c) -> p k c", k=8)
    pgB = carve(16384, [128, D], F32, base=hTraw)
    ot = [carve(20480 + i * 4096, [128, D], F32, base=hTraw) for i in range(2)]
    wst = [sb("wst%d" % i, [128, 8, 128], F32) for i in range(2)]
    wbf = [sb("wbf%d" % i, [128, 8, 128], BF16) for i in range(4)]
    xt = [sb("xt%d" % i, [128, D], F32) for i in range(3)]
    xs = [sb("xs%d" % i, [128, D], BF16) for i in range(2)]
    gt = [sb("gt%d" % i, [128, 512], F32) for i in range(2)]
    junk = gt[1].bitcast(BF16)
    K_JUNK = ("gt", id(gt[1]))
    sg = [sb("sg%d" % i, [128, 512], F32) for i in range(4)]
    identf = sb("identf", [128, 128], F32)
    identb = sb("identb", [128, 128], BF16)
    onesb = sb("onesb", [128, 128], BF16)
    gT = sb("gT", [128, 8], F32)
    cbT = sb("cbT", [128, 8], F32)
    lngT = sb("lngT", [128, 8], F32)
    lnbT = sb("lnbT", [128, 8], F32)
    bgluT = sb("bgluT", [128, 4], F32)
    dcol = sb("dcol", [128, 4], F32)
    ssq = sb("ssq", [128, 80], F32)
    rstd = sb("rstd", [128, 80], F32)
    cwT = sb("cwT", [128, 8, 32], F32)
    st_mean, st_rstd, st_tmp = sg[0], sg[1], sg[2]
    K_MEAN, K_RSTD, K_TMP = ("sg", id(sg[0])), ("sg", id(sg[1])), ("sg", id(sg[2]))
    sqb = [sb("sqb%d" % i, [128, 512], BF16) for i in range(2)]

    PS = [ps("ps%d" % i, [128, 512]) for i in range(8)]

    E = S.engs
    for i in range(8):
        pass

    S.dma("pool", identf[:], c_ident, w=["identf"])
    S.op("dve", lambda e: e.tensor_copy(out=identb[:], in_=identf[:]), r=["identf"], w=["identb"])
    S.op("dve", lambda e: e.memset(onesb[:], 1.0), w=["onesb"])
    S.op("dve", lambda e: e.memset(ssq[:], 0.0), w=["ssq"])

    def load_cols(dst, src, n, key):
        S.dma("pool", dst[:, 0:n], src.rearrange("(c p) -> p c", p=128), w=[key], allow_slow_non_contiguous=True)

    load_cols(gT, pre_g, 8, "gT")
    load_cols(cbT, conv_b, 8, "cbT")
    load_cols(lngT, ln_g, 8, "lngT")
    load_cols(lnbT, ln_b, 8, "lnbT")
    load_cols(bgluT, b_glu, 4, "bgluT")

    pst = PS[0].bitcast(BF16)

    p1_cnt = [0]
    p1_pend = []
    XB = xt + [w_.rearrange("p a b -> p (a b)") for w_ in wst]
    KX = [("xt", 0), ("xt", 1), ("xt", 2), ("wst", 0), ("wst", 1)]

    def p1_front(tt_, passno):
        rows = HALO if tt_ == 0 else 128
        r0 = 0 if tt_ == 0 else HALO + (tt_ - 1) * 128
        xr = r0 if passno == 3 else NT + passno * TOK + (tt_ - 1) * 128
        g = p1_cnt[0]
        p1_cnt[0] += 1
        b = g % 5
        tt = passno * 17 + tt_
        S.dma("sp", XB[b][:rows, :], x[xr:xr + rows, :], w=[KX[b]])
        S.op("act", lambda e: e.activation(out=junk[:rows, :], in_=XB[b][:rows, :], func=AF.Square,
                                           accum_out=ssq[:rows, tt:tt + 1]),
             r=[KX[b], "ssq"], w=[K_JUNK, ("ssq", tt)])
        S.op("act", lambda e: e.activation(out=rstd[:rows, tt:tt + 1], in_=ssq[:rows, tt:tt + 1], func=AF.Sqrt,
                                           scale=1.0 / D, bias=epsc[:rows, 0:1]),
             r=[("ssq", tt), "epsc"], w=[("rstd", tt)])
        S.op("dve", lambda e: e.reciprocal(out=rstd[:rows, tt:tt + 1], in_=rstd[:rows, tt:tt + 1]),
             r=[("rstd", tt)], w=[("rstd", tt)])
        return (tt_, passno, g)

    def p1_back(tt_, passno, g):
        rows = HALO if tt_ == 0 else 128
        r0 = 0 if tt_ == 0 else HALO + (tt_ - 1) * 128
        b = g % 5
        pbk = g % 2
        pst = PS[pbk].bitcast(BF16)
        tt = passno * 17 + tt_
        if g % 2 == 1:
            S.op("dve", lambda e: e.tensor_scalar(out=xs[pbk][:rows, :], in0=XB[b][:rows, :],
                                                  scalar1=rstd[:rows, tt:tt + 1], scalar2=None, op0=ALU.mult),
                 r=[KX[b], ("rstd", tt)], w=[("xs", pbk)])
        else:
            S.op("act", lambda e: e.activation(out=xs[pbk][:rows, :], in_=XB[b][:rows, :], func=AF.Copy,
                                               scale=rstd[:rows, tt:tt + 1]),
                 r=[KX[b], ("rstd", tt)], w=[("xs", pbk)])
        pv = pst.rearrange("p (k t) -> p k t", k=8)
        for kc in range(8):
            S.op("pe", lambda e: e.transpose(out=pv[:, kc, :rows], in_=xs[pbk][:rows, kc * 128:(kc + 1) * 128],
                                             identity=identb[:rows, :rows]),
                 r=[("xs", pbk), "identb"], w=[("ps", pbk)], signal=(kc == 7))
        S.op("dve", lambda e: e.tensor_tensor(out=hT[:, :, r0:r0 + rows], in0=pv[:, :, :rows],
                                              in1=gT[:, :].unsqueeze(2).broadcast_to([128, 8, rows]), op=ALU.mult),
             r=[("ps", pbk), "gT"], w=[("hT", tt_)])

    def p1_push(tt_, passno):
        st = p1_front(tt_, passno)
        if os.environ.get("NO_STAG"):
            p1_back(*st)
            return
        if p1_pend:
            p1_back(*p1_pend.pop())
        p1_pend.append(st)

    def p1_flush():
        if p1_pend:
            p1_back(*p1_pend.pop())

    epsc = sb("epsc", [128, 2], F32)
    S.op("dve", lambda e: e.memset(epsc[:, 0:1], 1e-6), w=["epsc"])
    S.op("dve", lambda e: e.memset(epsc[:, 1:2], 1e-5), r=["epsc"], w=["epsc"])

    def run_p1(passno):
        for tt in range(0 if passno == 3 else 1, 17):
            p1_push(tt, passno)
        p1_flush()

    def p1_tb_tiles(passno, tb):
        return ([0] if (passno == 3 and tb == 0) else []) + list(range(1 + 4 * tb, 5 + 4 * tb))

    FUSED = (mode == "F" and s5_on)
    if not FUSED:
        run_p1(3)

    HT_KEYS = [("hT", tt) for tt in range(17)]

    def ht_keys(tb):
        return [("hT", 1 + tb * 4 + i) for i in range(4)]

    slab_n = [0]

    plan = []
    plan += [(w_in, OFF_ZS + c * 128, 8) for c in range(4)]
    plan += [(w_glu, c * 128, 4) for c in range(4)]
    for i in range(8):
        plan += [(w_in, OFF_CA + i * 128, 8), (w_in, OFF_CB + i * 128, 8)]
    plan += [(w_in, OFF_ZC + i * 128, 8) for i in range(8)]
    for j in range(8):
        plan += [(w_co, j * 128, 8), (w_so, j * 128, 4), (w_in, OFF_GC + j * 128, 8), (w_in, OFF_GS + j * 128, 8)]
    issued = [0]

    def _issue(n):
        src, col0, kch = plan[n]
        a, b = n % 2, n % 4
        S.dma("sp", wst[a][:, :kch, :], src[:, col0:col0 + 128].rearrange("(k p) c -> p k c", p=128),
              w=[("wst", a)])
        S.op("pool", lambda e: e.tensor_copy(out=wbf[b][:, :kch, :], in_=wst[a][:, :kch, :]),
             r=[("wst", a)], w=[("wbf", b)])

    def slab_load(src, col0, kch=8):
        n = slab_n[0]
        slab_n[0] += 1
        assert plan[n][1] == col0 and plan[n][2] == kch and plan[n][0] is src, (n, col0, kch)
        while issued[0] <= min(n + 1, len(plan) - 1):
            _issue(issued[0])
            issued[0] += 1
        return n % 4

    def mm_block(pbank, b, rhs_fn, rkeys, kch=8, n=512):
        for k in range(kch):
            S.op("pe", lambda e: e.matmul(PS[pbank][:, :n], lhsT=wbf[b][:, k, :], rhs=rhs_fn(k),
                                          start=(k == 0), stop=(k == kch - 1)),
                 r=[("wbf", b)] + rkeys, w=[("ps", pbank)], signal=(k == kch - 1))

    bank = [1]

    def nextbank(lo=1, hi=8):
        b = bank[0]
        if not (lo <= b < hi):
            b = lo
        bank[0] = lo + (b + 1 - lo) % (hi - lo)
        return b

    for c in range(4):
        a = c % 2
        S.dma("sp", wst[a][:, :, :], w_in[:, OFF_U + c * 128: OFF_U + (c + 1) * 128].rearrange("(k p) c -> p k c", p=128),
              w=[("wst", a)])
        S.op("pool", lambda e: e.tensor_copy(out=wbf[c][:, :, :], in_=wst[a][:, :, :]), r=[("wst", a)], w=[("wbf", c)])

    def run_p2_tb(tb):
        for c in range(4):
            if True:
                pb = nextbank(2, 8)
                for k in range(8):
                    S.op("pe", lambda e: e.matmul(PS[pb][:, :], lhsT=wbf[c][:, k, :],
                                                  rhs=hT[:, k, HALO + tb * 512: HALO + (tb + 1) * 512],
                                                  start=(k == 0), stop=(k == 7)),
                         r=[("wbf", c)] + ht_keys(tb), w=[("ps", pb)], signal=(k == 7))
                if s5_on and c % 2 == 1:
                    S.op("dve", lambda e: e.tensor_copy(
                        out=uT.rearrange("p c (j b) -> p c j b", j=16)[:, c, :, tb * 32:(tb + 1) * 32],
                        in_=PS[pb][:, :].rearrange("p (b j) -> p j b", j=16)),
                         r=[("ps", pb)], w=[("uT", c, tb)])
                elif s5_on:
                    S.op("act", lambda e: e.activation(
                        out=uT.rearrange("p c (j b) -> p c j b", j=16)[:, c, :, tb * 32:(tb + 1) * 32],
                        in_=PS[pb][:, :].rearrange("p (b j) -> p j b", j=16), func=AF.Copy),
                         r=[("ps", pb)], w=[("uT", c, tb)])
                else:
                    S.op("act", lambda e: e.activation(out=uT[:, c, tb * 512:(tb + 1) * 512], in_=PS[pb][:, :],
                                                       func=AF.Copy),
                         r=[("ps", pb)], w=[("uT", c, tb)])

    def run_p2():
        for tb in range(4):
            run_p2_tb(tb)

    if mode != "F":
        run_p2()


    if s5_on:
        PI2 = float(np.pi / 2)
        A_ = lambda off, shape, dt: carve(off, shape, dt)
        ar = A_(0, [128, 19, 16], F32)
        ai = A_(1216, [128, 19, 16], F32)
        mag = A_(2432, [128, 19, 16], F32)
        ang = A_(3648, [128, 19, 16], F32)
        nn = A_(4864, [128, 19, 16], F32)
        sm = A_(6080, [128, 16, 16], F32)
        LR, LI, DT, LDR, LDI, DEN, ZR, ZI, AM1, T0, T1, T2 = [sm[:, i, :] for i in range(12)]
        Braw = [A_(7168 + i * 1024, [128, 16, 16], F32) for i in range(2)]
        bb = [A_(9216 + i * 1024, [128, 16, 16], F32) for i in range(2)]
        bbBD = [A_(11264 + i * 2048, [128, 16, 32], F32) for i in range(2)]
        pad = [A_(15360 + i * 4096, [128, 16, 128], BF16) for i in range(2)]
        Craw = [A_(23552 + i * 1024, [128, 4, 64], F32) for i in range(2)]
        Cexp = [A_(25600 + i * 512, [128, 128], F32) for i in range(2)]
        CBD = [A_(26624 + i * 2048, [128, 16, 32], F32) for i in range(2)]
        cosT = A_(30720, [128, 16, 128], F32)
        sinT = A_(38912, [128, 16, 128], F32)
        zz = [A_(47104 + i * 8192, [128, 16, 128], F32) for i in range(2)]
        ww = zz
        Eall = A_(63488, [128, 4, 16, 2, 128], BF16)
        tmp = A_(96256, [128, 4, 128], F32)
        tmp2 = A_(98304, [128, 4, 128], F32)
        Sst = [A_(63488 + i * 4224, [128, 16, 132], BF16) for i in range(2)]
        Xt = A_(96256, [128, 16, 2, 128], BF16)
        CAt = A_(47104, [128, 17, 4, 2, 32], BF16)
        lagT = A_(55296 + 1024, [128, 16, 128], BF16)
        CAtmp = [A_(72192 + i * 8704, [128, 17, 4, 32], F32) for i in range(2)]
        CAt_b = [CAt, A_(89600, [128, 17, 4, 2, 32], BF16)]
        lagT_b = [lagT, A_(98304, [128, 16, 128], BF16)]
        ZK0 = [("zz0", c) for c in range(4)]
        ZK1 = [("zz1", c) for c in range(4)]
        uPM = uT.rearrange("p c (j b) -> p c j b", j=16)
        kramp = A_(104448, [128, 19], F32)
        bramp = A_(104448 + 128, [128, NB], F32)
        pmask = sb("pmask", [128, 2], F32)
        selt = sb("selt", [128, 24], F32)
        agbuf = sb("agbuf", [128, 32], F32)
        Gt = sb("Gt", [128, 4, 32], F32)
        Tm = sb("Tm", [128, 3, 32], F32)
        Sin = sb("Sin", [128, 32], F32)

        from collections import deque
        from functools import partial
        bgq = deque()
        tick_n = [0]

        def tick():
            tick_n[0] += 1
            if bgq and tick_n[0] % 4 == 0:
                bgq.popleft()()

        def TT(out, a, b, op, r, w, e="dve"):
            tick()
            S.op(e, lambda g: g.tensor_tensor(out=out, in0=a, in1=b, op=op), r=r, w=w)

        def TS(out, a, s1, s2, op0, op1, r, w, e="dve"):
            if s2 is None:
                S.op(e, lambda g: g.tensor_scalar(out=out, in0=a, scalar1=s1, scalar2=None, op0=op0), r=r, w=w)
            else:
                S.op(e, lambda g: g.tensor_scalar(out=out, in0=a, scalar1=s1, scalar2=s2, op0=op0, op1=op1), r=r, w=w)

        def STT(out, a, sc, b, op0, op1, r, w, e="dve"):
            S.op(e, lambda g: g.scalar_tensor_tensor(out=out, in0=a, scalar=sc, in1=b, op0=op0, op1=op1), r=r, w=w)

        def ACTF(out, a, func, r, w, **kw):
            S.op("act", lambda g: g.activation(out=out, in_=a, func=func, **kw), r=r, w=w)

        def bc(ap, axis, shape):
            return ap.unsqueeze(axis).broadcast_to(shape)

        halfpi = sb("halfpi", [128, 1], F32)
        S.op("dve", lambda g: g.memset(halfpi[:, :], PI2), w=["halfpi"])

        def sincos(a, n, A, ka, kn, kA, bshape=None):
            KN = kn if isinstance(kn, list) else [kn]
            TS(n, a, 1.0 / TWO_PI, MAGIC, ALU.mult, ALU.add, [ka], KN)
            TS(n, n, MAGIC, None, ALU.subtract, None, KN, KN)
            STT(a, n, -C1, a, ALU.mult, ALU.add, KN + [ka], [ka])
            STT(a, n, -C2, a, ALU.mult, ALU.add, KN + [ka], [ka])
            STT(n, a, -1.0, a, ALU.mult, ALU.max, [ka], KN)
            ACTF(A, a, AF.Sin, [ka], [kA], scale=0.5)
            ACTF(a, n, AF.Sin, KN + [ka, "halfpi"], [ka], scale=-0.5, bias=halfpi[:, 0:1])
            STT(a, A, 2.0, a, ALU.mult, ALU.mult, [kA, ka], [ka])
            TT(A, A, A, ALU.mult, [kA], [kA])
            TS(A, A, -2.0, 1.0, ALU.mult, ALU.add, [kA], [kA])

        def range_reduce(a, n, ka, kn):
            TS(n, a, 1.0 / TWO_PI, MAGIC, ALU.mult, ALU.add, [ka], [kn])
            TS(n, n, MAGIC, None, ALU.subtract, None, [kn], [kn])
            STT(a, n, -C1, a, ALU.mult, ALU.add, [kn, ka], [ka])
            STT(a, n, -C2, a, ALU.mult, ALU.add, [kn, ka], [ka])
            TS(n, a, float(np.pi), -TWO_PI, ALU.is_gt, ALU.mult, [ka], [kn])
            TT(a, a, n, ALU.add, [ka, kn], [ka])
            TS(n, a, -float(np.pi), TWO_PI, ALU.is_lt, ALU.mult, [ka], [kn])
            TT(a, a, n, ALU.add, [ka, kn], [ka])
            TS(a, a, 3.1415925, -3.1415925, ALU.min, ALU.max, [ka], [ka])

        for gl in range(2):
            rs = slice(gl * 64, (gl + 1) * 64)
            S.dma("pool", LR[rs, :], lam_re.rearrange("(q gl) p -> gl p q", gl=2)[gl], w=["sm"],
                  allow_slow_non_contiguous=True)
            S.dma("pool", LI[rs, :], lam_im.rearrange("(q gl) p -> gl p q", gl=2)[gl], w=["sm"],
                  allow_slow_non_contiguous=True)
            S.dma("pool", DT[rs, :], bass.AP(log_dt.tensor, gl, [[0, 64], [2, 16]]), w=["sm"],
                  allow_slow_non_contiguous=True)
            S.dma("pool", Braw[0][rs, :, :], b_re.rearrange("(q gl) p h -> gl p q h", gl=2)[gl], w=["Braw"])
            S.dma("pool", Braw[1][rs, :, :], b_im.rearrange("(q gl) p h -> gl p q h", gl=2)[gl], w=["Braw"])
        S.dma("pool", Craw[0][:, :, :], c_re.rearrange("(c g) h p -> (g h) c p", c=4), w=["Craw"])
        S.dma("pool", Craw[1][:, :, :], c_im.rearrange("(c g) h p -> (g h) c p", c=4), w=["Craw"])
        S.dma("pool", dcol[:, :], d_in.rearrange("(c g) h -> (g h) c", c=4), w=["dcol"], allow_slow_non_contiguous=True)
        S.dma("pool", kramp[:], c_kramp, w=["kramp"])
        S.dma("pool", bramp[:], c_bramp, w=["bramp"])
        S.dma("pool", pmask[:], c_pmask, w=["pmask"])
        S.dma("pool", selt[:], c_sel, w=["selt"])

        if FUSED:
            for tt in range(1, 17):
                bgq.append(partial(p1_push, tt, 0))
            bgq.append(p1_flush)
            for tb in range(4):
                bgq.append(partial(run_p2_tb, tb))
                for tt in p1_tb_tiles(1, tb):
                    bgq.append(partial(p1_push, tt, 1))
            bgq.append(p1_flush)
            if True:
                while bgq:
                    bgq.popleft()()

        ACTF(DT, DT, AF.Exp, ["sm"], ["sm"])
        TT(LDR, LR, DT, ALU.mult, ["sm"], ["sm"])
        TT(LDI, LI, DT, ALU.mult, ["sm"], ["sm"])
        sh3 = [128, 19, 16]
        TT(mag, bc(kramp[:, :], 2, sh3), bc(LDR, 1, sh3), ALU.mult, ["kramp", "sm"], ["mag"])
        ACTF(mag, mag, AF.Exp, ["mag"], ["mag"])
        TT(ang, bc(kramp[:, :], 2, sh3), bc(LDI, 1, sh3), ALU.mult, ["kramp", "sm"], ["ang"])
        sincos(ang, nn, ar, "ang", "nn", "ar")
        TT(ar, ar, mag, ALU.mult, ["ar", "mag"], ["ar"])
        TT(ai, ang, mag, ALU.mult, ["ang", "mag"], ["ai"])
        AK = ["ar", "ai"]

        sh4 = [128, 16, NB]
        TT(sinT, bc(LDI, 2, sh4), bc(bramp[:, :], 1, sh4), ALU.mult, ["sm", "bramp"], ["sinT"])
        sincos(sinT, zz[0], cosT, "sinT", ZK0, "cosT")

        TT(DEN, LR, LR, ALU.mult, ["sm"], ["sm"])
        TT(T0, LI, LI, ALU.mult, ["sm"], ["sm"])
        TT(DEN, DEN, T0, ALU.add, ["sm"], ["sm"])
        S.op("dve", lambda g: g.reciprocal(out=DEN, in_=DEN), r=["sm"], w=["sm"])
        TS(AM1, ar[:, 1, :], -1.0, None, ALU.add, None, ["ar", "sm"], ["sm"])
        TT(T0, AM1, LR, ALU.mult, ["sm"], ["sm"])
        TT(T1, ai[:, 1, :], LI, ALU.mult, ["ai", "sm"], ["sm"])
        TT(T0, T0, T1, ALU.add, ["sm"], ["sm"])
        TT(ZR, T0, DEN, ALU.mult, ["sm"], ["sm"])
        TT(T0, ai[:, 1, :], LR, ALU.mult, ["ai", "sm"], ["sm"])
        TT(T1, AM1, LI, ALU.mult, ["sm"], ["sm"])
        TT(T0, T0, T1, ALU.subtract, ["sm"], ["sm"])
        TT(ZI, T0, DEN, ALU.mult, ["sm"], ["sm"])
        shb = [128, 16, 16]
        t_a, t_b = CBD[0][:, :, 0:16], CBD[0][:, :, 16:32]
        TT(t_a, bc(ZR, 2, shb), Braw[0][:, :, :], ALU.mult, ["sm", "Braw"], ["CBD"])
        TT(t_b, bc(ZI, 2, shb), Braw[1][:, :, :], ALU.mult, ["sm", "Braw"], ["CBD"])
        TT(bb[0][:, :, :], t_a, t_b, ALU.subtract, ["CBD"], ["bb"])
        TT(t_a, bc(ZR, 2, shb), Braw[1][:, :, :], ALU.mult, ["sm", "Braw", "bb"], ["CBD"])
        TT(t_b, bc(ZI, 2, shb), Braw[0][:, :, :], ALU.mult, ["sm", "Braw"], ["CBD"])
        TT(bb[1][:, :, :], t_a, t_b, ALU.add, ["CBD"], ["bb"])
        for i in range(2):
            S.op("dve", lambda g: g.memset(bbBD[i][:, :, :], 0.0), w=["bbBD"])
            S.op("pool", lambda g: g.memset(pad[i][:, :, :], 0.0), w=["pad"])
            for gl in range(2):
                rs = slice(gl * 64, (gl + 1) * 64)
                S.op("dve", lambda g: g.tensor_copy(out=bbBD[i][rs, :, gl * 16:(gl + 1) * 16], in_=bb[i][rs, :, :]),
                     r=["bb", "bbBD"], w=["bbBD"])
            for qq in range(4):
                S.op("dve", lambda g: g.tensor_copy(out=pad[i][:, qq::4, 32 * qq:32 * qq + 32], in_=bbBD[i][:, qq::4, :]),
                     r=["bbBD", "pad"], w=["pad"])

        for i in range(2):
            pc = PS[i][:, :].rearrange("p (q c) -> p q c", q=16)
            for c in range(4):
                for gl in range(2):
                    TS(Cexp[c % 2][:, gl * 64:(gl + 1) * 64], Craw[i][:, c, :], pmask[:, gl:gl + 1], None, ALU.mult, None,
                       ["Craw", "pmask", ("Cexp", c % 2)], [("Cexp", c % 2)])
                for qq in range(4):
                    S.op("pe", lambda g: g.matmul(pc[:, 4 * c + qq, :], lhsT=Cexp[c % 2][:, :],
                                                  rhs=identf[:, 32 * qq:32 * qq + 32], start=True, stop=True),
                         r=[("Cexp", c % 2), "identf"], w=[("ps", i)], signal=True)
            S.op("dve", lambda g: g.tensor_copy(out=CBD[i][:, :, :], in_=pc), r=[("ps", i), "CBD"], w=["CBD"])

        def build_E():
            shx = [128, 16, 4, 32]
            Xv = lambda ri: Xt[:, :, ri, :].rearrange("p k (q c) -> p k q c", q=4)
            W0 = zz[0].rearrange("p k (q c) -> p k q c", q=4)
            W1 = zz[1].rearrange("p k (q c) -> p k q c", q=4)
            for c in (0, 1, 2, 3):
                qs = slice(4 * c, 4 * c + 4)
                a_r = bc(ar[:, 0:16, qs], 3, shx)
                a_i = bc(ai[:, 0:16, qs], 3, shx)
                b_r = bc(bbBD[0][:, qs, :], 1, shx)
                b_i = bc(bbBD[1][:, qs, :], 1, shx)
                TT(W0, a_r, b_r, ALU.mult, AK + ["bbBD"], ZK0)
                TT(W1, a_i, b_i, ALU.mult, AK + ["bbBD"], ZK1)
                TT(Xv(0), W0, W1, ALU.subtract, ZK0 + ZK1, ["Xt", "tmp", "tmp2"])
                TT(W0, a_r, b_i, ALU.mult, AK + ["bbBD", "Xt"], ZK0)
                TT(W1, a_i, b_r, ALU.mult, AK + ["bbBD", "Xt"], ZK1)
                TT(Xv(1), W0, W1, ALU.add, ZK0 + ZK1, ["Xt", "tmp", "tmp2"])
                for k4 in range(4):
                    pe_ = PS[k4].bitcast(BF16).rearrange("p (s c) -> p s c", s=8)
                    for kk in range(4):
                        for ri in range(2):
                            k = 4 * k4 + kk
                            S.op("pe", lambda g: g.transpose(out=pe_[:, 2 * kk + ri, :], in_=Xt[:, k, ri, :], identity=identb[:, :]),
                                 r=["Xt", "tmp", "tmp2", "identb"], w=[("ps", k4)], signal=(kk == 3 and ri == 1))
                    ACTF(Eall[:, c, 4 * k4:4 * k4 + 4, :, :].rearrange("p a b c -> p (a b) c"), pe_, AF.Copy,
                         [("ps", k4)], ["Eall"])

        def p3a():
            for c in range(4):
                qs = slice(4 * c, 4 * c + 4)
                pl = [PS[4 + 2 * (c % 2) + ri][:, :].rearrange("p (q b) -> p q b", q=4) for ri in range(2)]
                for qq in range(4):
                    rs = slice(32 * qq, 32 * qq + 32)
                    for ri in range(2):
                        for j in range(16):
                            S.op("pe", lambda g: g.matmul(pl[ri][:, qq, :], lhsT=Eall[rs, c, 15 - j, ri, :], rhs=uPM[rs, c, j, :],
                                                          start=(j == 0), stop=(j == 15), tile_position=(32 * qq, 0)),
                                 r=["Eall", "EallT"] + [("uT", c, tb) for tb in range(4)], w=[("ps", 4 + 2 * (c % 2) + ri)],
                                 signal=(j == 15 and qq == 3))
                kl = [("ps", 4 + 2 * (c % 2)), ("ps", 5 + 2 * (c % 2))]
                cs, sn = cosT[:, qs, :], sinT[:, qs, :]
                TT(tmp, pl[1], sn, ALU.mult, [kl[1], "sinT"], ["tmp"])
                TT(zz[0][:, qs, :], pl[0], cs, ALU.mult, [kl[0], "cosT"], [("zz0", c)])
                TT(zz[0][:, qs, :], zz[0][:, qs, :], tmp, ALU.add, [("zz0", c), "tmp"], [("zz0", c)])
                TT(tmp2, pl[0], sn, ALU.mult, [kl[0], "sinT"], ["tmp2"])
                TT(zz[1][:, qs, :], pl[1], cs, ALU.mult, [kl[1], "cosT"], [("zz1", c)])
                TT(zz[1][:, qs, :], zz[1][:, qs, :], tmp2, ALU.subtract, [("zz1", c), "tmp2"], [("zz1", c)])

        ZK = [ZK0, ZK1]

        def scan_pass(init_fn, kin):
            for q in range(16):
                for ri in range(2):
                    S.op("dve", lambda g: g.tensor_tensor_scan(out=ww[ri][:, q, :],
                                                               data0=mag[:, 16, q:q + 1].broadcast_to([128, NB]),
                                                               data1=zz[ri][:, q, :], initial=init_fn(ri, q),
                                                               op0=ALU.mult, op1=ALU.add),
                         r=["mag", ("zz%d" % ri, q // 4)] + kin, w=[("zz%d" % ri, q // 4)])

        def local_final():
            c127, s127 = cosT[:, :, NB - 1], sinT[:, :, NB - 1]
            wr127, wi127 = ww[0][:, :, NB - 1], ww[1][:, :, NB - 1]
            TT(T0, c127, wr127, ALU.mult, ["cosT", *ZK0, "sm"], ["sm"])
            TT(T1, s127, wi127, ALU.mult, ["sinT", *ZK1, "sm"], ["sm"])
            TT(agbuf[:, 0:16], T0, T1, ALU.subtract, ["sm"], ["agbuf"])
            TT(T0, s127, wr127, ALU.mult, ["sinT", *ZK0, "sm"], ["sm"])
            TT(T1, c127, wi127, ALU.mult, ["cosT", *ZK1, "sm"], ["sm"])
            TT(agbuf[:, 16:32], T0, T1, ALU.add, ["sm", "agbuf"], ["agbuf"])

        def accumulate(idx):
            pr, pi_ = ar[:, idx, :], ai[:, idx, :]
            tr, ti = agbuf[:, 0:16], agbuf[:, 16:32]
            TT(T0, pr, tr, ALU.mult, AK + ["agbuf", "sm"], ["sm"])
            TT(T1, pi_, ti, ALU.mult, AK + ["agbuf", "sm"], ["sm"])
            TT(T0, T0, T1, ALU.subtract, ["sm"], ["sm"])
            TT(Sin[:, 0:16], Sin[:, 0:16], T0, ALU.add, ["sm", "Sin"], ["Sin"])
            TT(T0, pr, ti, ALU.mult, AK + ["agbuf", "sm"], ["sm"])
            TT(T1, pi_, tr, ALU.mult, AK + ["agbuf", "sm"], ["sm"])
            TT(T0, T0, T1, ALU.add, ["sm"], ["sm"])
            TT(Sin[:, 16:32], Sin[:, 16:32], T0, ALU.add, ["sm", "Sin"], ["Sin"])

        build_E()
        if mode == "F":
            S.op("dve", lambda g: g.memset(Sin[:, :], 0.0), w=["Sin"])
            while bgq:
                bgq.popleft()()
            for m_ in range(4):
                if m_ > 0:
                    for tb in range(4):
                        run_p2_tb(tb)
                        if m_ < 3:
                            for tt in p1_tb_tiles(m_ + 1, tb):
                                p1_push(tt, m_ + 1)
                    p1_flush()
                p3a()
                if m_ < 3:
                    scan_pass(lambda ri, q: 0.0, [])
                    local_final()
                    accumulate([0, 17, 18][m_])
        else:
            p3a()
            scan_pass(lambda ri, q: 0.0, [])
            local_final()
            if mode == 'A':
                sloc = nc.dram_tensor("sloc", [128, 32], F32, kind="ExternalOutput").ap()
                S.dma("sp", sloc, agbuf[:, :], r=["agbuf"], w=["sloc"])
                S.finish("sp", ["sloc"])
                return nc, S
            if mode == 'B':
                gin = nc.dram_tensor("gin", [4 * 128, 32], F32, kind="ExternalInput").ap()
                S.dma("pool", ag_out, gin, w=["ag_out"])
            elif NO_CC:
                for jj in range(4):
                    S.dma("pool", ag_out[jj * 128:(jj + 1) * 128, :], ag_in, r=["ag_in"], w=["ag_out"])
            else:
                S.dma("pool", ag_in, agbuf[:, :], r=["agbuf"], w=["ag_in"])
                S.custom_dma("pool", lambda g: g.collective_compute("AllGather", ALU.bypass,
                                                                    replica_groups=[[0, 1, 2, 3], [4, 5, 6, 7]],
                                                                    ins=[ag_in], outs=[ag_out]),
                             r=["ag_in"], w=["ag_out"])
            S.dma("pool", Gt[:, :, :], ag_out.rearrange("(j p) c -> p j c", p=128), r=["ag_out"], w=["Gt"])
            for m in range(3):
                for j in range(4):
                    if j == 0:
                        TS(Tm[:, m, :], Gt[:, 0, :], selt[:, m:m + 1], None, ALU.mult, None, ["Gt", "selt", "Tm"], ["Tm"])
                    else:
                        STT(Tm[:, m, :], Gt[:, j, :], selt[:, 3 * j + m:3 * j + m + 1], Tm[:, m, :], ALU.mult, ALU.add,
                            ["Gt", "selt", "Tm"], ["Tm"])
            S.op("dve", lambda g: g.tensor_copy(out=Sin[:, :], in_=Tm[:, 0, :]), r=["Tm"], w=["Sin"])
            for m in (1, 2):
                pr, pi_ = ar[:, 16 + m, :], ai[:, 16 + m, :]
                tr, ti = Tm[:, m, 0:16], Tm[:, m, 16:32]
                TT(T0, pr, tr, ALU.mult, AK + ["Tm", "sm"], ["sm"])
                TT(T1, pi_, ti, ALU.mult, AK + ["Tm", "sm"], ["sm"])
                TT(T0, T0, T1, ALU.subtract, ["sm"], ["sm"])
                TT(Sin[:, 0:16], Sin[:, 0:16], T0, ALU.add, ["sm", "Sin"], ["Sin"])
                TT(T0, pr, ti, ALU.mult, AK + ["Tm", "sm"], ["sm"])
                TT(T1, pi_, tr, ALU.mult, AK + ["Tm", "sm"], ["sm"])
                TT(T0, T0, T1, ALU.add, ["sm"], ["sm"])
                TT(Sin[:, 16:32], Sin[:, 16:32], T0, ALU.add, ["sm", "Sin"], ["Sin"])
        scan_pass(lambda ri, q: Sin[:, 16 * ri + q:16 * ri + q + 1], ["Sin"])
        for ri in range(2):
            S.op("dve", lambda g: g.tensor_copy(out=Sst[ri][:, :, 0], in_=Sin[:, 16 * ri:16 * ri + 16]),
                 r=["Sin"], w=[("Sst", ri), "Eall"])
        for c in range(4):
            qs = slice(4 * c, 4 * c + 4)
            cs, sn = cosT[:, qs, :], sinT[:, qs, :]
            TT(tmp, cs, ww[0][:, qs, :], ALU.mult, ["cosT", *ZK0], ["tmp"])
            TT(tmp2, sn, ww[1][:, qs, :], ALU.mult, ["sinT", *ZK1], ["tmp2"])
            TT(Sst[0][:, qs, 1:NB + 1], tmp, tmp2, ALU.subtract, ["tmp", "tmp2"], [("Sst", 0), "Eall"])
            TT(tmp, sn, ww[0][:, qs, :], ALU.mult, ["sinT", *ZK0], ["tmp"])
            TT(tmp2, cs, ww[1][:, qs, :], ALU.mult, ["cosT", *ZK1], ["tmp2"])
            TT(Sst[1][:, qs, 1:NB + 1], tmp, tmp2, ALU.add, ["tmp", "tmp2"], [("Sst", 1), "Eall"])

        shc = [128, 17, 4, 32]
        ZZW = [*ZK0, *ZK1, "tmp", "tmp2"] + ZK[0] + ZK[1]
        for c in range(4):
            CAt, lagT = CAt_b[c % 2], lagT_b[c % 2]
            KCA, KLG = ("CAt", c % 2), ("lagT", c % 2)
            XK = ["tmp", "tmp2"] if c % 2 == 1 else []
            qs = slice(4 * c, 4 * c + 4)
            a_r = bc(ar[:, 0:17, qs], 3, shc)
            a_i = bc(ai[:, 0:17, qs], 3, shc)
            c_r = bc(CBD[0][:, qs, :], 1, shc)
            c_i = bc(CBD[1][:, qs, :], 1, shc)
            TT(CAtmp[0], a_r, c_r, ALU.mult, AK + ["CBD"] + ZZW, ["CAtmp0", "EallT"])
            TT(CAtmp[1], a_i, c_i, ALU.mult, AK + ["CBD"] + ZZW, ["CAtmp1", "EallT"])
            TT(CAt[:, :, :, 0, :], CAtmp[0], CAtmp[1], ALU.subtract, ["CAtmp0", "CAtmp1"] + ZZW, [KCA] + XK)
            TT(CAtmp[0], a_r, c_i, ALU.mult, AK + ["CBD", KCA], ["CAtmp0"])
            TT(CAtmp[1], a_i, c_r, ALU.mult, AK + ["CBD", KCA], ["CAtmp1"])
            STT(CAt[:, :, :, 1, :], CAtmp[0], -1.0, CAtmp[1], ALU.mult, ALU.subtract, ["CAtmp0", "CAtmp1"], [KCA])
            for k4 in range(4):
                pg = PS[k4][:, :].rearrange("p (s c) -> p s c", s=4)
                for kk in range(4):
                    k = 4 * k4 + kk
                    for qq in range(4):
                        q = 4 * c + qq
                        S.op("pe", lambda g: g.matmul(pg[:, kk, 32 * qq:32 * qq + 32], lhsT=pad[0][:, q, :],
                                                      rhs=CAt[:, k, qq, 0, :], start=True, stop=False),
                             r=["pad", KCA], w=[("ps", k4)], signal=False)
                        S.op("pe", lambda g: g.matmul(pg[:, kk, 32 * qq:32 * qq + 32], lhsT=pad[1][:, q, :],
                                                      rhs=CAt[:, k, qq, 1, :], start=False, stop=True),
                             r=["pad", KCA], w=[("ps", k4)], signal=(kk == 3 and qq == 3))
                if k4 == 0:
                    STT(pg[:, 0, :], identf[:, :], dcol[:, c:c + 1], pg[:, 0, :], ALU.mult, ALU.add,
                        ["identf", "dcol", ("ps", 0)], [("ps", 0)])
                ACTF(lagT[:, 4 * k4:4 * k4 + 4, :], pg, AF.Copy, [("ps", k4)] + ZZW, [KLG] + XK)
            UK = [("uT", c, tb) for tb in range(4)]
            for j in range(16):
                py = PS[4 + j // 4][:, :].rearrange("p (s b) -> p s b", s=4)[:, j % 4, :]
                kb = ("ps", 4 + j // 4)
                for qq in range(4):
                    q = 4 * c + qq
                    rs = slice(32 * qq, 32 * qq + 32)
                    for ri in range(2):
                        S.op("pe", lambda g: g.matmul(py[rs, :], lhsT=CAt[:, j + 1, qq, ri, :], rhs=Sst[ri][:, q, 0:NB],
                                                      start=(ri == 0), stop=False, tile_position=(0, 32 * qq)),
                             r=[KCA, ("Sst", ri)], w=[kb], signal=False)
                for i in range(j + 1):
                    S.op("pe", lambda g: g.matmul(py[:, :], lhsT=lagT[:, j - i, :], rhs=uPM[:, c, i, :],
                                                  start=False, stop=(i == j)),
                         r=[KLG] + UK, w=[kb], signal=(i == j and j % 4 == 3))
            uv = uT[:, c, :].rearrange("p (b j) -> p j b", j=16)
            for j4 in range(4):
                pyv = PS[4 + j4][:, :].rearrange("p (s b) -> p s b", s=4)
                ACTF(uv[:, 4 * j4:4 * j4 + 4, :], pyv, AF.Copy, [("ps", 4 + j4)], UK)
        S.fence("s5done", ["sm", "ar", "ai", "mag", "ang", "nn", "Braw", "bb", "bbBD", "pad", "Craw", "CBD", "cosT",
                           "sinT", *ZK0, *ZK1, "tmp", "tmp2", "Eall", "Xt", ("CAt", 0), ("CAt", 1), ("lagT", 0), ("lagT", 1), "CAtmp0", "CAtmp1",
                           ("Sst", 0), ("Sst", 1), ("Cexp", 0), ("Cexp", 1), "zz0"] + ZK[0] + ZK[1])

    UT_ALL = [("uT", c, tb) for c in range(4) for tb in range(4)]
    for c in range(4):
        wb = slab_load(w_in, OFF_ZS + c * 128)
        for tb in range(4):
            pb = nextbank()
            mm_block(pb, wb, lambda k: hT[:, k, HALO + tb * 512: HALO + (tb + 1) * 512], ht_keys(tb))
            S.op("act", lambda e: e.activation(out=y2[:, c, tb * 512:(tb + 1) * 512], in_=PS[pb][:, :], func=AF.Silu),
                 r=[("ps", pb)], w=[("y2", c, tb)])
    gi = 0
    GELU_SQ = float(math.sqrt(0.044715 * 1.5957691216))
    for c in range(4):
        for tb in range(4):
            g0 = gt[gi % 2]
            gi += 1
            yv = uT[:, c, tb * 512:(tb + 1) * 512]
            S.op("act", lambda e: e.activation(out=g0[:], in_=yv, func=AF.Square, scale=GELU_SQ),
                 r=[("uT", c, tb)], w=[("gt", id(g0))])
            S.op("dve", lambda e: e.scalar_tensor_tensor(out=g0[:], in0=g0[:], scalar=1.5957691216, in1=yv,
                                                         op0=ALU.add, op1=ALU.mult),
                 r=[("gt", id(g0)), ("uT", c, tb)], w=[("gt", id(g0))])
            S.op("act", lambda e: e.activation(out=g0[:], in_=g0[:], func=AF.Sigmoid),
                 r=[("gt", id(g0))], w=[("gt", id(g0))])
            S.op("dve", lambda e: e.tensor_tensor(out=yv, in0=g0[:], in1=yv, op=ALU.mult),
                 r=[("gt", id(g0)), ("uT", c, tb)], w=[("uT", c, tb)])

    for c in range(4):
        wb = slab_load(w_glu, c * 128, kch=4)
        for tb in range(4):
            pb = nextbank()
            mm_block(pb, wb, lambda k: uT[:, k, tb * 512:(tb + 1) * 512], [("uT", k, tb) for k in range(4)], kch=4)
            s0 = sg[(c * 4 + tb) % 4]
            S.op("act", lambda e: e.activation(out=s0[:], in_=PS[pb][:, :], func=AF.Sigmoid,
                                               bias=bgluT[:, c:c + 1]),
                 r=[("ps", pb), "bgluT"], w=[("sg", id(s0))])
            S.op("dve", lambda e: e.tensor_tensor(out=s0[:], in0=s0[:], in1=uT[:, c, tb * 512:(tb + 1) * 512], op=ALU.mult),
                 r=[("sg", id(s0)), ("uT", c, tb)], w=[("sg", id(s0))])
            S.op("dve", lambda e: e.tensor_tensor(out=y2[:, c, tb * 512:(tb + 1) * 512], in0=s0[:],
                                                  in1=y2[:, c, tb * 512:(tb + 1) * 512], op=ALU.mult),
                 r=[("sg", id(s0)), ("y2", c, tb)], w=[("y2", c, tb)])

    def wout_slab(jc):
        a = jc % 2
        S.dma("sp", wst[a][:, :, :], w_out[:, jc * 128:(jc + 1) * 128].rearrange("(k p) c -> p k c", p=128),
              w=[("wst", a)])
        S.op("pool", lambda e: e.tensor_copy(out=woutb[:, :, jc * 128:(jc + 1) * 128], in_=wst[a][:, :, :]),
             r=[("wst", a)], w=[("woutb", jc)] + UT_ALL)

    cwsb = xt[0]
    S.dma("pool", cwsb[0:31, :], conv_w, w=[("xt", 0)])
    pcw = PS[0][:, 0:256].rearrange("p (c k) -> p c k", c=8)
    for ch in range(8):
        S.op("pe", lambda e: e.matmul(pcw[:, ch, 0:31], lhsT=cwsb[0:31, ch * 128:(ch + 1) * 128],
                                      rhs=identf[0:31, 0:31], start=True, stop=True),
             r=[("xt", 0), "identf"], w=[("ps", 0)], signal=(ch == 7))
    S.op("dve", lambda e: e.tensor_copy(out=cwT[:, :, 0:31], in_=pcw[:, :, 0:31]), r=[("ps", 0)], w=["cwT"])

    def tokblocks():
        return [(0, HALO, [("hT", 0)])] + [(HALO + tb * 512, 512, ht_keys(tb)) for tb in range(4)]

    for i in range(8):
        wa = slab_load(w_in, OFF_CA + i * 128)
        wbb = slab_load(w_in, OFF_CB + i * 128)
        for bi, (c0, n, keys) in enumerate(tokblocks()):
            pa = nextbank()
            pbk = nextbank()
            mm_block(pa, wa, lambda k: hT[:, k, c0:c0 + n], keys, n=n)
            mm_block(pbk, wbb, lambda k: hT[:, k, c0:c0 + n], keys, n=n)
            s0 = sg[(i * 5 + bi) % 4]
            S.op("act", lambda e: e.activation(out=s0[:, :n], in_=PS[pbk][:, :n], func=AF.Sigmoid),
                 r=[("ps", pbk)], w=[("sg", id(s0))])
            S.op("dve", lambda e: e.tensor_tensor(out=cu[:, i, c0:c0 + n], in0=PS[pa][:, :n], in1=s0[:, :n],
                                                  op=ALU.mult),
                 r=[("ps", pa), ("sg", id(s0))], w=[("cu", i, bi)])

    SUMB, SQB = 6, 7
    for i in range(8):
        pass
    dg_built = {}

    dg_cnt = [0]

    def build_dg(i):
        nb_ = dg_cnt[0] % 2
        dg_cnt[0] += 1
        t = dg[nb_]
        S.op("dve", lambda e: e.tensor_tensor(out=t[:, :, :], in0=identf[:, :].unsqueeze(1).broadcast_to([128, 31, 128]),
                                              in1=cwT[:, i, 0:31].unsqueeze(2).broadcast_to([128, 31, 128]), op=ALU.mult),
             r=["identf", "cwT"], w=[("dg", nb_)])
        return t, nb_

    def conv_tb(tb):
        for i in range(8):
            t, nb_ = build_dg(i)
            if tb == 3:
                wout_slab(i)
            pb = nextbank(1, 6)
            base = HALO + tb * 512 - 30
            for k in range(31):
                S.op("pe", lambda e: e.matmul(PS[pb][:, :], lhsT=t[:, k, :], rhs=cu[:, i, base + k: base + k + 512],
                                              start=(k == 0), stop=(k == 30)),
                     r=[("dg", nb_), ("cu", i, tb), ("cu", i, 1 + tb)], w=[("ps", pb)], signal=(k == 30))
            S.op("act", lambda e: e.activation(out=cu[:, i, HALO + tb * 512: HALO + (tb + 1) * 512], in_=PS[pb][:, :],
                                               func=AF.Identity, bias=cbT[:, i:i + 1]),
                 r=[("ps", pb), "cbT"], w=[("cu", i, 1 + tb)])

    CU_KEYS = lambda i: [("cu", i, bi) for bi in range(5)]
    vi = 0

    def zc_slab(i):
        wb = slab_load(w_in, OFF_ZC + i * 128)
        for tb2 in range(4):
            pb = nextbank(1, 6)
            mm_block(pb, wb, lambda k: hT[:, k, HALO + tb2 * 512: HALO + (tb2 + 1) * 512], ht_keys(tb2))
            S.op("act", lambda e: e.activation(out=mrg[:, i, tb2 * 512:(tb2 + 1) * 512], in_=PS[pb][:, :], func=AF.Silu),
                 r=[("ps", pb)], w=[("mrg", i, tb2)])

    for tb in reversed(range(4)):
        conv_tb(tb)
        vs = lambda i: cu[:, i, HALO + tb * 512: HALO + (tb + 1) * 512]
        for i in range(8):
            q = sqb[vi % 2]
            vi += 1
            S.op("dve", lambda e: e.tensor_tensor(out=q[:], in0=vs(i), in1=vs(i), op=ALU.mult),
                 r=[("cu", i, 1 + tb)], w=[("sqb", id(q))])
            S.op("pe", lambda e: e.matmul(PS[SUMB][:, :], lhsT=onesb[:], rhs=vs(i), start=(i == 0), stop=(i == 7)),
                 r=["onesb", ("cu", i, 1 + tb)], w=[("ps", SUMB)], signal=False)
            S.op("pe", lambda e: e.matmul(PS[SQB][:, :], lhsT=onesb[:], rhs=q[:], start=(i == 0), stop=(i == 7)),
                 r=["onesb", ("sqb", id(q))], w=[("ps", SQB)], signal=True)
        zc_slab(2 * (3 - tb))
        zc_slab(2 * (3 - tb) + 1)
        S.op("act", lambda e: e.activation(out=st_mean[:], in_=PS[SUMB][:, :], func=AF.Copy, scale=1.0 / D),
             r=[("ps", SUMB)], w=[K_MEAN])
        S.op("dve", lambda e: e.tensor_tensor(out=st_tmp[:], in0=st_mean[:], in1=st_mean[:], op=ALU.mult),
             r=[K_MEAN], w=[K_TMP])
        S.op("dve", lambda e: e.scalar_tensor_tensor(out=st_tmp[:], in0=PS[SQB][:, :], scalar=1.0 / D, in1=st_tmp[:],
                                                     op0=ALU.mult, op1=ALU.subtract),
             r=[("ps", SQB), K_TMP], w=[K_TMP])
        S.op("act", lambda e: e.activation(out=st_rstd[:], in_=st_tmp[:], func=AF.Sqrt, bias=epsc[:, 1:2]),
             r=[K_TMP, "epsc"], w=[K_RSTD])
        S.op("dve", lambda e: e.reciprocal(out=st_rstd[:], in_=st_rstd[:]), r=[K_RSTD], w=[K_RSTD])
        for i in range(8):
            g0 = gt[i % 2]
            S.op("dve", lambda e: e.tensor_tensor(out=g0[:], in0=vs(i), in1=st_mean[:], op=ALU.subtract),
                 r=[("cu", i, 1 + tb), K_MEAN], w=[("gt", id(g0))])
            S.op("dve", lambda e: e.tensor_tensor(out=g0[:], in0=g0[:], in1=st_rstd[:], op=ALU.mult),
                 r=[("gt", id(g0)), K_RSTD], w=[("gt", id(g0))])
            S.op("act", lambda e: e.activation(out=vs(i), in_=g0[:], func=AF.Silu, scale=lngT[:, i:i + 1],
                                               bias=lnbT[:, i:i + 1]),
                 r=[("gt", id(g0)), "lngT", "lnbT"], w=[("cu", i, 1 + tb)])
    for tb in range(4):
        for i in range(8):
            hsl = slice(HALO + tb * 512, HALO + (tb + 1) * 512)
            S.op("dve", lambda e: e.tensor_tensor(out=cu[:, i, hsl], in0=cu[:, i, hsl], in1=mrg[:, i, tb * 512:(tb + 1) * 512],
                                                  op=ALU.mult),
                 r=[("cu", i, 1 + tb), ("mrg", i, tb)], w=[("cu", i, 1 + tb)])

    for j in range(8):
        wco_b = slab_load(w_co, j * 128)
        wso_b = slab_load(w_so, j * 128, kch=4)
        wgc_b = slab_load(w_in, OFF_GC + j * 128)
        for phase in range(2):
            if phase == 1:
                wgs_b = slab_load(w_in, OFF_GS + j * 128)
            for tb in range(4):
                tsl = slice(tb * 512, (tb + 1) * 512)
                hsl = slice(HALO + tb * 512, HALO + (tb + 1) * 512)
                if phase == 0:
                    pa = nextbank(1, 8)
                    pc = nextbank(1, 8)
                    mm_block(pa, wco_b, lambda k: cu[:, k, hsl], [("cu", k, 1 + tb) for k in range(8)])
                    mm_block(pc, wgc_b, lambda k: hT[:, k, hsl], ht_keys(tb))
                    s0 = sg[tb % 4]
                    S.op("act", lambda e: e.activation(out=s0[:], in_=PS[pc][:, :], func=AF.Sigmoid),
                         r=[("ps", pc)], w=[("sg", id(s0))])
                    S.op("dve", lambda e: e.tensor_tensor(out=mrg[:, j, tsl], in0=PS[pa][:, :], in1=s0[:], op=ALU.mult),
                         r=[("ps", pa), ("sg", id(s0))], w=[("mrg", j, tb)])
                else:
                    pb2 = nextbank(1, 8)
                    pd = nextbank(1, 8)
                    mm_block(pb2, wso_b, lambda k: y2[:, k, tsl], [("y2", k, tb) for k in range(4)], kch=4)
                    mm_block(pd, wgs_b, lambda k: hT[:, k, hsl], ht_keys(tb))
                    s0 = sg[tb % 4]
                    S.op("act", lambda e: e.activation(out=s0[:], in_=PS[pd][:, :], func=AF.Sigmoid),
                         r=[("ps", pd)], w=[("sg", id(s0))])
                    S.op("dve", lambda e: e.tensor_tensor(out=s0[:], in0=PS[pb2][:, :], in1=s0[:], op=ALU.mult),
                         r=[("ps", pb2), ("sg", id(s0))], w=[("sg", id(s0))])
                    S.op("dve", lambda e: e.tensor_tensor(out=mrg[:, j, tsl], in0=s0[:], in1=mrg[:, j, tsl], op=ALU.add),
                         r=[("sg", id(s0)), ("mrg", j, tb)], w=[("mrg", j, tb)])

    S.fence("hTdead", HT_KEYS)
    S.dma("pool", pgB[:], post_g.partition_broadcast(128), w=["pgB"])
    WOUT_KEYS = [("woutb", jc) for jc in range(8)]
    ssq2 = sb("ssq2", [128, 32], F32)
    rstd2 = sb("rstd2", [128, 16], F32)
    S.op("dve", lambda e: e.memset(ssq2[:], 0.0), w=["ssq2"])
    for tt in range(16):
        b = tt % 2
        tb = tt // 4
        xb3 = tt % 3
        S.dma("sp", xt[xb3][:, :], x[HALO + tt * 128: HALO + (tt + 1) * 128, :], w=[("xt", xb3)])
        pbs = [nextbank(1, 8), nextbank(1, 8)]
        for hf in range(2):
            for k in range(8):
                S.op("pe", lambda e: e.matmul(PS[pbs[hf]][:, :], lhsT=mrg[:, k, tt * 128:(tt + 1) * 128],
                                              rhs=woutb[:, k, hf * 512:(hf + 1) * 512], start=(k == 0), stop=(k == 7)),
                     r=[("mrg", k, tb)] + WOUT_KEYS[hf * 4:(hf + 1) * 4], w=[("ps", pbs[hf])], signal=(k == 7))
            S.op("act", lambda e: e.activation(out=junk[:, 0:512], in_=PS[pbs[hf]][:, :], func=AF.Square,
                                               accum_out=ssq2[:, 2 * tt + hf: 2 * tt + hf + 1]),
                 r=[("ps", pbs[hf]), "ssq2"], w=[K_JUNK, ("ssq2", tt, hf)])
        S.op("dve", lambda e: e.tensor_tensor(out=rstd2[:, tt:tt + 1], in0=ssq2[:, 2 * tt:2 * tt + 1],
                                              in1=ssq2[:, 2 * tt + 1:2 * tt + 2], op=ALU.add),
             r=[("ssq2", tt, 0), ("ssq2", tt, 1)], w=[("rstd2", tt)])
        S.op("act", lambda e: e.activation(out=rstd2[:, tt:tt + 1], in_=rstd2[:, tt:tt + 1], func=AF.Sqrt,
                                           scale=1.0 / D, bias=epsc[:, 0:1]),
             r=[("rstd2", tt), "epsc"], w=[("rstd2", tt)])
        S.op("dve", lambda e: e.reciprocal(out=rstd2[:, tt:tt + 1], in_=rstd2[:, tt:tt + 1]),
             r=[("rstd2", tt)], w=[("rstd2", tt)])
        for hf in range(2):
            hs = slice(hf * 512, (hf + 1) * 512)
            S.op("dve", lambda e: e.scalar_tensor_tensor(out=ot[b][:, hs], in0=PS[pbs[hf]][:, :],
                                                         scalar=rstd2[:, tt:tt + 1], in1=pgB[:, hs],
                                                         op0=ALU.mult, op1=ALU.mult),
                 r=[("ps", pbs[hf]), ("rstd2", tt), "pgB"], w=[("ot", b, hf)])
            S.op("pool", lambda e: e.tensor_tensor(out=ot[b][:, hs], in0=ot[b][:, hs], in1=xt[xb3][:, hs], op=ALU.add),
                 r=[("ot", b, hf), ("xt", xb3)], w=[("ot", b, hf)])
        S.dma("pool", out_d[tt * 128:(tt + 1) * 128, :], ot[b][:, :], r=[("ot", b, 0), ("ot", b, 1)], w=[("outd", tt)])
    dbg_aps = {"hT": hT, "uT": uT, "cu": cu, "mrg": mrg, "y2": y2}
    dbg_keys = {"hT": HT_KEYS, "uT": UT_ALL, "cu": [("cu", i, b) for i in range(8) for b in range(5)],
                "mrg": [("mrg", j, tb) for j in range(8) for tb in range(4)],
                "y2": [("y2", c, tb) for c in range(4) for tb in range(4)]}
    fin = [("outd", tt) for tt in range(16)]
    for name in debug:
        ap = dbg_aps[name]
        dd = nc.dram_tensor("dbg_" + name, list(ap.shape), ap.dtype, kind="ExternalOutput").ap()
        S.dma("sp", dd, ap, r=dbg_keys[name], w=[("dbgout", name)])
        fin.append(("dbgout", name))
    S.finish("sp", fin)
    return nc, S


_CONST = {}


def _consts():
    if not _CONST:
        _CONST["c_ident"] = np.eye(128, dtype=np.float32)
        kr = np.concatenate([np.arange(NEXP), [2048, 4096]]).astype(np.float32)
        _CONST["c_kramp"] = np.ascontiguousarray(np.broadcast_to(kr, (128, NEXP + 2)))
        br = (R0 * (np.arange(NB) + 1)).astype(np.float32)
        _CONST["c_bramp"] = np.ascontiguousarray(np.broadcast_to(br, (128, NB)))
        pm = np.zeros((128, 2), np.float32)
        par = (np.arange(128) // 16) % 2
        pm[par == 0, 0] = 1.0
        pm[par == 1, 1] = 1.0
        _CONST["c_pmask"] = pm
    return _CONST


def make_in_maps(inputs, fused=False):
    f = lambda a: np.ascontiguousarray(np.asarray(a, dtype=np.float32))
    x = f(inputs["x"])
    shared = {
        "w_in": f(inputs["w_in"][0]), "pre_g": f(inputs["pre_norm_gain"][0]), "conv_w": f(inputs["conv_w"][0]),
        "conv_b": f(inputs["conv_b"][0]), "ln_g": f(inputs["conv_ln_gain"][0]), "ln_b": f(inputs["conv_ln_bias"][0]),
        "w_co": f(inputs["w_conv_out"][0]), "lam_re": f(inputs["ssm_lambda_re"][0]),
        "lam_im": f(inputs["ssm_lambda_im"][0]), "log_dt": f(inputs["ssm_log_dt"][0]),
        "b_re": f(inputs["ssm_b_re"][0]), "b_im": f(inputs["ssm_b_im"][0]), "c_re": f(inputs["ssm_c_re"][0]),
        "c_im": f(inputs["ssm_c_im"][0]), "d_in": f(inputs["ssm_d"][0]), "w_glu": f(inputs["w_ssm_glu"][0]),
        "b_glu": f(inputs["b_ssm_glu"][0]), "w_so": f(inputs["w_ssm_out"][0]), "w_out": f(inputs["w_out"][0]),
        "post_g": f(inputs["post_norm_gain"][0]),
    }
    shared.update(_consts())
    maps = []
    for c in range(NCORES):
        bi, ci = c // 4, c % 4
        xc = np.zeros((NT + 3 * TOK if fused else NT, D), np.float32)
        t0 = ci * TOK
        xc[HALO:NT] = x[bi, t0:t0 + TOK]
        if fused:
            for m_ in range(3):
                cj = ci - 1 - m_
                if cj >= 0:
                    xc[NT + m_ * TOK: NT + (m_ + 1) * TOK] = x[bi, cj * TOK:(cj + 1) * TOK]
        if ci > 0:
            xc[:HALO] = x[bi, t0 - HALO:t0]
        sel = np.zeros((8, 3), np.float32)
        for j in range(4):
            if j < ci:
                sel[j, ci - 1 - j] = 1.0
        m = dict(shared)
        m["x"] = xc
        m["c_sel"] = np.ascontiguousarray(np.broadcast_to(sel.reshape(1, 24), (128, 24)))
        maps.append(m)
    return maps


_PROG = {}


def kernel(**inputs):
    if "ncF" not in _PROG:
        _PROG["ncF"] = build_program(mode='F')[0]
    maps = make_in_maps(inputs, fused=True)
    res = run_bass_kernel_spmd(_PROG["ncF"], maps, core_ids=list(range(NCORES)))
    out = np.empty((2, 8192, D), np.float32)
    for c in range(NCORES):
        out[c // 4, (c % 4) * TOK:(c % 4 + 1) * TOK] = res.results[c]["out"]
    return out
```

```python
import math
import numpy as np
import concourse.bass as bass
import concourse.mybir as mybir
from concourse.bass_utils import run_bass_kernel_spmd

F32 = mybir.dt.float32
BF16 = mybir.dt.bfloat16
AF = mybir.ActivationFunctionType
ALU = mybir.AluOpType

NCORES = 8
import os
NO_CC = bool(os.environ.get('NO_CC'))
D = 1024
TOK = 2048
HALO = 32
NT = TOK + HALO
INW = 6144
R0 = 16
NB = TOK // R0
NEXP = R0 + 1
TWO_PI = float(2.0 * np.pi)
MAGIC = 12582912.0
C1 = 6.28125
C2 = float(2.0 * np.pi - 6.28125)

OFF_CA, OFF_CB, OFF_ZC, OFF_U, OFF_ZS, OFF_GC, OFF_GS = 0, 1024, 2048, 3072, 3584, 4096, 5120


class Sched:
    def __init__(self, nc, n_dma_sems=24):
        self.nc = nc
        self.engs = {"pe": nc.tensor, "act": nc.scalar, "dve": nc.vector, "pool": nc.gpsimd, "sp": nc.sync}
        self.sem = {k: nc.alloc_semaphore("prog_" + k) for k in self.engs}
        self.cnt = {k: 0 for k in self.engs}
        self.waited = {k: {} for k in self.engs}
        self.bufs = {}
        self.semobj = {("E", k): self.sem[k] for k in self.engs}
        self.dpool = {}
        for q, n in (("sp", n_dma_sems), ("pool", 16)):
            sems = [nc.alloc_semaphore("dma_%s%d" % (q, i)) for i in range(n)]
            self.dpool[q] = {"sems": sems, "n": 0}
            for i, sm_ in enumerate(sems):
                self.semobj[("D", q, i)] = sm_
        self.dn = 0
        self.nwaits = 0
        self.fences = []
        self.fence_scratch = nc.alloc_sbuf_tensor("fence_scr", [128, 8], F32).ap()

    def _need(self, e, deps):
        best = {}
        for d in deps:
            if d is None:
                continue
            k, v = d
            if e == "pe" and k == ("E", "pe"):
                continue
            if v > best.get(k, 0):
                best[k] = v
        for k, v in best.items():
            if v > self.waited[e].get(k, 0):
                self.engs[e].wait_ge(self.semobj[k], v)
                self.waited[e][k] = v
                self.nwaits += 1

    def _deps(self, r, w):
        deps = []
        for k in r:
            b = self.bufs.get(k)
            if b is not None:
                deps.append(b["w"])
        for k in w:
            b = self.bufs.get(k)
            if b is not None:
                deps.append(b["w"])
                deps.extend(b["r"].items())
        return deps

    def _mark(self, tag, r, w):
        for k in r:
            b = self.bufs.setdefault(k, {"w": None, "r": {}})
            if tag[1] > b["r"].get(tag[0], 0):
                b["r"][tag[0]] = tag[1]
        for k in w:
            self.bufs[k] = {"w": tag, "r": {}}

    def fence(self, name, old_keys):
        self.op("dve", lambda e: e.memset(self.fence_scratch, 0.0), w=list(old_keys) + [("fence", name)])
        self.fences.append(("fence", name))

    def op(self, e, fn, r=(), w=(), signal=True):
        r = list(r) + self.fences
        self._need(e, self._deps(r, w))
        ins = fn(self.engs[e])
        if signal:
            self.cnt[e] += 1
            ins.then_inc(self.sem[e], 1)
            idx = self.cnt[e]
        else:
            assert e == "pe"
            idx = self.cnt[e] + 1
        self._mark((("E", e), idx), r, w)
        return ins

    def dma(self, e, out, in_, r=(), w=(), **kw):
        self.custom_dma(e, lambda g: g.dma_start(out=out, in_=in_, **kw), r=r, w=w)

    def custom_dma(self, e, fn, r=(), w=()):
        P = self.dpool[e]
        ns = len(P["sems"])
        s = P["n"] % ns
        val = 16 * (P["n"] // ns + 1)
        r = list(r) + self.fences
        deps = self._deps(r, w)
        if P["n"] >= ns:
            deps.append((("D", e, s), val - 16))
        self._need(e, deps)
        fn(self.engs[e]).then_inc(P["sems"][s], 16)
        P["n"] += 1
        self.dn += 1
        self._mark((("D", e, s), val), r, w)

    def finish(self, e, keys):
        self._need(e, self._deps(keys, ()))


def build_program(s5_on=True, debug=(), mode='B'):
    nc = bass.Bass("TRN2", target_bir_lowering=False)
    S = Sched(nc)

    def din(name, shape):
        return nc.dram_tensor(name, list(shape), F32, kind="ExternalInput").ap()

    x = din("x", [NT + 3 * TOK, D] if mode == "F" else [NT, D])
    w_in = din("w_in", [D, INW])
    pre_g = din("pre_g", [D])
    conv_w = din("conv_w", [31, D])
    conv_b = din("conv_b", [D])
    ln_g = din("ln_g", [D])
    ln_b = din("ln_b", [D])
    w_co = din("w_co", [D, D])
    lam_re = din("lam_re", [32, 64])
    lam_im = din("lam_im", [32, 64])
    log_dt = din("log_dt", [32])
    b_re = din("b_re", [32, 64, 16])
    b_im = din("b_im", [32, 64, 16])
    c_re = din("c_re", [32, 16, 64])
    c_im = din("c_im", [32, 16, 64])
    d_in = din("d_in", [32, 16])
    w_glu = din("w_glu", [512, 512])
    b_glu = din("b_glu", [512])
    w_so = din("w_so", [512, D])
    w_out = din("w_out", [D, D])
    post_g = din("post_g", [D])
    c_ident = din("c_ident", [128, 128])
    c_kramp = din("c_kramp", [128, NEXP + 2])
    c_bramp = din("c_bramp", [128, NB])
    c_pmask = din("c_pmask", [128, 2])
    c_sel = din("c_sel", [128, 24])
    out_d = nc.dram_tensor("out", [TOK, D], F32, kind="ExternalOutput").ap()
    ag_in = nc.dram_tensor("ag_in", [128, 32], F32).ap()
    ag_out = nc.dram_tensor("ag_out", [4 * 128, 32], F32).ap()

    dbg_out = {}

    def sb(name, shape, dt):
        return nc.alloc_sbuf_tensor(name, list(shape), dt).ap()

    def ps(name, shape, dt=F32):
        return nc.alloc_psum_tensor(name, list(shape), dt).ap()

    hTraw = sb("hT", [128, 8 * NT], BF16)
    hT = hTraw.rearrange("p (k t) -> p k t", k=8)
    uT = sb("uT", [128, 4, TOK], BF16)
    BIGB = 104 * 1024
    big = sb("big", [128, BIGB // 2], BF16)

    def carve(off, shape, dt, base=None):
        base = big if base is None else base
        n = int(np.prod(shape[1:]))
        esz = 4 if dt == F32 else 2
        assert off % 4 == 0
        v = base[:, off // 2: off // 2 + n * esz // 2]
        if dt == F32:
            v = v.bitcast(F32)
        if len(shape) == 3:
            v = v.rearrange("p (a b) -> p a b", a=shape[1])
        elif len(shape) == 4:
            v = v.rearrange("p (a b c) -> p a b c", a=shape[1], b=shape[2])
        elif len(shape) == 5:
            v = v.rearrange("p (a b c d) -> p a b c d", a=shape[1], b=shape[2], c=shape[3])
        return v

    y2 = carve(0, [128, 4, TOK], BF16)
    cu = carve(16384, [128, 8, NT], BF16)
    dg = [carve(16384 + 33280 + i * 7936, [128, 31, 128], BF16) for i in range(2)]
    mrg = carve(16384 + 33280 + 15872, [128, 8, TOK], BF16)
    assert 16384 + 33280 + 15872 + 32768 <= BIGB
    woutb = uT.rearrange("p a b -> p (a b)").rearrange("p (k c) -> p k c", k=8)
    pgB = carve(16384, [128, D], F32, base=hTraw)
    ot = [carve(20480 + i * 4096, [128, D], F32, base=hTraw) for i in range(2)]
    wst = [sb("wst%d" % i, [128, 8, 128], F32) for i in range(2)]
    wbf = [sb("wbf%d" % i, [128, 8, 128], BF16) for i in range(4)]
    xt = [sb("xt%d" % i, [128, D], F32) for i in range(3)]
    xs = [sb("xs%d" % i, [128, D], BF16) for i in range(2)]
    gt = [sb("gt%d" % i, [128, 512], F32) for i in range(2)]
    junk = gt[1].bitcast(BF16)
    K_JUNK = ("gt", id(gt[1]))
    sg = [sb("sg%d" % i, [128, 512], F32) for i in range(4)]
    identf = sb("identf", [128, 128], F32)
    identb = sb("identb", [128, 128], BF16)
    onesb = sb("onesb", [128, 128], BF16)
    gT = sb("gT", [128, 8], F32)
    cbT = sb("cbT", [128, 8], F32)
    lngT = sb("lngT", [128, 8], F32)
    lnbT = sb("lnbT", [128, 8], F32)
    bgluT = sb("bgluT", [128, 4], F32)
    dcol = sb("dcol", [128, 4], F32)
    ssq = sb("ssq", [128, 80], F32)
    rstd = sb("rstd", [128, 80], F32)
    cwT = sb("cwT", [128, 8, 32], F32)
    st_mean, st_rstd, st_tmp = sg[0], sg[1], sg[2]
    K_MEAN, K_RSTD, K_TMP = ("sg", id(sg[0])), ("sg", id(sg[1])), ("sg", id(sg[2]))
    sqb = [sb("sqb%d" % i, [128, 512], BF16) for i in range(2)]

    PS = [ps("ps%d" % i, [128, 512]) for i in range(8)]

    E = S.engs
    for i in range(8):
        pass

    S.dma("pool", identf[:], c_ident, w=["identf"])
    S.op("dve", lambda e: e.tensor_copy(out=identb[:], in_=identf[:]), r=["identf"], w=["identb"])
    S.op("dve", lambda e: e.memset(onesb[:], 1.0), w=["onesb"])
    S.op("dve", lambda e: e.memset(ssq[:], 0.0), w=["ssq"])

    def load_cols(dst, src, n, key):
        S.dma("pool", dst[:, 0:n], src.rearrange("(c p) -> p c", p=128), w=[key], allow_slow_non_contiguous=True)

    load_cols(gT, pre_g, 8, "gT")
    load_cols(cbT, conv_b, 8, "cbT")
    load_cols(lngT, ln_g, 8, "lngT")
    load_cols(lnbT, ln_b, 8, "lnbT")
    load_cols(bgluT, b_glu, 4, "bgluT")

    pst = PS[0].bitcast(BF16)

    p1_cnt = [0]
    p1_pend = []
    XB = xt + [w_.rearrange("p a b -> p (a b)") for w_ in wst]
    KX = [("xt", 0), ("xt", 1), ("xt", 2), ("wst", 0), ("wst", 1)]

    def p1_front(tt_, passno):
        rows = HALO if tt_ == 0 else 128
        r0 = 0 if tt_ == 0 else HALO + (tt_ - 1) * 128
        xr = r0 if passno == 3 else NT + passno * TOK + (tt_ - 1) * 128
        g = p1_cnt[0]
        p1_cnt[0] += 1
        b = g % 5
        tt = passno * 17 + tt_
        S.dma("sp", XB[b][:rows, :], x[xr:xr + rows, :], w=[KX[b]])
        S.op("act", lambda e: e.activation(out=junk[:rows, :], in_=XB[b][:rows, :], func=AF.Square,
                                           accum_out=ssq[:rows, tt:tt + 1]),
             r=[KX[b], "ssq"], w=[K_JUNK, ("ssq", tt)])
        S.op("act", lambda e: e.activation(out=rstd[:rows, tt:tt + 1], in_=ssq[:rows, tt:tt + 1], func=AF.Sqrt,
                                           scale=1.0 / D, bias=epsc[:rows, 0:1]),
             r=[("ssq", tt), "epsc"], w=[("rstd", tt)])
        S.op("dve", lambda e: e.reciprocal(out=rstd[:rows, tt:tt + 1], in_=rstd[:rows, tt:tt + 1]),
             r=[("rstd", tt)], w=[("rstd", tt)])
        return (tt_, passno, g)

    def p1_back(tt_, passno, g):
        rows = HALO if tt_ == 0 else 128
        r0 = 0 if tt_ == 0 else HALO + (tt_ - 1) * 128
        b = g % 5
        pbk = g % 2
        pst = PS[pbk].bitcast(BF16)
        tt = passno * 17 + tt_
        if g % 2 == 1:
            S.op("dve", lambda e: e.tensor_scalar(out=xs[pbk][:rows, :], in0=XB[b][:rows, :],
                                                  scalar1=rstd[:rows, tt:tt + 1], scalar2=None, op0=ALU.mult),
                 r=[KX[b], ("rstd", tt)], w=[("xs", pbk)])
        else:
            S.op("act", lambda e: e.activation(out=xs[pbk][:rows, :], in_=XB[b][:rows, :], func=AF.Copy,
                                               scale=rstd[:rows, tt:tt + 1]),
                 r=[KX[b], ("rstd", tt)], w=[("xs", pbk)])
        pv = pst.rearrange("p (k t) -> p k t", k=8)
        for kc in range(8):
            S.op("pe", lambda e: e.transpose(out=pv[:, kc, :rows], in_=xs[pbk][:rows, kc * 128:(kc + 1) * 128],
                                             identity=identb[:rows, :rows]),
                 r=[("xs", pbk), "identb"], w=[("ps", pbk)], signal=(kc == 7))
        S.op("dve", lambda e: e.tensor_tensor(out=hT[:, :, r0:r0 + rows], in0=pv[:, :, :rows],
                                              in1=gT[:, :].unsqueeze(2).broadcast_to([128, 8, rows]), op=ALU.mult),
             r=[("ps", pbk), "gT"], w=[("hT", tt_)])

    def p1_push(tt_, passno):
        st = p1_front(tt_, passno)
        if os.environ.get("NO_STAG"):
            p1_back(*st)
            return
        if p1_pend:
            p1_back(*p1_pend.pop())
        p1_pend.append(st)

    def p1_flush():
        if p1_pend:
            p1_back(*p1_pend.pop())

    epsc = sb("epsc", [128, 2], F32)
    S.op("dve", lambda e: e.memset(epsc[:, 0:1], 1e-6), w=["epsc"])
    S.op("dve", lambda e: e.memset(epsc[:, 1:2], 1e-5), r=["epsc"], w=["epsc"])

    def run_p1(passno):
        for tt in range(0 if passno == 3 else 1, 17):
            p1_push(tt, passno)
        p1_flush()

    def p1_tb_tiles(passno, tb):
        return ([0] if (passno == 3 and tb == 0) else []) + list(range(1 + 4 * tb, 5 + 4 * tb))

    FUSED = (mode == "F" and s5_on)
    if not FUSED:
        run_p1(3)

    HT_KEYS = [("hT", tt) for tt in range(17)]

    def ht_keys(tb):
        return [("hT", 1 + tb * 4 + i) for i in range(4)]

    slab_n = [0]

    plan = []
    plan += [(w_in, OFF_ZS + c * 128, 8) for c in range(4)]
    plan += [(w_glu, c * 128, 4) for c in range(4)]
    for i in range(8):
        plan += [(w_in, OFF_CA + i * 128, 8), (w_in, OFF_CB + i * 128, 8)]
    plan += [(w_in, OFF_ZC + i * 128, 8) for i in range(8)]
    for j in range(8):
        plan += [(w_co, j * 128, 8), (w_so, j * 128, 4), (w_in, OFF_GC + j * 128, 8), (w_in, OFF_GS + j * 128, 8)]
    issued = [0]

    def _issue(n):
        src, col0, kch = plan[n]
        a, b = n % 2, n % 4
        S.dma("sp", wst[a][:, :kch, :], src[:, col0:col0 + 128].rearrange("(k p) c -> p k c", p=128),
              w=[("wst", a)])
        S.op("pool", lambda e: e.tensor_copy(out=wbf[b][:, :kch, :], in_=wst[a][:, :kch, :]),
             r=[("wst", a)], w=[("wbf", b)])

    def slab_load(src, col0, kch=8):
        n = slab_n[0]
        slab_n[0] += 1
        assert plan[n][1] == col0 and plan[n][2] == kch and plan[n][0] is src, (n, col0, kch)
        while issued[0] <= min(n + 1, len(plan) - 1):
            _issue(issued[0])
            issued[0] += 1
        return n % 4

    def mm_block(pbank, b, rhs_fn, rkeys, kch=8, n=512):
        for k in range(kch):
            S.op("pe", lambda e: e.matmul(PS[pbank][:, :n], lhsT=wbf[b][:, k, :], rhs=rhs_fn(k),
                                          start=(k == 0), stop=(k == kch - 1)),
                 r=[("wbf", b)] + rkeys, w=[("ps", pbank)], signal=(k == kch - 1))

    bank = [1]

    def nextbank(lo=1, hi=8):
        b = bank[0]
        if not (lo <= b < hi):
            b = lo
        bank[0] = lo + (b + 1 - lo) % (hi - lo)
        return b

    for c in range(4):
        a = c % 2
        S.dma("sp", wst[a][:, :, :], w_in[:, OFF_U + c * 128: OFF_U + (c + 1) * 128].rearrange("(k p) c -> p k c", p=128),
              w=[("wst", a)])
        S.op("pool", lambda e: e.tensor_copy(out=wbf[c][:, :, :], in_=wst[a][:, :, :]), r=[("wst", a)], w=[("wbf", c)])

    def run_p2_tb(tb):
        for c in range(4):
            if True:
                pb = nextbank(2, 8)
                for k in range(8):
                    S.op("pe", lambda e: e.matmul(PS[pb][:, :], lhsT=wbf[c][:, k, :],
                                                  rhs=hT[:, k, HALO + tb * 512: HALO + (tb + 1) * 512],
                                                  start=(k == 0), stop=(k == 7)),
                         r=[("wbf", c)] + ht_keys(tb), w=[("ps", pb)], signal=(k == 7))
                if s5_on and c % 2 == 1:
                    S.op("dve", lambda e: e.tensor_copy(
                        out=uT.rearrange("p c (j b) -> p c j b", j=16)[:, c, :, tb * 32:(tb + 1) * 32],
                        in_=PS[pb][:, :].rearrange("p (b j) -> p j b", j=16)),
                         r=[("ps", pb)], w=[("uT", c, tb)])
                elif s5_on:
                    S.op("act", lambda e: e.activation(
                        out=uT.rearrange("p c (j b) -> p c j b", j=16)[:, c, :, tb * 32:(tb + 1) * 32],
                        in_=PS[pb][:, :].rearrange("p (b j) -> p j b", j=16), func=AF.Copy),
                         r=[("ps", pb)], w=[("uT", c, tb)])
                else:
                    S.op("act", lambda e: e.activation(out=uT[:, c, tb * 512:(tb + 1) * 512], in_=PS[pb][:, :],
                                                       func=AF.Copy),
                         r=[("ps", pb)], w=[("uT", c, tb)])

    def run_p2():
        for tb in range(4):
            run_p2_tb(tb)

    if mode != "F":
        run_p2()


    if s5_on:
        PI2 = float(np.pi / 2)
        A_ = lambda off, shape, dt: carve(off, shape, dt)
        ar = A_(0, [128, 19, 16], F32)
        ai = A_(1216, [128, 19, 16], F32)
        mag = A_(2432, [128, 19, 16], F32)
        ang = A_(3648, [128, 19, 16], F32)
        nn = A_(4864, [128, 19, 16], F32)
        sm = A_(6080, [128, 16, 16], F32)
        LR, LI, DT, LDR, LDI, DEN, ZR, ZI, AM1, T0, T1, T2 = [sm[:, i, :] for i in range(12)]
        Braw = [A_(7168 + i * 1024, [128, 16, 16], F32) for i in range(2)]
        bb = [A_(9216 + i * 1024, [128, 16, 16], F32) for i in range(2)]
        bbBD = [A_(11264 + i * 2048, [128, 16, 32], F32) for i in range(2)]
        pad = [A_(15360 + i * 4096, [128, 16, 128], BF16) for i in range(2)]
        Craw = [A_(23552 + i * 1024, [128, 4, 64], F32) for i in range(2)]
        Cexp = [A_(25600 + i * 512, [128, 128], F32) for i in range(2)]
        CBD = [A_(26624 + i * 2048, [128, 16, 32], F32) for i in range(2)]
        cosT = A_(30720, [128, 16, 128], F32)
        sinT = A_(38912, [128, 16, 128], F32)
        zz = [A_(47104 + i * 8192, [128, 16, 128], F32) for i in range(2)]
        ww = zz
        Eall = A_(63488, [128, 4, 16, 2, 128], BF16)
        tmp = A_(96256, [128, 4, 128], F32)
        tmp2 = A_(98304, [128, 4, 128], F32)
        Sst = [A_(63488 + i * 4224, [128, 16, 132], BF16) for i in range(2)]
        Xt = A_(96256, [128, 16, 2, 128], BF16)
        CAt = A_(47104, [128, 17, 4, 2, 32], BF16)
        lagT = A_(55296 + 1024, [128, 16, 128], BF16)
        CAtmp = [A_(72192 + i * 8704, [128, 17, 4, 32], F32) for i in range(2)]
        CAt_b = [CAt, A_(89600, [128, 17, 4, 2, 32], BF16)]
        lagT_b = [lagT, A_(98304, [128, 16, 128], BF16)]
        ZK0 = [("zz0", c) for c in range(4)]
        ZK1 = [("zz1", c) for c in range(4)]
        uPM = uT.rearrange("p c (j b) -> p c j b", j=16)
        kramp = A_(104448, [128, 19], F32)
        bramp = A_(104448 + 128, [128, NB], F32)
        pmask = sb("pmask", [128, 2], F32)
        selt = sb("selt", [128, 24], F32)
        agbuf = sb("agbuf", [128, 32], F32)
        Gt = sb("Gt", [128, 4, 32], F32)
        Tm = sb("Tm", [128, 3, 32], F32)
        Sin = sb("Sin", [128, 32], F32)

        from collections import deque
        from functools import partial
        bgq = deque()
        tick_n = [0]

        def tick():
            tick_n[0] += 1
            if bgq and tick_n[0] % 4 == 0:
                bgq.popleft()()

        def TT(out, a, b, op, r, w, e="dve"):
            tick()
            S.op(e, lambda g: g.tensor_tensor(out=out, in0=a, in1=b, op=op), r=r, w=w)

        def TS(out, a, s1, s2, op0, op1, r, w, e="dve"):
            if s2 is None:
                S.op(e, lambda g: g.tensor_scalar(out=out, in0=a, scalar1=s1, scalar2=None, op0=op0), r=r, w=w)
            else:
                S.op(e, lambda g: g.tensor_scalar(out=out, in0=a, scalar1=s1, scalar2=s2, op0=op0, op1=op1), r=r, w=w)

        def STT(out, a, sc, b, op0, op1, r, w, e="dve"):
            S.op(e, lambda g: g.scalar_tensor_tensor(out=out, in0=a, scalar=sc, in1=b, op0=op0, op1=op1), r=r, w=w)

        def ACTF(out, a, func, r, w, **kw):
            S.op("act", lambda g: g.activation(out=out, in_=a, func=func, **kw), r=r, w=w)

        def bc(ap, axis, shape):
            return ap.unsqueeze(axis).broadcast_to(shape)

        halfpi = sb("halfpi", [128, 1], F32)
        S.op("dve", lambda g: g.memset(halfpi[:, :], PI2), w=["halfpi"])

        def sincos(a, n, A, ka, kn, kA, bshape=None):
            KN = kn if isinstance(kn, list) else [kn]
            TS(n, a, 1.0 / TWO_PI, MAGIC, ALU.mult, ALU.add, [ka], KN)
            TS(n, n, MAGIC, None, ALU.subtract, None, KN, KN)
            STT(a, n, -C1, a, ALU.mult, ALU.add, KN + [ka], [ka])
            STT(a, n, -C2, a, ALU.mult, ALU.add, KN + [ka], [ka])
            STT(n, a, -1.0, a, ALU.mult, ALU.max, [ka], KN)
            ACTF(A, a, AF.Sin, [ka], [kA], scale=0.5)
            ACTF(a, n, AF.Sin, KN + [ka, "halfpi"], [ka], scale=-0.5, bias=halfpi[:, 0:1])
            STT(a, A, 2.0, a, ALU.mult, ALU.mult, [kA, ka], [ka])
            TT(A, A, A, ALU.mult, [kA], [kA])
            TS(A, A, -2.0, 1.0, ALU.mult, ALU.add, [kA], [kA])

        def range_reduce(a, n, ka, kn):
            TS(n, a, 1.0 / TWO_PI, MAGIC, ALU.mult, ALU.add, [ka], [kn])
            TS(n, n, MAGIC, None, ALU.subtract, None, [kn], [kn])
            STT(a, n, -C1, a, ALU.mult, ALU.add, [kn, ka], [ka])
            STT(a, n, -C2, a, ALU.mult, ALU.add, [kn, ka], [ka])
            TS(n, a, float(np.pi), -TWO_PI, ALU.is_gt, ALU.mult, [ka], [kn])
            TT(a, a, n, ALU.add, [ka, kn], [ka])
            TS(n, a, -float(np.pi), TWO_PI, ALU.is_lt, ALU.mult, [ka], [kn])
            TT(a, a, n, ALU.add, [ka, kn], [ka])
            TS(a, a, 3.1415925, -3.1415925, ALU.min, ALU.max, [ka], [ka])

        for gl in range(2):
            rs = slice(gl * 64, (gl + 1) * 64)
            S.dma("pool", LR[rs, :], lam_re.rearrange("(q gl) p -> gl p q", gl=2)[gl], w=["sm"],
                  allow_slow_non_contiguous=True)
            S.dma("pool", LI[rs, :], lam_im.rearrange("(q gl) p -> gl p q", gl=2)[gl], w=["sm"],
                  allow_slow_non_contiguous=True)
            S.dma("pool", DT[rs, :], bass.AP(log_dt.tensor, gl, [[0, 64], [2, 16]]), w=["sm"],
                  allow_slow_non_contiguous=True)
            S.dma("pool", Braw[0][rs, :, :], b_re.rearrange("(q gl) p h -> gl p q h", gl=2)[gl], w=["Braw"])
            S.dma("pool", Braw[1][rs, :, :], b_im.rearrange("(q gl) p h -> gl p q h", gl=2)[gl], w=["Braw"])
        S.dma("pool", Craw[0][:, :, :], c_re.rearrange("(c g) h p -> (g h) c p", c=4), w=["Craw"])
        S.dma("pool", Craw[1][:, :, :], c_im.rearrange("(c g) h p -> (g h) c p", c=4), w=["Craw"])
        S.dma("pool", dcol[:, :], d_in.rearrange("(c g) h -> (g h) c", c=4), w=["dcol"], allow_slow_non_contiguous=True)
        S.dma("pool", kramp[:], c_kramp, w=["kramp"])
        S.dma("pool", bramp[:], c_bramp, w=["bramp"])
        S.dma("pool", pmask[:], c_pmask, w=["pmask"])
        S.dma("pool", selt[:], c_sel, w=["selt"])

        if FUSED:
            for tt in range(1, 17):
                bgq.append(partial(p1_push, tt, 0))
            bgq.append(p1_flush)
            for tb in range(4):
                bgq.append(partial(run_p2_tb, tb))
                for tt in p1_tb_tiles(1, tb):
                    bgq.append(partial(p1_push, tt, 1))
            bgq.append(p1_flush)
            if True:
                while bgq:
                    bgq.popleft()()

        ACTF(DT, DT, AF.Exp, ["sm"], ["sm"])
        TT(LDR, LR, DT, ALU.mult, ["sm"], ["sm"])
        TT(LDI, LI, DT, ALU.mult, ["sm"], ["sm"])
        sh3 = [128, 19, 16]
        TT(mag, bc(kramp[:, :], 2, sh3), bc(LDR, 1, sh3), ALU.mult, ["kramp", "sm"], ["mag"])
        ACTF(mag, mag, AF.Exp, ["mag"], ["mag"])
        TT(ang, bc(kramp[:, :], 2, sh3), bc(LDI, 1, sh3), ALU.mult, ["kramp", "sm"], ["ang"])
        sincos(ang, nn, ar, "ang", "nn", "ar")
        TT(ar, ar, mag, ALU.mult, ["ar", "mag"], ["ar"])
        TT(ai, ang, mag, ALU.mult, ["ang", "mag"], ["ai"])
        AK = ["ar", "ai"]

        sh4 = [128, 16, NB]
        TT(sinT, bc(LDI, 2, sh4), bc(bramp[:, :], 1, sh4), ALU.mult, ["sm", "bramp"], ["sinT"])
        sincos(sinT, zz[0], cosT, "sinT", ZK0, "cosT")

        TT(DEN, LR, LR, ALU.mult, ["sm"], ["sm"])
        TT(T0, LI, LI, ALU.mult, ["sm"], ["sm"])
        TT(DEN, DEN, T0, ALU.add, ["sm"], ["sm"])
        S.op("dve", lambda g: g.reciprocal(out=DEN, in_=DEN), r=["sm"], w=["sm"])
        TS(AM1, ar[:, 1, :], -1.0, None, ALU.add, None, ["ar", "sm"], ["sm"])
        TT(T0, AM1, LR, ALU.mult, ["sm"], ["sm"])
        TT(T1, ai[:, 1, :], LI, ALU.mult, ["ai", "sm"], ["sm"])
        TT(T0, T0, T1, ALU.add, ["sm"], ["sm"])
        TT(ZR, T0, DEN, ALU.mult, ["sm"], ["sm"])
        TT(T0, ai[:, 1, :], LR, ALU.mult, ["ai", "sm"], ["sm"])
        TT(T1, AM1, LI, ALU.mult, ["sm"], ["sm"])
        TT(T0, T0, T1, ALU.subtract, ["sm"], ["sm"])
        TT(ZI, T0, DEN, ALU.mult, ["sm"], ["sm"])
        shb = [128, 16, 16]
        t_a, t_b = CBD[0][:, :, 0:16], CBD[0][:, :, 16:32]
        TT(t_a, bc(ZR, 2, shb), Braw[0][:, :, :], ALU.mult, ["sm", "Braw"], ["CBD"])
        TT(t_b, bc(ZI, 2, shb), Braw[1][:, :, :], ALU.mult, ["sm", "Braw"], ["CBD"])
        TT(bb[0][:, :, :], t_a, t_b, ALU.subtract, ["CBD"], ["bb"])
        TT(t_a, bc(ZR, 2, shb), Braw[1][:, :, :], ALU.mult, ["sm", "Braw", "bb"], ["CBD"])
        TT(t_b, bc(ZI, 2, shb), Braw[0][:, :, :], ALU.mult, ["sm", "Braw"], ["CBD"])
        TT(bb[1][:, :, :], t_a, t_b, ALU.add, ["CBD"], ["bb"])
        for i in range(2):
            S.op("dve", lambda g: g.memset(bbBD[i][:, :, :], 0.0), w=["bbBD"])
            S.op("pool", lambda g: g.memset(pad[i][:, :, :], 0.0), w=["pad"])
            for gl in range(2):
                rs = slice(gl * 64, (gl + 1) * 64)
                S.op("dve", lambda g: g.tensor_copy(out=bbBD[i][rs, :, gl * 16:(gl + 1) * 16], in_=bb[i][rs, :, :]),
                     r=["bb", "bbBD"], w=["bbBD"])
            for qq in range(4):
                S.op("dve", lambda g: g.tensor_copy(out=pad[i][:, qq::4, 32 * qq:32 * qq + 32], in_=bbBD[i][:, qq::4, :]),
                     r=["bbBD", "pad"], w=["pad"])

        for i in range(2):
            pc = PS[i][:, :].rearrange("p (q c) -> p q c", q=16)
            for c in range(4):
                for gl in range(2):
                    TS(Cexp[c % 2][:, gl * 64:(gl + 1) * 64], Craw[i][:, c, :], pmask[:, gl:gl + 1], None, ALU.mult, None,
                       ["Craw", "pmask", ("Cexp", c % 2)], [("Cexp", c % 2)])
                for qq in range(4):
                    S.op("pe", lambda g: g.matmul(pc[:, 4 * c + qq, :], lhsT=Cexp[c % 2][:, :],
                                                  rhs=identf[:, 32 * qq:32 * qq + 32], start=True, stop=True),
                         r=[("Cexp", c % 2), "identf"], w=[("ps", i)], signal=True)
            S.op("dve", lambda g: g.tensor_copy(out=CBD[i][:, :, :], in_=pc), r=[("ps", i), "CBD"], w=["CBD"])

        def build_E():
            shx = [128, 16, 4, 32]
            Xv = lambda ri: Xt[:, :, ri, :].rearrange("p k (q c) -> p k q c", q=4)
            W0 = zz[0].rearrange("p k (q c) -> p k q c", q=4)
            W1 = zz[1].rearrange("p k (q c) -> p k q c", q=4)
            for c in (0, 1, 2, 3):
                qs = slice(4 * c, 4 * c + 4)
                a_r = bc(ar[:, 0:16, qs], 3, shx)
                a_i = bc(ai[:, 0:16, qs], 3, shx)
                b_r = bc(bbBD[0][:, qs, :], 1, shx)
                b_i = bc(bbBD[1][:, qs, :], 1, shx)
                TT(W0, a_r, b_r, ALU.mult, AK + ["bbBD"], ZK0)
                TT(W1, a_i, b_i, ALU.mult, AK + ["bbBD"], ZK1)
                TT(Xv(0), W0, W1, ALU.subtract, ZK0 + ZK1, ["Xt", "tmp", "tmp2"])
                TT(W0, a_r, b_i, ALU.mult, AK + ["bbBD", "Xt"], ZK0)
                TT(W1, a_i, b_r, ALU.mult, AK + ["bbBD", "Xt"], ZK1)
                TT(Xv(1), W0, W1, ALU.add, ZK0 + ZK1, ["Xt", "tmp", "tmp2"])
                for k4 in range(4):
                    pe_ = PS[k4].bitcast(BF16).rearrange("p (s c) -> p s c", s=8)
                    for kk in range(4):
                        for ri in range(2):
                            k = 4 * k4 + kk
                            S.op("pe", lambda g: g.transpose(out=pe_[:, 2 * kk + ri, :], in_=Xt[:, k, ri, :], identity=identb[:, :]),
                                 r=["Xt", "tmp", "tmp2", "identb"], w=[("ps", k4)], signal=(kk == 3 and ri == 1))
                    ACTF(Eall[:, c, 4 * k4:4 * k4 + 4, :, :].rearrange("p a b c -> p (a b) c"), pe_, AF.Copy,
                         [("ps", k4)], ["Eall"])

        def p3a():
            for c in range(4):
                qs = slice(4 * c, 4 * c + 4)
                pl = [PS[4 + 2 * (c % 2) + ri][:, :].rearrange("p (q b) -> p q b", q=4) for ri in range(2)]
                for qq in range(4):
                    rs = slice(32 * qq, 32 * qq + 32)
                    for ri in range(2):
                        for j in range(16):
                            S.op("pe", lambda g: g.matmul(pl[ri][:, qq, :], lhsT=Eall[rs, c, 15 - j, ri, :], rhs=uPM[rs, c, j, :],
                                                          start=(j == 0), stop=(j == 15), tile_position=(32 * qq, 0)),
                                 r=["Eall", "EallT"] + [("uT", c, tb) for tb in range(4)], w=[("ps", 4 + 2 * (c % 2) + ri)],
                                 signal=(j == 15 and qq == 3))
                kl = [("ps", 4 + 2 * (c % 2)), ("ps", 5 + 2 * (c % 2))]
                cs, sn = cosT[:, qs, :], sinT[:, qs, :]
                TT(tmp, pl[1], sn, ALU.mult, [kl[1], "sinT"], ["tmp"])
                TT(zz[0][:, qs, :], pl[0], cs, ALU.mult, [kl[0], "cosT"], [("zz0", c)])
                TT(zz[0][:, qs, :], zz[0][:, qs, :], tmp, ALU.add, [("zz0", c), "tmp"], [("zz0", c)])
                TT(tmp2, pl[0], sn, ALU.mult, [kl[0], "sinT"], ["tmp2"])
                TT(zz[1][:, qs, :], pl[1], cs, ALU.mult, [kl[1], "cosT"], [("zz1", c)])
                TT(zz[1][:, qs, :], zz[1][:, qs, :], tmp2, ALU.subtract, [("zz1", c), "tmp2"], [("zz1", c)])

        ZK = [ZK0, ZK1]

        def scan_pass(init_fn, kin):
            for q in range(16):
                for ri in range(2):
                    S.op("dve", lambda g: g.tensor_tensor_scan(out=ww[ri][:, q, :],
                                                               data0=mag[:, 16, q:q + 1].broadcast_to([128, NB]),
                                                               data1=zz[ri][:, q, :], initial=init_fn(ri, q),
                                                               op0=ALU.mult, op1=ALU.add),
                         r=["mag", ("zz%d" % ri, q // 4)] + kin, w=[("zz%d" % ri, q // 4)])

        def local_final():
            c127, s127 = cosT[:, :, NB - 1], sinT[:, :, NB - 1]
            wr127, wi127 = ww[0][:, :, NB - 1], ww[1][:, :, NB - 1]
            TT(T0, c127, wr127, ALU.mult, ["cosT", *ZK0, "sm"], ["sm"])
            TT(T1, s127, wi127, ALU.mult, ["sinT", *ZK1, "sm"], ["sm"])
            TT(agbuf[:, 0:16], T0, T1, ALU.subtract, ["sm"], ["agbuf"])
            TT(T0, s127, wr127, ALU.mult, ["sinT", *ZK0, "sm"], ["sm"])
            TT(T1, c127, wi127, ALU.mult, ["cosT", *ZK1, "sm"], ["sm"])
            TT(agbuf[:, 16:32], T0, T1, ALU.add, ["sm", "agbuf"], ["agbuf"])

        def accumulate(idx):
            pr, pi_ = ar[:, idx, :], ai[:, idx, :]
            tr, ti = agbuf[:, 0:16], agbuf[:, 16:32]
            TT(T0, pr, tr, ALU.mult, AK + ["agbuf", "sm"], ["sm"])
            TT(T1, pi_, ti, ALU.mult, AK + ["agbuf", "sm"], ["sm"])
            TT(T0, T0, T1, ALU.subtract, ["sm"], ["sm"])
            TT(Sin[:, 0:16], Sin[:, 0:16], T0, ALU.add, ["sm", "Sin"], ["Sin"])
            TT(T0, pr, ti, ALU.mult, AK + ["agbuf", "sm"], ["sm"])
            TT(T1, pi_, tr, ALU.mult, AK + ["agbuf", "sm"], ["sm"])
            TT(T0, T0, T1, ALU.add, ["sm"], ["sm"])
            TT(Sin[:, 16:32], Sin[:, 16:32], T0, ALU.add, ["sm", "Sin"], ["Sin"])

        build_E()
        if mode == "F":
            S.op("dve", lambda g: g.memset(Sin[:, :], 0.0), w=["Sin"])
            while bgq:
                bgq.popleft()()
            for m_ in range(4):
                if m_ > 0:
                    for tb in range(4):
                        run_p2_tb(tb)
                        if m_ < 3:
                            for tt in p1_tb_tiles(m_ + 1, tb):
                                p1_push(tt, m_ + 1)
                    p1_flush()
                p3a()
                if m_ < 3:
                    scan_pass(lambda ri, q: 0.0, [])
                    local_final()
                    accumulate([0, 17, 18][m_])
        else:
            p3a()
            scan_pass(lambda ri, q: 0.0, [])
            local_final()
            if mode == 'A':
                sloc = nc.dram_tensor("sloc", [128, 32], F32, kind="ExternalOutput").ap()
                S.dma("sp", sloc, agbuf[:, :], r=["agbuf"], w=["sloc"])
                S.finish("sp", ["sloc"])
                return nc, S
            if mode == 'B':
                gin = nc.dram_tensor("gin", [4 * 128, 32], F32, kind="ExternalInput").ap()
                S.dma("pool", ag_out, gin, w=["ag_out"])
            elif NO_CC:
                for jj in range(4):
                    S.dma("pool", ag_out[jj * 128:(jj + 1) * 128, :], ag_in, r=["ag_in"], w=["ag_out"])
            else:
                S.dma("pool", ag_in, agbuf[:, :], r=["agbuf"], w=["ag_in"])
                S.custom_dma("pool", lambda g: g.collective_compute("AllGather", ALU.bypass,
                                                                    replica_groups=[[0, 1, 2, 3], [4, 5, 6, 7]],
                                                                    ins=[ag_in], outs=[ag_out]),
                             r=["ag_in"], w=["ag_out"])
            S.dma("pool", Gt[:, :, :], ag_out.rearrange("(j p) c -> p j c", p=128), r=["ag_out"], w=["Gt"])
            for m in range(3):
                for j in range(4):
                    if j == 0:
                        TS(Tm[:, m, :], Gt[:, 0, :], selt[:, m:m + 1], None, ALU.mult, None, ["Gt", "selt", "Tm"], ["Tm"])
                    else:
                        STT(Tm[:, m, :], Gt[:, j, :], selt[:, 3 * j + m:3 * j + m + 1], Tm[:, m, :], ALU.mult, ALU.add,
                            ["Gt", "selt", "Tm"], ["Tm"])
            S.op("dve", lambda g: g.tensor_copy(out=Sin[:, :], in_=Tm[:, 0, :]), r=["Tm"], w=["Sin"])
            for m in (1, 2):
                pr, pi_ = ar[:, 16 + m, :], ai[:, 16 + m, :]
                tr, ti = Tm[:, m, 0:16], Tm[:, m, 16:32]
                TT(T0, pr, tr, ALU.mult, AK + ["Tm", "sm"], ["sm"])
                TT(T1, pi_, ti, ALU.mult, AK + ["Tm", "sm"], ["sm"])
                TT(T0, T0, T1, ALU.subtract, ["sm"], ["sm"])
                TT(Sin[:, 0:16], Sin[:, 0:16], T0, ALU.add, ["sm", "Sin"], ["Sin"])
                TT(T0, pr, ti, ALU.mult, AK + ["Tm", "sm"], ["sm"])
                TT(T1, pi_, tr, ALU.mult, AK + ["Tm", "sm"], ["sm"])
                TT(T0, T0, T1, ALU.add, ["sm"], ["sm"])
                TT(Sin[:, 16:32], Sin[:, 16:32], T0, ALU.add, ["sm", "Sin"], ["Sin"])
        scan_pass(lambda ri, q: Sin[:, 16 * ri + q:16 * ri + q + 1], ["Sin"])
        for ri in range(2):
            S.op("dve", lambda g: g.tensor_copy(out=Sst[ri][:, :, 0], in_=Sin[:, 16 * ri:16 * ri + 16]),
                 r=["Sin"], w=[("Sst", ri), "Eall"])
        for c in range(4):
            qs = slice(4 * c, 4 * c + 4)
            cs, sn = cosT[:, qs, :], sinT[:, qs, :]
            TT(tmp, cs, ww[0][:, qs, :], ALU.mult, ["cosT", *ZK0], ["tmp"])
            TT(tmp2, sn, ww[1][:, qs, :], ALU.mult, ["sinT", *ZK1], ["tmp2"])
            TT(Sst[0][:, qs, 1:NB + 1], tmp, tmp2, ALU.subtract, ["tmp", "tmp2"], [("Sst", 0), "Eall"])
            TT(tmp, sn, ww[0][:, qs, :], ALU.mult, ["sinT", *ZK0], ["tmp"])
            TT(tmp2, cs, ww[1][:, qs, :], ALU.mult, ["cosT", *ZK1], ["tmp2"])
            TT(Sst[1][:, qs, 1:NB + 1], tmp, tmp2, ALU.add, ["tmp", "tmp2"], [("Sst", 1), "Eall"])

        shc = [128, 17, 4, 32]
        ZZW = [*ZK0, *ZK1, "tmp", "tmp2"] + ZK[0] + ZK[1]
        for c in range(4):
            CAt, lagT = CAt_b[c % 2], lagT_b[c % 2]
            KCA, KLG = ("CAt", c % 2), ("lagT", c % 2)
            XK = ["tmp", "tmp2"] if c % 2 == 1 else []
            qs = slice(4 * c, 4 * c + 4)
            a_r = bc(ar[:, 0:17, qs], 3, shc)
            a_i = bc(ai[:, 0:17, qs], 3, shc)
            c_r = bc(CBD[0][:, qs, :], 1, shc)
            c_i = bc(CBD[1][:, qs, :], 1, shc)
            TT(CAtmp[0], a_r, c_r, ALU.mult, AK + ["CBD"] + ZZW, ["CAtmp0", "EallT"])
            TT(CAtmp[1], a_i, c_i, ALU.mult, AK + ["CBD"] + ZZW, ["CAtmp1", "EallT"])
            TT(CAt[:, :, :, 0, :], CAtmp[0], CAtmp[1], ALU.subtract, ["CAtmp0", "CAtmp1"] + ZZW, [KCA] + XK)
            TT(CAtmp[0], a_r, c_i, ALU.mult, AK + ["CBD", KCA], ["CAtmp0"])
            TT(CAtmp[1], a_i, c_r, ALU.mult, AK + ["CBD", KCA], ["CAtmp1"])
            STT(CAt[:, :, :, 1, :], CAtmp[0], -1.0, CAtmp[1], ALU.mult, ALU.subtract, ["CAtmp0", "CAtmp1"], [KCA])
            for k4 in range(4):
                pg = PS[k4][:, :].rearrange("p (s c) -> p s c", s=4)
                for kk in range(4):
                    k = 4 * k4 + kk
                    for qq in range(4):
                        q = 4 * c + qq
                        S.op("pe", lambda g: g.matmul(pg[:, kk, 32 * qq:32 * qq + 32], lhsT=pad[0][:, q, :],
                                                      rhs=CAt[:, k, qq, 0, :], start=True, stop=False),
                             r=["pad", KCA], w=[("ps", k4)], signal=False)
                        S.op("pe", lambda g: g.matmul(pg[:, kk, 32 * qq:32 * qq + 32], lhsT=pad[1][:, q, :],
                                                      rhs=CAt[:, k, qq, 1, :], start=False, stop=True),
                             r=["pad", KCA], w=[("ps", k4)], signal=(kk == 3 and qq == 3))
                if k4 == 0:
                    STT(pg[:, 0, :], identf[:, :], dcol[:, c:c + 1], pg[:, 0, :], ALU.mult, ALU.add,
                        ["identf", "dcol", ("ps", 0)], [("ps", 0)])
                ACTF(lagT[:, 4 * k4:4 * k4 + 4, :], pg, AF.Copy, [("ps", k4)] + ZZW, [KLG] + XK)
            UK = [("uT", c, tb) for tb in range(4)]
            for j in range(16):
                py = PS[4 + j // 4][:, :].rearrange("p (s b) -> p s b", s=4)[:, j % 4, :]
                kb = ("ps", 4 + j // 4)
                for qq in range(4):
                    q = 4 * c + qq
                    rs = slice(32 * qq, 32 * qq + 32)
                    for ri in range(2):
                        S.op("pe", lambda g: g.matmul(py[rs, :], lhsT=CAt[:, j + 1, qq, ri, :], rhs=Sst[ri][:, q, 0:NB],
                                                      start=(ri == 0), stop=False, tile_position=(0, 32 * qq)),
                             r=[KCA, ("Sst", ri)], w=[kb], signal=False)
                for i in range(j + 1):
                    S.op("pe", lambda g: g.matmul(py[:, :], lhsT=lagT[:, j - i, :], rhs=uPM[:, c, i, :],
                                                  start=False, stop=(i == j)),
                         r=[KLG] + UK, w=[kb], signal=(i == j and j % 4 == 3))
            uv = uT[:, c, :].rearrange("p (b j) -> p j b", j=16)
            for j4 in range(4):
                pyv = PS[4 + j4][:, :].rearrange("p (s b) -> p s b", s=4)
                ACTF(uv[:, 4 * j4:4 * j4 + 4, :], pyv, AF.Copy, [("ps", 4 + j4)], UK)
        S.fence("s5done", ["sm", "ar", "ai", "mag", "ang", "nn", "Braw", "bb", "bbBD", "pad", "Craw", "CBD", "cosT",
                           "sinT", *ZK0, *ZK1, "tmp", "tmp2", "Eall", "Xt", ("CAt", 0), ("CAt", 1), ("lagT", 0), ("lagT", 1), "CAtmp0", "CAtmp1",
                           ("Sst", 0), ("Sst", 1), ("Cexp", 0), ("Cexp", 1), "zz0"] + ZK[0] + ZK[1])

    UT_ALL = [("uT", c, tb) for c in range(4) for tb in range(4)]
    for c in range(4):
        wb = slab_load(w_in, OFF_ZS + c * 128)
        for tb in range(4):
            pb = nextbank()
            mm_block(pb, wb, lambda k: hT[:, k, HALO + tb * 512: HALO + (tb + 1) * 512], ht_keys(tb))
            S.op("act", lambda e: e.activation(out=y2[:, c, tb * 512:(tb + 1) * 512], in_=PS[pb][:, :], func=AF.Silu),
                 r=[("ps", pb)], w=[("y2", c, tb)])
    gi = 0
    GELU_SQ = float(math.sqrt(0.044715 * 1.5957691216))
    for c in range(4):
        for tb in range(4):
            g0 = gt[gi % 2]
            gi += 1
            yv = uT[:, c, tb * 512:(tb + 1) * 512]
            S.op("act", lambda e: e.activation(out=g0[:], in_=yv, func=AF.Square, scale=GELU_SQ),
                 r=[("uT", c, tb)], w=[("gt", id(g0))])
            S.op("dve", lambda e: e.scalar_tensor_tensor(out=g0[:], in0=g0[:], scalar=1.5957691216, in1=yv,
                                                         op0=ALU.add, op1=ALU.mult),
                 r=[("gt", id(g0)), ("uT", c, tb)], w=[("gt", id(g0))])
            S.op("act", lambda e: e.activation(out=g0[:], in_=g0[:], func=AF.Sigmoid),
                 r=[("gt", id(g0))], w=[("gt", id(g0))])
            S.op("dve", lambda e: e.tensor_tensor(out=yv, in0=g0[:], in1=yv, op=ALU.mult),
                 r=[("gt", id(g0)), ("uT", c, tb)], w=[("uT", c, tb)])

    for c in range(4):
        wb = slab_load(w_glu, c * 128, kch=4)
        for tb in range(4):
            pb = nextbank()
            mm_block(pb, wb, lambda k: uT[:, k, tb * 512:(tb + 1) * 512], [("uT", k, tb) for k in range(4)], kch=4)
            s0 = sg[(c * 4 + tb) % 4]
            S.op("act", lambda e: e.activation(out=s0[:], in_=PS[pb][:, :], func=AF.Sigmoid,
                                               bias=bgluT[:, c:c + 1]),
                 r=[("ps", pb), "bgluT"], w=[("sg", id(s0))])
            S.op("dve", lambda e: e.tensor_tensor(out=s0[:], in0=s0[:], in1=uT[:, c, tb * 512:(tb + 1) * 512], op=ALU.mult),
                 r=[("sg", id(s0)), ("uT", c, tb)], w=[("sg", id(s0))])
            S.op("dve", lambda e: e.tensor_tensor(out=y2[:, c, tb * 512:(tb + 1) * 512], in0=s0[:],
                                                  in1=y2[:, c, tb * 512:(tb + 1) * 512], op=ALU.mult),
                 r=[("sg", id(s0)), ("y2", c, tb)], w=[("y2", c, tb)])

    def wout_slab(jc):
        a = jc % 2
        S.dma("pool", wst[a][:, :, :], w_out[:, jc * 128:(jc + 1) * 128].rearrange("(k p) c -> p k c", p=128),
              w=[("wst", a)])
        S.op("pool", lambda e: e.tensor_copy(out=woutb[:, :, jc * 128:(jc + 1) * 128], in_=wst[a][:, :, :]),
             r=[("wst", a)], w=[("woutb", jc)] + UT_ALL)

    cwsb = xt[0]
    S.dma("pool", cwsb[0:31, :], conv_w, w=[("xt", 0)])
    pcw = PS[0][:, 0:256].rearrange("p (c k) -> p c k", c=8)
    for ch in range(8):
        S.op("pe", lambda e: e.matmul(pcw[:, ch, 0:31], lhsT=cwsb[0:31, ch * 128:(ch + 1) * 128],
                                      rhs=identf[0:31, 0:31], start=True, stop=True),
             r=[("xt", 0), "identf"], w=[("ps", 0)], signal=(ch == 7))
    S.op("dve", lambda e: e.tensor_copy(out=cwT[:, :, 0:31], in_=pcw[:, :, 0:31]), r=[("ps", 0)], w=["cwT"])

    def tokblocks():
        return [(0, HALO, [("hT", 0)])] + [(HALO + tb * 512, 512, ht_keys(tb)) for tb in range(4)]

    for i in range(8):
        wa = slab_load(w_in, OFF_CA + i * 128)
        wbb = slab_load(w_in, OFF_CB + i * 128)
        for bi, (c0, n, keys) in enumerate(tokblocks()):
            pa = nextbank()
            pbk = nextbank()
            mm_block(pa, wa, lambda k: hT[:, k, c0:c0 + n], keys, n=n)
            mm_block(pbk, wbb, lambda k: hT[:, k, c0:c0 + n], keys, n=n)
            s0 = sg[(i * 5 + bi) % 4]
            S.op("act", lambda e: e.activation(out=s0[:, :n], in_=PS[pbk][:, :n], func=AF.Sigmoid),
                 r=[("ps", pbk)], w=[("sg", id(s0))])
            S.op("dve", lambda e: e.tensor_tensor(out=cu[:, i, c0:c0 + n], in0=PS[pa][:, :n], in1=s0[:, :n],
                                                  op=ALU.mult),
                 r=[("ps", pa), ("sg", id(s0))], w=[("cu", i, bi)])

    SUMB, SQB = 6, 7
    for i in range(8):
        pass
    dg_built = {}

    dg_cnt = [0]

    def build_dg(i):
        nb_ = dg_cnt[0] % 2
        dg_cnt[0] += 1
        t = dg[nb_]
        S.op("dve", lambda e: e.tensor_tensor(out=t[:, :, :], in0=identf[:, :].unsqueeze(1).broadcast_to([128, 31, 128]),
                                              in1=cwT[:, i, 0:31].unsqueeze(2).broadcast_to([128, 31, 128]), op=ALU.mult),
             r=["identf", "cwT"], w=[("dg", nb_)])
        return t, nb_

    def conv_tb(tb):
        for i in range(8):
            t, nb_ = build_dg(i)
            if tb == 3:
                wout_slab(i)
            pb = nextbank(1, 6)
            base = HALO + tb * 512 - 30
            for k in range(31):
                S.op("pe", lambda e: e.matmul(PS[pb][:, :], lhsT=t[:, k, :], rhs=cu[:, i, base + k: base + k + 512],
                                              start=(k == 0), stop=(k == 30)),
                     r=[("dg", nb_), ("cu", i, tb), ("cu", i, 1 + tb)], w=[("ps", pb)], signal=(k == 30))
            S.op("act", lambda e: e.activation(out=cu[:, i, HALO + tb * 512: HALO + (tb + 1) * 512], in_=PS[pb][:, :],
                                               func=AF.Identity, bias=cbT[:, i:i + 1]),
                 r=[("ps", pb), "cbT"], w=[("cu", i, 1 + tb)])

    CU_KEYS = lambda i: [("cu", i, bi) for bi in range(5)]
    vi = 0

    def zc_slab(i):
        wb = slab_load(w_in, OFF_ZC + i * 128)
        for tb2 in range(4):
            pb = nextbank(1, 6)
            mm_block(pb, wb, lambda k: hT[:, k, HALO + tb2 * 512: HALO + (tb2 + 1) * 512], ht_keys(tb2))
            S.op("act", lambda e: e.activation(out=mrg[:, i, tb2 * 512:(tb2 + 1) * 512], in_=PS[pb][:, :], func=AF.Silu),
                 r=[("ps", pb)], w=[("mrg", i, tb2)])

    for tb in reversed(range(4)):
        conv_tb(tb)
        vs = lambda i: cu[:, i, HALO + tb * 512: HALO + (tb + 1) * 512]
        for i in range(8):
            q = sqb[vi % 2]
            vi += 1
            S.op("dve", lambda e: e.tensor_tensor(out=q[:], in0=vs(i), in1=vs(i), op=ALU.mult),
                 r=[("cu", i, 1 + tb)], w=[("sqb", id(q))])
            S.op("pe", lambda e: e.matmul(PS[SUMB][:, :], lhsT=onesb[:], rhs=vs(i), start=(i == 0), stop=(i == 7)),
                 r=["onesb", ("cu", i, 1 + tb)], w=[("ps", SUMB)], signal=False)
            S.op("pe", lambda e: e.matmul(PS[SQB][:, :], lhsT=onesb[:], rhs=q[:], start=(i == 0), stop=(i == 7)),
                 r=["onesb", ("sqb", id(q))], w=[("ps", SQB)], signal=True)
        zc_slab(2 * (3 - tb))
        zc_slab(2 * (3 - tb) + 1)
        S.op("act", lambda e: e.activation(out=st_mean[:], in_=PS[SUMB][:, :], func=AF.Copy, scale=1.0 / D),
             r=[("ps", SUMB)], w=[K_MEAN])
        S.op("dve", lambda e: e.tensor_tensor(out=st_tmp[:], in0=st_mean[:], in1=st_mean[:], op=ALU.mult),
             r=[K_MEAN], w=[K_TMP])
        S.op("dve", lambda e: e.scalar_tensor_tensor(out=st_tmp[:], in0=PS[SQB][:, :], scalar=1.0 / D, in1=st_tmp[:],
                                                     op0=ALU.mult, op1=ALU.subtract),
             r=[("ps", SQB), K_TMP], w=[K_TMP])
        S.op("act", lambda e: e.activation(out=st_rstd[:], in_=st_tmp[:], func=AF.Sqrt, bias=epsc[:, 1:2]),
             r=[K_TMP, "epsc"], w=[K_RSTD])
        S.op("dve", lambda e: e.reciprocal(out=st_rstd[:], in_=st_rstd[:]), r=[K_RSTD], w=[K_RSTD])
        for i in range(8):
            g0 = gt[i % 2]
            S.op("dve", lambda e: e.tensor_tensor(out=g0[:], in0=vs(i), in1=st_mean[:], op=ALU.subtract),
                 r=[("cu", i, 1 + tb), K_MEAN], w=[("gt", id(g0))])
            S.op("dve", lambda e: e.tensor_tensor(out=g0[:], in0=g0[:], in1=st_rstd[:], op=ALU.mult),
                 r=[("gt", id(g0)), K_RSTD], w=[("gt", id(g0))])
            S.op("act", lambda e: e.activation(out=vs(i), in_=g0[:], func=AF.Silu, scale=lngT[:, i:i + 1],
                                               bias=lnbT[:, i:i + 1]),
                 r=[("gt", id(g0)), "lngT", "lnbT"], w=[("cu", i, 1 + tb)])
    for tb in range(4):
        for i in range(8):
            hsl = slice(HALO + tb * 512, HALO + (tb + 1) * 512)
            S.op("dve", lambda e: e.tensor_tensor(out=cu[:, i, hsl], in0=cu[:, i, hsl], in1=mrg[:, i, tb * 512:(tb + 1) * 512],
                                                  op=ALU.mult),
                 r=[("cu", i, 1 + tb), ("mrg", i, tb)], w=[("cu", i, 1 + tb)])

    for j in range(8):
        wco_b = slab_load(w_co, j * 128)
        wso_b = slab_load(w_so, j * 128, kch=4)
        wgc_b = slab_load(w_in, OFF_GC + j * 128)
        for phase in range(2):
            if phase == 1:
                wgs_b = slab_load(w_in, OFF_GS + j * 128)
            for tb in range(4):
                tsl = slice(tb * 512, (tb + 1) * 512)
                hsl = slice(HALO + tb * 512, HALO + (tb + 1) * 512)
                if phase == 0:
                    pa = nextbank(1, 8)
                    pc = nextbank(1, 8)
                    mm_block(pa, wco_b, lambda k: cu[:, k, hsl], [("cu", k, 1 + tb) for k in range(8)])
                    mm_block(pc, wgc_b, lambda k: hT[:, k, hsl], ht_keys(tb))
                    s0 = sg[tb % 4]
                    S.op("act", lambda e: e.activation(out=s0[:], in_=PS[pc][:, :], func=AF.Sigmoid),
                         r=[("ps", pc)], w=[("sg", id(s0))])
                    S.op("dve", lambda e: e.tensor_tensor(out=mrg[:, j, tsl], in0=PS[pa][:, :], in1=s0[:], op=ALU.mult),
                         r=[("ps", pa), ("sg", id(s0))], w=[("mrg", j, tb)])
                else:
                    pb2 = nextbank(1, 8)
                    pd = nextbank(1, 8)
                    mm_block(pb2, wso_b, lambda k: y2[:, k, tsl], [("y2", k, tb) for k in range(4)], kch=4)
                    mm_block(pd, wgs_b, lambda k: hT[:, k, hsl], ht_keys(tb))
                    s0 = sg[tb % 4]
                    S.op("act", lambda e: e.activation(out=s0[:], in_=PS[pd][:, :], func=AF.Sigmoid),
                         r=[("ps", pd)], w=[("sg", id(s0))])
                    S.op("dve", lambda e: e.tensor_tensor(out=s0[:], in0=PS[pb2][:, :], in1=s0[:], op=ALU.mult),
                         r=[("ps", pb2), ("sg", id(s0))], w=[("sg", id(s0))])
                    S.op("dve", lambda e: e.tensor_tensor(out=mrg[:, j, tsl], in0=s0[:], in1=mrg[:, j, tsl], op=ALU.add),
                         r=[("sg", id(s0)), ("mrg", j, tb)], w=[("mrg", j, tb)])

    S.fence("hTdead", HT_KEYS)
    S.dma("pool", pgB[:], post_g.partition_broadcast(128), w=["pgB"])
    WOUT_KEYS = [("woutb", jc) for jc in range(8)]
    ssq2 = sb("ssq2", [128, 32], F32)
    rstd2 = sb("rstd2", [128, 16], F32)
    S.op("dve", lambda e: e.memset(ssq2[:], 0.0), w=["ssq2"])
    for tt in range(16):
        b = tt % 2
        tb = tt // 4
        xb3 = tt % 3
        S.dma("sp", xt[xb3][:, :], x[HALO + tt * 128: HALO + (tt + 1) * 128, :], w=[("xt", xb3)])
        pbs = [nextbank(1, 8), nextbank(1, 8)]
        for hf in range(2):
            for k in range(8):
                S.op("pe", lambda e: e.matmul(PS[pbs[hf]][:, :], lhsT=mrg[:, k, tt * 128:(tt + 1) * 128],
                                              rhs=woutb[:, k, hf * 512:(hf + 1) * 512], start=(k == 0), stop=(k == 7)),
                     r=[("mrg", k, tb)] + WOUT_KEYS[hf * 4:(hf + 1) * 4], w=[("ps", pbs[hf])], signal=(k == 7))
            S.op("act", lambda e: e.activation(out=junk[:, 0:512], in_=PS[pbs[hf]][:, :], func=AF.Square,
                                               accum_out=ssq2[:, 2 * tt + hf: 2 * tt + hf + 1]),
                 r=[("ps", pbs[hf]), "ssq2"], w=[K_JUNK, ("ssq2", tt, hf)])
        S.op("dve", lambda e: e.tensor_tensor(out=rstd2[:, tt:tt + 1], in0=ssq2[:, 2 * tt:2 * tt + 1],
                                              in1=ssq2[:, 2 * tt + 1:2 * tt + 2], op=ALU.add),
             r=[("ssq2", tt, 0), ("ssq2", tt, 1)], w=[("rstd2", tt)])
        S.op("act", lambda e: e.activation(out=rstd2[:, tt:tt + 1], in_=rstd2[:, tt:tt + 1], func=AF.Sqrt,
                                           scale=1.0 / D, bias=epsc[:, 0:1]),
             r=[("rstd2", tt), "epsc"], w=[("rstd2", tt)])
        S.op("dve", lambda e: e.reciprocal(out=rstd2[:, tt:tt + 1], in_=rstd2[:, tt:tt + 1]),
             r=[("rstd2", tt)], w=[("rstd2", tt)])
        for hf in range(2):
            hs = slice(hf * 512, (hf + 1) * 512)
            S.op("dve", lambda e: e.scalar_tensor_tensor(out=ot[b][:, hs], in0=PS[pbs[hf]][:, :],
                                                         scalar=rstd2[:, tt:tt + 1], in1=pgB[:, hs],
                                                         op0=ALU.mult, op1=ALU.mult),
                 r=[("ps", pbs[hf]), ("rstd2", tt), "pgB"], w=[("ot", b, hf)])
            S.op("pool", lambda e: e.tensor_tensor(out=ot[b][:, hs], in0=ot[b][:, hs], in1=xt[xb3][:, hs], op=ALU.add),
                 r=[("ot", b, hf), ("xt", xb3)], w=[("ot", b, hf)])
        S.dma("pool", out_d[tt * 128:(tt + 1) * 128, :], ot[b][:, :], r=[("ot", b, 0), ("ot", b, 1)], w=[("outd", tt)])
    dbg_aps = {"hT": hT, "uT": uT, "cu": cu, "mrg": mrg, "y2": y2}
    dbg_keys = {"hT": HT_KEYS, "uT": UT_ALL, "cu": [("cu", i, b) for i in range(8) for b in range(5)],
                "mrg": [("mrg", j, tb) for j in range(8) for tb in range(4)],
                "y2": [("y2", c, tb) for c in range(4) for tb in range(4)]}
    fin = [("outd", tt) for tt in range(16)]
    for name in debug:
        ap = dbg_aps[name]
        dd = nc.dram_tensor("dbg_" + name, list(ap.shape), ap.dtype, kind="ExternalOutput").ap()
        S.dma("sp", dd, ap, r=dbg_keys[name], w=[("dbgout", name)])
        fin.append(("dbgout", name))
    S.finish("sp", fin)
    return nc, S


_CONST = {}


def _consts():
    if not _CONST:
        _CONST["c_ident"] = np.eye(128, dtype=np.float32)
        kr = np.concatenate([np.arange(NEXP), [2048, 4096]]).astype(np.float32)
        _CONST["c_kramp"] = np.ascontiguousarray(np.broadcast_to(kr, (128, NEXP + 2)))
        br = (R0 * (np.arange(NB) + 1)).astype(np.float32)
        _CONST["c_bramp"] = np.ascontiguousarray(np.broadcast_to(br, (128, NB)))
        pm = np.zeros((128, 2), np.float32)
        par = (np.arange(128) // 16) % 2
        pm[par == 0, 0] = 1.0
        pm[par == 1, 1] = 1.0
        _CONST["c_pmask"] = pm
    return _CONST


def make_in_maps(inputs, fused=False):
    f = lambda a: np.ascontiguousarray(np.asarray(a, dtype=np.float32))
    x = f(inputs["x"])
    shared = {
        "w_in": f(inputs["w_in"][0]), "pre_g": f(inputs["pre_norm_gain"][0]), "conv_w": f(inputs["conv_w"][0]),
        "conv_b": f(inputs["conv_b"][0]), "ln_g": f(inputs["conv_ln_gain"][0]), "ln_b": f(inputs["conv_ln_bias"][0]),
        "w_co": f(inputs["w_conv_out"][0]), "lam_re": f(inputs["ssm_lambda_re"][0]),
        "lam_im": f(inputs["ssm_lambda_im"][0]), "log_dt": f(inputs["ssm_log_dt"][0]),
        "b_re": f(inputs["ssm_b_re"][0]), "b_im": f(inputs["ssm_b_im"][0]), "c_re": f(inputs["ssm_c_re"][0]),
        "c_im": f(inputs["ssm_c_im"][0]), "d_in": f(inputs["ssm_d"][0]), "w_glu": f(inputs["w_ssm_glu"][0]),
        "b_glu": f(inputs["b_ssm_glu"][0]), "w_so": f(inputs["w_ssm_out"][0]), "w_out": f(inputs["w_out"][0]),
        "post_g": f(inputs["post_norm_gain"][0]),
    }
    shared.update(_consts())
    maps = []
    for c in range(NCORES):
        bi, ci = c // 4, c % 4
        xc = np.zeros((NT + 3 * TOK if fused else NT, D), np.float32)
        t0 = ci * TOK
        xc[HALO:NT] = x[bi, t0:t0 + TOK]
        if fused:
            for m_ in range(3):
                cj = ci - 1 - m_
                if cj >= 0:
                    xc[NT + m_ * TOK: NT + (m_ + 1) * TOK] = x[bi, cj * TOK:(cj + 1) * TOK]
        if ci > 0:
            xc[:HALO] = x[bi, t0 - HALO:t0]
        sel = np.zeros((8, 3), np.float32)
        for j in range(4):
            if j < ci:
                sel[j, ci - 1 - j] = 1.0
        m = dict(shared)
        m["x"] = xc
        m["c_sel"] = np.ascontiguousarray(np.broadcast_to(sel.reshape(1, 24), (128, 24)))
        maps.append(m)
    return maps


_PROG = {}


def kernel(**inputs):
    if "ncF" not in _PROG:
        _PROG["ncF"] = build_program(mode='F')[0]
    maps = make_in_maps(inputs, fused=True)
    res = run_bass_kernel_spmd(_PROG["ncF"], maps, core_ids=list(range(NCORES)))
    out = np.empty((2, 8192, D), np.float32)
    for c in range(NCORES):
        out[c // 4, (c % 4) * TOK:(c % 4 + 1) * TOK] = res.results[c]["out"]
    return out
```
